# Optimizing a Trainium2 kernel written in Bass

```python
import jax, jax.numpy as jnp
from jax import lax
import numpy as np

D_MODEL = 1024
BATCH = 8
SEQ = 4096
DEPTH = 2

N_A_LAYERS = DEPTH // 2
N_B_LAYERS = DEPTH - N_A_LAYERS
N_SUBLAYERS = 3
GLA_HEADS = 4
GLA_DK = D_MODEL // 2 // GLA_HEADS
GLA_DV = D_MODEL // GLA_HEADS
GLA_GATE_RANK = 16
GLA_TAU = 16.0
GLA_CHUNK = 64
MLA_HEADS = 8
QK_NOPE = 128
QK_ROPE = 64
V_HEAD = 128
KV_LORA = 256
Q_LORA = 384
ROPE_THETA = 10000.0
Q_BLOCK = 128
D_FF = 2816
EPS = 1e-6
MAX_POS_OFFSET = 1024

kernel_name = 'gla_mla_yoco_hybrid'


def rmsnorm(x, g):
    xf = x.astype(jnp.float32)
    y = xf * lax.rsqrt(jnp.mean(xf * xf, axis=-1, keepdims=True) + EPS)
    return (y * g.astype(jnp.float32)).astype(x.dtype)


def swiglu(h, w_gu, w_down):
    gate, up = jnp.split(h @ w_gu, 2, axis=-1)
    return (jax.nn.silu(gate) * up) @ w_down


def sublayer(x, shift, scale, gate, g_pre, g_post, fn, res_weight):
    h = rmsnorm(x, g_pre) * (1 + scale[:, None, :]) + shift[:, None, :]
    return x + res_weight * gate[:, None, :] * rmsnorm(fn(h), g_post)


def rope_tables(positions):
    inv_freq = ROPE_THETA ** (-jnp.arange(0, QK_ROPE, 2, dtype=jnp.float32) / QK_ROPE)
    ang = positions.astype(jnp.float32)[..., None] * inv_freq
    return jnp.cos(ang), jnp.sin(ang)


def apply_rope(x, cos, sin):
    cos = cos.astype(x.dtype)
    sin = sin.astype(x.dtype)
    x1, x2 = jnp.split(x, 2, axis=-1)
    return jnp.concatenate([x1 * cos - x2 * sin, x1 * sin + x2 * cos], axis=-1)


def gla_chunked(q, k, v, log_a):
    B, S, H, DK = q.shape
    C = GLA_CHUNK
    N = S // C
    f32 = jnp.float32

    def chunks(t):
        return t.astype(f32).reshape(B, N, C, H, t.shape[-1]).transpose(0, 3, 1, 2, 4)

    q, k, v, log_a = chunks(q) * DK ** -0.5, chunks(k), chunks(v), chunks(log_a)
    b = jnp.cumsum(log_a, axis=3)
    b_last = b[:, :, :, -1:, :]
    q_dec = q * jnp.exp(b)
    k_dec = k * jnp.exp(-b)
    causal = jnp.tril(jnp.ones((C, C), dtype=bool))
    att = jnp.where(causal, jnp.einsum('bhnid,bhnjd->bhnij', q_dec, k_dec), 0.0)
    o_intra = jnp.einsum('bhnij,bhnjv->bhniv', att, v)
    upd = jnp.einsum('bhncd,bhncv->bhndv', k * jnp.exp(b_last - b), v)
    decay = jnp.exp(b_last[:, :, :, 0, :])

    def step(state, inp):
        dec, u = inp
        return dec[..., None] * state + u, state

    s0 = jnp.zeros((B, H, DK, v.shape[-1]), f32)
    _, states = lax.scan(step, s0, (jnp.moveaxis(decay, 2, 0), jnp.moveaxis(upd, 2, 0)))
    states = jnp.moveaxis(states, 0, 2)
    o = o_intra + jnp.einsum('bhncd,bhndv->bhncv', q_dec, states)
    return o.transpose(0, 2, 3, 1, 4).reshape(B, S, H, v.shape[-1])


def gla_mixer(h, w_in, w_gate_up, b_gate, g_out, w_out):
    B, S, _ = h.shape
    kd = GLA_HEADS * GLA_DK
    vd = GLA_HEADS * GLA_DV
    q, k, v, g_low, r = jnp.split(h @ w_in, [kd, 2 * kd, 2 * kd + vd, 2 * kd + vd + GLA_GATE_RANK], axis=-1)
    log_a = jax.nn.log_sigmoid((g_low @ w_gate_up + b_gate).astype(jnp.float32)) / GLA_TAU
    o = gla_chunked(q.reshape(B, S, GLA_HEADS, GLA_DK), k.reshape(B, S, GLA_HEADS, GLA_DK),
                    v.reshape(B, S, GLA_HEADS, GLA_DV), log_a.reshape(B, S, GLA_HEADS, GLA_DK))
    o = rmsnorm(o, g_out).astype(h.dtype).reshape(B, S, vd)
    return (o * jax.nn.silu(r)) @ w_out


def mla_shared_kv(h, w_kv_a, g_kv, w_kv_b, cos, sin):
    B, S, _ = h.shape
    c_kv, k_pe = jnp.split(h @ w_kv_a, [KV_LORA], axis=-1)
    c_kv = rmsnorm(c_kv, g_kv)
    k_rope = apply_rope(k_pe, cos, sin)
    kv = (c_kv @ w_kv_b).reshape(B, S, MLA_HEADS, QK_NOPE + V_HEAD)
    k_nope, v = jnp.split(kv, [QK_NOPE], axis=-1)
    return k_nope, k_rope, v


def mla_mixer(h, k_nope, k_rope, v, w_dq, g_q, w_uq, w_out, cos, sin):
    B, S, _ = h.shape
    c_q = rmsnorm(h @ w_dq, g_q)
    q = (c_q @ w_uq).reshape(B, S, MLA_HEADS, QK_NOPE + QK_ROPE)
    q_nope, q_rope = jnp.split(q, [QK_NOPE], axis=-1)
    q_rope = apply_rope(q_rope, cos[:, :, None, :], sin[:, :, None, :])
    scale = (QK_NOPE + QK_ROPE) ** -0.5
    nb = S // Q_BLOCK
    qn_blocks = q_nope.reshape(B, nb, Q_BLOCK, MLA_HEADS, QK_NOPE).transpose(1, 0, 2, 3, 4)
    qr_blocks = q_rope.reshape(B, nb, Q_BLOCK, MLA_HEADS, QK_ROPE).transpose(1, 0, 2, 3, 4)
    key_pos = jnp.arange(S)
    neg = jnp.finfo(jnp.float32).min

    def attend(args):
        qn, qr, blk = args
        s = (jnp.einsum('bqhd,bkhd->bhqk', qn, k_nope) +
             jnp.einsum('bqhr,bkr->bhqk', qr, k_rope)).astype(jnp.float32) * scale
        q_pos = blk * Q_BLOCK + jnp.arange(Q_BLOCK)
        s = jnp.where(key_pos[None, :] <= q_pos[:, None], s, neg)
        p = jax.nn.softmax(s, axis=-1).astype(v.dtype)
        return jnp.einsum('bhqk,bkhv->bqhv', p, v)

    o = lax.map(attend, (qn_blocks, qr_blocks, jnp.arange(nb)))
    o = o.transpose(1, 0, 2, 3, 4).reshape(B, S, MLA_HEADS * V_HEAD)
    return o @ w_out


def setup_inputs(seed: int = 0) -> dict:
    key = jax.random.key(seed)
    ks = jax.random.split(key, 24)
    D = D_MODEL
    f32 = jnp.float32

    def w(k, shape, fan_in):
        return jax.random.normal(k, shape, f32) * fan_in ** -0.5

    def gain(k, shape):
        return 1.0 + 0.1 * jax.random.normal(k, shape, f32)

    def bias(k, shape):
        return 0.01 * jax.random.normal(k, shape, f32)

    gla_in_cols = 2 * GLA_HEADS * GLA_DK + 2 * GLA_HEADS * GLA_DV + GLA_GATE_RANK
    offsets = jax.random.randint(ks[2], (BATCH, 1), 0, MAX_POS_OFFSET, dtype=jnp.int32)
    return {
        'x': jax.random.normal(ks[0], (BATCH, SEQ, D), f32),
        'c': jax.random.normal(ks[1], (BATCH, D), f32),
        'positions': offsets + jnp.arange(SEQ, dtype=jnp.int32)[None, :],
        'cond_w': w(ks[3], (DEPTH, D, 3 * N_SUBLAYERS * D), D),
        'cond_b': bias(ks[4], (DEPTH, 3 * N_SUBLAYERS * D)),
        'norm_g': gain(ks[5], (DEPTH, N_SUBLAYERS, 2, D)),
        'ffn_w_gu': w(ks[6], (DEPTH, 2, D, 2 * D_FF), D),
        'ffn_w_down': w(ks[7], (DEPTH, 2, D_FF, D), D_FF),
        'gla_w_in': w(ks[8], (N_A_LAYERS, D, gla_in_cols), D),
        'gla_w_gate_up': w(ks[9], (N_A_LAYERS, GLA_GATE_RANK, GLA_HEADS * GLA_DK), GLA_GATE_RANK),
        'gla_b_gate': 0.1 * jax.random.normal(ks[10], (N_A_LAYERS, GLA_HEADS * GLA_DK), f32),
        'gla_g_out': gain(ks[11], (N_A_LAYERS, GLA_DV)),
        'gla_w_out': w(ks[12], (N_A_LAYERS, GLA_HEADS * GLA_DV, D), GLA_HEADS * GLA_DV),
        'kv_g_in': gain(ks[13], (D,)),
        'kv_cond_w': w(ks[14], (D, 2 * D), D),
        'kv_cond_b': bias(ks[15], (2 * D,)),
        'mla_w_kv_a': w(ks[16], (D, KV_LORA + QK_ROPE), D),
        'mla_g_kv': gain(ks[17], (KV_LORA,)),
        'mla_w_kv_b': w(ks[18], (KV_LORA, MLA_HEADS * (QK_NOPE + V_HEAD)), KV_LORA),
        'mla_w_dq': w(ks[19], (N_B_LAYERS, D, Q_LORA), D),
        'mla_g_q': gain(ks[20], (N_B_LAYERS, Q_LORA)),
        'mla_w_uq': w(ks[21], (N_B_LAYERS, Q_LORA, MLA_HEADS * (QK_NOPE + QK_ROPE)), Q_LORA),
        'mla_w_out': w(ks[22], (N_B_LAYERS, MLA_HEADS * V_HEAD, D), MLA_HEADS * V_HEAD),
    }


def reference(x, c, positions, cond_w, cond_b, norm_g, ffn_w_gu, ffn_w_down,
              gla_w_in, gla_w_gate_up, gla_b_gate, gla_g_out, gla_w_out,
              kv_g_in, kv_cond_w, kv_cond_b, mla_w_kv_a, mla_g_kv, mla_w_kv_b,
              mla_w_dq, mla_g_q, mla_w_uq, mla_w_out):
    cos, sin = rope_tables(positions)
    c_act = jax.nn.silu(c)
    k_nope = k_rope = v_shared = None
    for layer in range(DEPTH):
        mods = jnp.split(c_act @ cond_w[layer] + cond_b[layer], 3 * N_SUBLAYERS, axis=-1)
        g = norm_g[layer]
        x = sublayer(x, mods[0], mods[1], mods[2], g[0, 0], g[0, 1],
                     lambda h: swiglu(h, ffn_w_gu[layer, 0], ffn_w_down[layer, 0]), 0.5)
        if layer < N_A_LAYERS:
            i = layer
            x = sublayer(x, mods[3], mods[4], mods[5], g[1, 0], g[1, 1],
                         lambda h: gla_mixer(h, gla_w_in[i], gla_w_gate_up[i], gla_b_gate[i],
                                             gla_g_out[i], gla_w_out[i]), 1.0)
        else:
            i = layer - N_A_LAYERS
            x = sublayer(x, mods[3], mods[4], mods[5], g[1, 0], g[1, 1],
                         lambda h: mla_mixer(h, k_nope, k_rope, v_shared, mla_w_dq[i], mla_g_q[i],
                                             mla_w_uq[i], mla_w_out[i], cos, sin), 1.0)
        x = sublayer(x, mods[6], mods[7], mods[8], g[2, 0], g[2, 1],
                     lambda h: swiglu(h, ffn_w_gu[layer, 1], ffn_w_down[layer, 1]), 0.5)
        if layer == N_A_LAYERS - 1:
            kv_shift, kv_scale = jnp.split(c_act @ kv_cond_w + kv_cond_b, 2, axis=-1)
            h_kv = rmsnorm(x, kv_g_in) * (1 + kv_scale[:, None, :]) + kv_shift[:, None, :]
            k_nope, k_rope, v_shared = mla_shared_kv(h_kv, mla_w_kv_a, mla_g_kv, mla_w_kv_b, cos, sin)
    return x
```

```python
import contextlib
import math
import os
import numpy as np
import concourse.bass as bass
import concourse.mybir as mybir
from concourse.bass_utils import run_bass_kernel_spmd

F32 = mybir.dt.float32
BF16 = mybir.dt.bfloat16
I32 = mybir.dt.int32
AF = mybir.ActivationFunctionType
ALU = mybir.AluOpType
AX = mybir.AxisListType
ESZ = {F32: 4, BF16: 2, I32: 4}

ENGS = ("pe", "act", "dve", "pool", "sp")
EIDX = {e: i for i, e in enumerate(ENGS)}
EPOCH = 16000
RING = 14

D = 1024
SEQ = 4096
NT = SEQ // 128
DFF = 2816
EPS = 1e-6
ARENA_BYTES = 212480
STRICT = os.environ.get("MK_STRICT", "0") == "1"


class Region:
    def __init__(self, size, bucket=2048):
        self.size = size
        self.B = bucket
        self.b = {}


class View:
    __slots__ = ("ap", "r", "lo", "hi")

    def __init__(self, ap, r, lo, hi):
        self.ap = ap; self.r = r; self.lo = lo; self.hi = hi


class Sub:
    def __init__(self, ap, region, off, esz, shape, track_from):
        self.ap = ap; self.r = region; self.off = off; self.esz = esz
        self.shape = tuple(shape); self.skip = track_from
        st = []; acc = 1
        for d in reversed(self.shape[track_from:]):
            st.append(acc); acc *= d
        self.strides = tuple(reversed(st))
        self.nbytes = acc * esz
        self.whole = False

    def __getitem__(self, idx):
        if self.whole:
            if not isinstance(idx, tuple):
                idx = (idx,)
            return View(self.ap[idx], self.r, 0, self.r.size)
        if not isinstance(idx, tuple):
            idx = (idx,)
        idx = tuple(idx) + (slice(None),) * (len(self.shape) - len(idx))
        lo = 0; hi = 0
        for d in range(self.skip, len(self.shape)):
            i = idx[d]; s = self.strides[d - self.skip]
            if isinstance(i, slice):
                a = 0 if i.start is None else i.start
                b = self.shape[d] if i.stop is None else i.stop
                assert i.step in (None, 1) and 0 <= a < b <= self.shape[d], (idx, self.shape)
            else:
                a = i; b = i + 1
                assert 0 <= a < self.shape[d], (idx, self.shape)
            lo += a * s; hi += (b - 1) * s
        return View(self.ap[idx], self.r, self.off + lo * self.esz, self.off + (hi + 1) * self.esz)

    def all(self):
        if self.whole:
            return View(self.ap, self.r, 0, self.r.size)
        return View(self.ap, self.r, self.off, self.off + self.nbytes)

    def cv(self, ap, lo_e=0, hi_e=None):
        hi_e = self.nbytes // self.esz if hi_e is None else hi_e
        return View(ap, self.r, self.off + lo_e * self.esz, self.off + hi_e * self.esz)


class Op:
    __slots__ = ("eng", "fn", "dma", "pos", "waits", "signal", "clock", "slot", "val", "sidx")


class Sched:
    def __init__(self, nc):
        self.nc = nc
        self.ops = []
        self.pos = {e: 0 for e in ENGS}
        self.known = {e: [0] * len(ENGS) for e in ENGS}
        self.known_dma = {e: {} for e in ENGS}
        self.ring_next = {e: 0 for e in ENGS}
        self.ring_last = {}
        self.last_op = {e: None for e in ENGS}
        self.n_waits = 0

    def _need(self, X, Y, raw):
        if Y is X:
            return
        E = X.eng
        if Y.dma:
            key = (Y.eng, Y.slot)
            if self.known_dma[E].get(key, 0) >= Y.val:
                return
            self.known_dma[E][key] = Y.val
            X.waits.append(Y)
            self._merge(E, Y.clock)
            return
        if Y.eng == E and not X.dma:
            if E == "pe" or (not raw and not STRICT):
                return
        pi = EIDX[Y.eng]
        if self.known[E][pi] >= Y.pos:
            return
        Y.signal = True
        X.waits.append(Y)
        self._merge(E, Y.clock)

    def _merge(self, E, clock):
        k = self.known[E]
        for i, v in enumerate(clock):
            if v > k[i]:
                k[i] = v

    def op(self, eng, fn, reads=(), writes=(), dma=False):
        X = Op()
        X.eng = eng; X.fn = fn; X.dma = dma; X.waits = []; X.signal = False
        X.slot = None; X.val = 0; X.sidx = 0
        self.pos[eng] += 1
        X.pos = self.pos[eng]
        deps = {}
        for v in reads:
            rb = v.r.b; B = v.r.B
            for bi in range(v.lo // B, (v.hi - 1) // B + 1):
                for r in rb.get(bi, ()):
                    if r[3] and r[0] < v.hi and v.lo < r[1]:
                        deps[id(r[2])] = (r[2], True)
        for v in writes:
            rb = v.r.b; B = v.r.B
            for bi in range(v.lo // B, (v.hi - 1) // B + 1):
                for r in rb.get(bi, ()):
                    if r[0] < v.hi and v.lo < r[1]:
                        k = id(r[2])
                        if k not in deps:
                            deps[k] = (r[2], r[3] and False)
        for Y, raw in sorted(deps.values(), key=lambda t: -t[0].pos):
            self._need(X, Y, raw)
        if dma:
            s = self.ring_next[eng]; self.ring_next[eng] = (s + 1) % RING
            prev = self.ring_last.get((eng, s))
            if prev is not None:
                self._need(X, prev, False)
                X.val = prev.val + 16
            else:
                X.val = 16
            X.slot = s
            self.ring_last[(eng, s)] = X
        ck = list(self.known[eng])
        if not dma:
            ck[EIDX[eng]] = X.pos
        X.clock = ck
        for v in reads:
            rb = v.r.b; B = v.r.B
            rec = (v.lo, v.hi, X, False)
            for bi in range(v.lo // B, (v.hi - 1) // B + 1):
                lst = rb.get(bi)
                if lst is None:
                    rb[bi] = [rec]; continue
                if not dma:
                    lst[:] = [r for r in lst if not ((not r[3]) and (not r[2].dma) and r[2].eng == eng
                                                     and v.lo <= r[0] and r[1] <= v.hi)]
                lst.append(rec)
        for v in writes:
            rb = v.r.b; B = v.r.B
            rec = (v.lo, v.hi, X, True)
            for bi in range(v.lo // B, (v.hi - 1) // B + 1):
                lst = rb.get(bi)
                if lst is None:
                    rb[bi] = [rec]; continue
                lst[:] = [r for r in lst if not (v.lo <= r[0] and r[1] <= v.hi)]
                lst.append(rec)
        self.ops.append(X)
        if not dma:
            self.last_op[eng] = X
        return X

    def finish(self):
        X = Op()
        X.eng = "sp"; X.fn = None; X.dma = False; X.waits = []; X.signal = False
        X.slot = None; X.val = 0; X.sidx = 0
        self.pos["sp"] += 1; X.pos = self.pos["sp"]
        for (e, s), Dm in self.ring_last.items():
            self._need(X, Dm, False)
        for e in ENGS:
            if e != "sp" and self.last_op[e] is not None:
                Y = self.last_op[e]
                if self.known["sp"][EIDX[e]] < Y.pos:
                    Y.signal = True; X.waits.append(Y)
        X.clock = list(self.known["sp"])
        self.ops.append(X)

    def emit(self, stack):
        nc = self.nc
        engobj = {"pe": nc.tensor, "act": nc.scalar, "dve": nc.vector, "pool": nc.gpsimd, "sp": nc.sync}
        nsig = {e: 0 for e in ENGS}
        for X in self.ops:
            if X.signal:
                nsig[X.eng] += 1
        sems = {}
        for e in ENGS:
            n = (nsig[e] + EPOCH - 1) // EPOCH
            sems[e] = [stack.enter_context(nc.semaphore(f"s_{e}_{i}")) for i in range(n)]
        rings = {}
        for (e, s) in self.ring_last:
            rings[(e, s)] = stack.enter_context(nc.semaphore(f"r_{e}_{s}"))
        cnt = {e: 0 for e in ENGS}
        nw = 0
        for X in self.ops:
            eo = engobj[X.eng]
            for Y in X.waits:
                if Y.dma:
                    eo.wait_ge(rings[(Y.eng, Y.slot)], Y.val)
                else:
                    n = Y.sidx
                    assert n > 0
                    eo.wait_ge(sems[Y.eng][(n - 1) // EPOCH], (n - 1) % EPOCH + 1)
                nw += 1
            if X.fn is None:
                continue
            ins = X.fn(eo)
            if X.dma:
                ins.then_inc(rings[(X.eng, X.slot)], 16)
            if X.signal:
                cnt[X.eng] += 1
                X.sidx = cnt[X.eng]
                ins.then_inc(sems[X.eng][(X.sidx - 1) // EPOCH], 1)
        self.n_waits = nw
        return nsig


class Alloc:
    def __init__(self, K, base=0):
        self.K = K; self.off = base

    def take(self, shape, dt):
        n = 1
        for d in shape[1:]:
            n *= d
        nb = n * ESZ[dt]
        off = (self.off + 31) // 32 * 32
        assert off + nb <= self.K.arena_limit, ("arena overflow", off + nb, self.K.arena_limit)
        self.off = off + nb
        return self.K.carve(off, shape, dt)


class K:
    def __init__(self):
        self.nc = bass.Bass("TRN2", target_bir_lowering=False)
        self.S = Sched(self.nc)
        self.arena_h = self.nc.alloc_sbuf_tensor("arena", [128, ARENA_BYTES // 2], BF16)
        self.arena_r = Region(ARENA_BYTES)
        self.arena_limit = ARENA_BYTES
        self.banks = []
        for i in range(8):
            h = self.nc.alloc_psum_tensor(f"bank{i}", [128, 512], F32)
            self.banks.append((h, Region(2048, 256)))
        self.dr = {}

    def carve(self, off, shape, dt, parts=128):
        esz = ESZ[dt]
        n = 1
        for d in shape[1:]:
            n *= d
        assert off % 2 == 0
        ap = self.arena_h[0:shape[0], off // 2: off // 2 + n * esz // 2]
        if dt != BF16:
            ap = ap.bitcast(dt)
        if len(shape) == 3:
            ap = ap.rearrange("p (a b) -> p a b", a=shape[1])
        elif len(shape) == 4:
            ap = ap.rearrange("p (a b c) -> p a b c", a=shape[1], b=shape[2])
        return Sub(ap, self.arena_r, off, esz, shape, 1)

    def pbank(self, i, shape, dt=F32, coff=0):
        h, r = self.banks[i]
        esz = ESZ[dt]
        n = 1
        for d in shape[1:]:
            n *= d
        ap = h[0:shape[0], coff // 4: coff // 4 + n * esz // 4]
        if dt != F32:
            ap = ap.bitcast(dt)
        if len(shape) == 3:
            ap = ap.rearrange("p (a b) -> p a b", a=shape[1])
        sb = Sub(ap, r, coff, esz, shape, 1)
        sb.whole = True
        return sb

    def dram(self, name, shape, dt, kind="Internal"):
        h = self.nc.dram_tensor(name, list(shape), dt, kind=kind)
        n = 1
        for d in shape:
            n *= d
        r = Region(n * ESZ[dt], max(4096, n * ESZ[dt] // 512))
        s = Sub(h[tuple(slice(None) for _ in shape)], r, 0, ESZ[dt], shape, 0)
        s.h = h
        self.dr[name] = s
        return s

    def mm(self, out, lhsT, rhs, start, stop):
        self.S.op("pe", lambda e: e.matmul(out.ap, lhsT=lhsT.ap, rhs=rhs.ap, start=start, stop=stop),
                  reads=[lhsT, rhs], writes=[out])

    def tr(self, out, in_, ident):
        self.S.op("pe", lambda e: e.transpose(out=out.ap, in_=in_.ap, identity=ident.ap),
                  reads=[in_, ident], writes=[out])

    def act(self, out, in_, func, bias=None, scale=None, accum=None):
        kw = {}
        rd = [in_]; wr = [out]
        if bias is not None:
            if isinstance(bias, View):
                kw["bias"] = bias.ap; rd.append(bias)
            else:
                kw["bias"] = bias
        if scale is not None:
            if isinstance(scale, View):
                kw["scale"] = scale.ap; rd.append(scale)
            else:
                kw["scale"] = scale
        if accum is not None:
            kw["accum_out"] = accum.ap; wr.append(accum)
        self.S.op("act", lambda e: e.activation(out=out.ap, in_=in_.ap, func=func, **kw), reads=rd, writes=wr)

    def tsc(self, eng, out, in0, s1, s2, op0, op1=None):
        rd = [in0]
        a1 = s1.ap if isinstance(s1, View) else s1
        a2 = s2.ap if isinstance(s2, View) else s2
        if isinstance(s1, View):
            rd.append(s1)
        if isinstance(s2, View):
            rd.append(s2)
        if op1 is None:
            self.S.op(eng, lambda e: e.tensor_scalar(out=out.ap, in0=in0.ap, scalar1=a1, scalar2=None, op0=op0),
                      reads=rd, writes=[out])
        else:
            self.S.op(eng, lambda e: e.tensor_scalar(out=out.ap, in0=in0.ap, scalar1=a1, scalar2=a2, op0=op0, op1=op1),
                      reads=rd, writes=[out])

    def stt(self, eng, out, in0, sc, in1, op0, op1):
        rd = [in0, in1]
        a = sc.ap if isinstance(sc, View) else sc
        if isinstance(sc, View):
            rd.append(sc)
        self.S.op(eng, lambda e: e.scalar_tensor_tensor(out=out.ap, in0=in0.ap, scalar=a, in1=in1.ap, op0=op0, op1=op1),
                  reads=rd, writes=[out])

    def tt(self, eng, out, in0, in1, op):
        self.S.op(eng, lambda e: e.tensor_tensor(out=out.ap, in0=in0.ap, in1=in1.ap, op=op),
                  reads=[in0, in1], writes=[out])

    def cp(self, eng, out, in_):
        if eng == "act":
            self.S.op("act", lambda e: e.copy(out=out.ap, in_=in_.ap), reads=[in_], writes=[out])
        else:
            self.S.op(eng, lambda e: e.tensor_copy(out=out.ap, in_=in_.ap), reads=[in_], writes=[out])

    def memset(self, eng, out, val):
        self.S.op(eng, lambda e: e.memset(out.ap, val), reads=[], writes=[out])

    def recip(self, out, in_):
        self.S.op("dve", lambda e: e.reciprocal(out=out.ap, in_=in_.ap), reads=[in_], writes=[out])

    def dma(self, eng, out, in_, slow=False):
        if slow:
            self.S.op(eng, lambda e: e.dma_start(out=out.ap, in_=in_.ap, allow_slow_non_contiguous=True),
                      reads=[in_], writes=[out], dma=True)
        else:
            self.S.op(eng, lambda e: e.dma_start(out=out.ap, in_=in_.ap), reads=[in_], writes=[out], dma=True)


def bcast_last(v, sub, n):
    ap = v.ap.unsqueeze(2).broadcast_to([v.ap.shape[0], v.ap.shape[1], n])
    return View(ap, v.r, v.lo, v.hi)


C_ID, C_M2, C_M2S, C_U2S, C_NEG, C_ONE, C_INVF = 0, 128, 256, 384, 512, 640, 768
NCONST = 776


def make_consts():
    c = np.zeros((128, NCONST), np.float32)
    c[:, C_ID:C_ID + 128] = np.eye(128, dtype=np.float32)
    j = np.arange(128)[:, None]; i = np.arange(128)[None, :]
    same = (j // 64) == (i // 64)
    c[:, C_M2:C_M2 + 128] = (same & (j <= i)).astype(np.float32)
    c[:, C_M2S:C_M2S + 128] = -(same & (j <= i)).astype(np.float32) / 16.0
    c[:, C_U2S:C_U2S + 128] = -(same & (j > i)).astype(np.float32) / 16.0
    c[:, C_NEG:C_NEG + 128] = np.where(j <= i, 0.0, -30000.0).astype(np.float32)
    c[:, C_ONE:C_ONE + 128] = 1.0
    invf = (np.float32(10000.0) ** (-np.arange(0, 64, 2, dtype=np.float32) / np.float32(64))).astype(np.float32)
    c[0:32, C_INVF] = invf; c[32:64, C_INVF] = invf
    return c


LIGHT_BASE = 49152
ATTN_BASE = 143744
MLAO_BASE = 108544
CONST_BYTES = 3200


def build(plan, debug_out=None):
    kb = K()
    S = kb.S
    nc = kb.nc
    kb.arena_limit = ARENA_BYTES - CONST_BYTES
    din = {}
    def inp(name, shape, dt=F32):
        din[name] = kb.dram(name, shape, dt, kind="ExternalInput")
        return din[name]
    x_in = inp("x", [SEQ, D])
    c_in = inp("c_fm", [128, 8])
    pos_in = inp("pos", [1, SEQ], I32)
    consts_in = inp("consts", [128, NCONST])
    cond_w = inp("cond_w", [2, D, 9 * D]); cond_b = inp("cond_b", [2, 9 * D])
    norm_g = inp("norm_g", [2, 3, 2, D])
    ffn_w_gu = inp("ffn_w_gu", [2, 2, D, 2 * DFF]); ffn_w_down = inp("ffn_w_down", [2, 2, DFF, D])
    gla_w_in = inp("gla_w_in", [D, 3088]); gla_w_gate_up = inp("gla_w_gate_up", [16, 512])
    gla_b_gate = inp("gla_b_gate", [1, 512]); gla_g_out = inp("gla_g_out", [1, 256])
    gla_w_out = inp("gla_w_out", [D, D])
    kv_g_in = inp("kv_g_in", [1, D]); kv_cond_w = inp("kv_cond_w", [D, 2 * D]); kv_cond_b = inp("kv_cond_b", [1, 2 * D])
    mla_w_kv_a = inp("mla_w_kv_a", [D, 320]); mla_g_kv = inp("mla_g_kv", [1, 256])
    mla_w_kv_b = inp("mla_w_kv_b", [256, 2048])
    mla_w_dq = inp("mla_w_dq", [D, 384]); mla_g_q = inp("mla_g_q", [1, 384])
    mla_w_uq = inp("mla_w_uq", [384, 1536]); mla_w_out = inp("mla_w_out", [D, D])
    mods_in = inp("mods_dbg", [1, 20480]) if "mods" not in plan else None
    xkv_in = inp("x_kv", [SEQ, D]) if ("kv" in plan and "ffn0b" not in plan) else None
    kb.din = din
    y_out = kb.dram("y", [SEQ, D], F32, kind="ExternalOutput")
    mods_d = kb.dram("mods_d", [1, 20480], F32)
    xs = [kb.dram(f"xs{i}", [SEQ, D], F32) for i in range(5)]
    qT_d = kb.dram("qT_d", [8, 192, SEQ], BF16)
    kT_d = kb.dram("kT_d", [8, 128, SEQ], BF16)
    krT_d = kb.dram("krT_d", [64, SEQ], BF16)
    v_d = kb.dram("v_d", [SEQ, D], BF16)
    oT_d = kb.dram("oT_d", [8, 128, SEQ], BF16)
    dbg = {}

    cbase = ARENA_BYTES - CONST_BYTES
    identf = kb.carve(cbase, [128, 128], F32)
    identb = kb.carve(cbase + 512, [128, 128], BF16)
    m2b = kb.carve(cbase + 768, [128, 128], BF16)
    negb = kb.carve(cbase + 1024, [128, 128], BF16)
    oneb = kb.carve(cbase + 1280, [128, 128], BF16)
    m2sf = kb.carve(cbase + 1536, [128, 128], F32)
    u2sf = kb.carve(cbase + 2048, [128, 128], F32)
    invf = kb.carve(cbase + 2560, [128, 8], F32)
    onef = kb.carve(cbase + 2592, [128, 128], F32)
    epsc = kb.carve(cbase + 3104, [128, 8], F32)
    mab = kb.carve(cbase + 3136, [128, 2], F32)
    kb.memset("dve", epsc.all(), EPS)
    cstage = kb.carve(0, [128, 640], F32)
    kb.dma("sp", identf.all(), consts_in[:, C_ID:C_ID + 128])
    kb.dma("sp", m2sf.all(), consts_in[:, C_M2S:C_M2S + 128])
    kb.dma("sp", u2sf.all(), consts_in[:, C_U2S:C_U2S + 128])
    kb.dma("sp", invf.all(), consts_in[:, C_INVF:C_INVF + 8])
    kb.dma("sp", onef.all(), consts_in[:, C_ONE:C_ONE + 128])
    kb.dma("sp", cstage[:, 0:128], consts_in[:, C_ID:C_ID + 128])
    kb.dma("sp", cstage[:, 128:256], consts_in[:, C_M2:C_M2 + 128])
    kb.dma("sp", cstage[:, 256:384], consts_in[:, C_NEG:C_NEG + 128])
    kb.dma("sp", cstage[:, 384:512], consts_in[:, C_ONE:C_ONE + 128])
    kb.cp("dve", identb.all(), cstage[:, 0:128])
    kb.cp("dve", m2b.all(), cstage[:, 128:256])
    kb.cp("dve", negb.all(), cstage[:, 256:384])
    kb.cp("dve", oneb.all(), cstage[:, 384:512])
    kb.cp("dve", mab[:, 0:1], cstage[:, 128 + 63:128 + 64])
    kb.cp("dve", mab[:, 1:2], cstage[:, 128 + 127:128 + 128])

    plan_s = ["mla" if p == "mlaq" else p for p in plan]
    stream_phases = [p for p in plan_s if p in ("ffn0a", "gla", "ffn0b", "ffn1a", "mla", "ffn1b")]
    route = {"ffn0a": (x_in, xs[0]), "gla": (xs[0], xs[1]), "ffn0b": (xs[1], xs[2]), "ffn1a": (xs[2], xs[3]),
             "mla": (xs[3], xs[4]), "ffn1b": (xs[4], y_out)}
    if stream_phases:
        route[stream_phases[0]] = (x_in, route[stream_phases[0]][1])
        route[stream_phases[-1]] = (route[stream_phases[-1]][0], y_out)
    kv_src = xs[2]
    if "kv" in plan and "ffn0b" not in plan:
        kv_src = xkv_in
    mods_src = mods_d if "mods" in plan else mods_in

    mods_state = {}

    def mods_setup(base):
        A = Alloc(kb, base)
        cfm = A.take([128, 8], F32)
        cact = A.take([128, 8], F32)
        wt = [A.take([128, 8, 512], BF16) for _ in range(2)]
        cactb = A.take([128, 8], BF16)
        brow = [A.take([1, 512], F32) for _ in range(3)]
        orow = [A.take([1, 512], F32) for _ in range(3)]
        kb.dma("sp", cfm.all(), c_in.all())
        kb.act(cact.all(), cfm.all(), AF.Silu)
        kb.cp("dve", cactb.all(), cact.all())
        jobs = []
        for n in range(18):
            jobs.append((cond_w, 0, cond_b, n, n * 512))
        for n in range(4):
            jobs.append((kv_cond_w, None, kv_cond_b, n, 18432 + n * 512))
        for n in range(18):
            jobs.append((cond_w, 1, cond_b, n, 9216 + n * 512))
        mods_state.update(cact=cactb, wt=wt, brow=brow, orow=orow, jobs=jobs, next=mods_state.get("next", 0))
        mods_state["next_load"] = mods_state["next"]
        mods_load(); mods_load()

    def mods_load():
        jl = mods_state.get("next_load", 0)
        jl = max(jl, mods_state["next"])
        if jl >= len(mods_state["jobs"]) or jl - mods_state["next"] >= 2:
            mods_state["next_load"] = jl
            return
        wd, l, bd, n, off = mods_state["jobs"][jl]
        w_ = mods_state["wt"][jl % 2]; b_ = mods_state["brow"][jl % 3]
        if l is None:
            src = wd.cv(wd.h[:, n * 512:(n + 1) * 512].rearrange("(k p) n -> p k n", p=128))
            bsrc = bd[0:1, n * 512:(n + 1) * 512]
        else:
            src = wd.cv(wd.h[l, :, n * 512:(n + 1) * 512].rearrange("(k p) n -> p k n", p=128),
                        l * D * 9216, (l + 1) * D * 9216)
            bsrc = bd[l:l + 1, n * 512:(n + 1) * 512]
        kb.dma("pool", w_.all(), src)
        kb.dma("sp", b_.all(), bsrc)
        mods_state["next_load"] = jl + 1

    def mods_job(pbanks=(0, 1)):
        ji = mods_state["next"]
        if ji >= len(mods_state["jobs"]):
            return False
        if mods_state.get("next_load", 0) <= ji:
            mods_state["next_load"] = ji
            mods_load()
        mods_state["next"] = ji + 1
        wd, l, bd, n, off = mods_state["jobs"][ji]
        w_ = mods_state["wt"][ji % 2]; b_ = mods_state["brow"][ji % 3]; o_ = mods_state["orow"][ji % 3]
        cact = mods_state["cact"]
        pm = kb.pbank(pbanks[ji % len(pbanks)], [128, 512])
        for k in range(8):
            kb.mm(pm[0:1, :], cact[:, k:k + 1], w_[:, k, :], k == 0, k == 7)
        kb.tt("dve", o_.all(), pm[0:1, :], b_.all(), ALU.add)
        kb.dma("sp", mods_d[0:1, off:off + 512], o_.all())
        mods_load()
        return True

    def load_mod_vectors(A, moff, g_pre_v, g_post_v, rw):
        AT = A.take([128, 8], F32); BT = A.take([128, 8], F32)
        tmp = A.take([128, 8], F32)
        G = None
        if g_pre_v is not None:
            kb.dma("sp", BT.all(), mods_src.cv(mods_src.h[0, moff:moff + 1024].rearrange("(k p) -> p k", p=128),
                                               moff, moff + 1024), slow=True)
            kb.dma("sp", tmp.all(), mods_src.cv(mods_src.h[0, moff + 1024:moff + 2048].rearrange("(k p) -> p k", p=128),
                                                moff + 1024, moff + 2048), slow=True)
            kb.dma("sp", AT.all(), g_pre_v, slow=True)
            kb.stt("dve", AT.all(), tmp.all(), 1.0, AT.all(), ALU.add, ALU.mult)
        if g_post_v is not None:
            G = A.take([128, 1024], F32)
            gt = A.take([128, 1024], F32)
            kb.dma("sp", G.all(), mods_src.cv(mods_src.h[0, moff + 2048:moff + 3072].partition_broadcast(128),
                                              moff + 2048, moff + 3072))
            kb.dma("sp", gt.all(), g_post_v)
            kb.stt("dve", G.all(), G.all(), float(rw), gt.all(), ALU.mult, ALU.mult)
            A.off -= 4096
        return AT, BT, G

    def rstd(out, ss, tmp, n):
        kb.act(tmp, ss, AF.Sqrt, bias=epsc[:, 0:1], scale=1.0 / n)
        kb.recip(out, tmp)

    def gvec_fm(t, idx):
        ap = t.h[idx].rearrange("(k p) -> p k", p=128)
        return t.cv(ap)

    def gvec_rep(t, idx, n=1024):
        ap = t.h[idx].partition_broadcast(128)
        return t.cv(ap)

    def make_scratch(A, n, with_xo=True):
        out = []
        for _ in range(n):
            d = dict(hb=A.take([128, D], BF16), tmpm=A.take([128, D], F32), junk=A.take([128, D], BF16),
                     stat=A.take([128, 16], F32))
            if with_xo:
                d["xo"] = A.take([128, D], F32)
            out.append(d)
        return out

    def g_prenorm(src, t, xt_v, sc, ptb, hT_dst, AT, BT):
        hb = sc["hb"]; stat = sc["stat"]; tmpm = sc["tmpm"]
        kb.dma("sp", xt_v, src[t * 128:(t + 1) * 128, :])
        yield
        kb.act(sc["junk"].all(), xt_v, AF.Square, accum=stat[:, 0:1])
        yield
        rstd(stat[:, 2:3], stat[:, 0:1], stat[:, 1:2], D)
        yield
        kb.act(hb.all(), xt_v, AF.Identity, scale=stat[:, 2:3])
        yield
        pT = kb.pbank(ptb, [128, 8, 128], BF16)
        for k in range(8):
            kb.tr(pT[:, k, :], hb[:, k * 128:(k + 1) * 128], identb.all())
        yield
        t3 = tmpm.cv(tmpm.ap.rearrange("p (a b) -> p a b", a=8))
        kb.tt("dve", t3, pT.all(), bcast_last(AT.all(), AT, 128), ALU.mult)
        yield
        kb.tt("pool", hT_dst, t3, bcast_last(BT.all(), BT, 128), ALU.add)
        yield

    def prenorm_tile(*a):
        for _ in g_prenorm(*a):
            pass

    def g_postnorm(py_views, xt_v, sc, G, dst, t):
        stat = sc["stat"]; xo = sc["xo"]; junk = sc["junk"]
        kb.act(junk[:, 0:512], py_views[0], AF.Square, accum=stat[:, 4:5])
        kb.act(junk[:, 512:1024], py_views[1], AF.Square, accum=stat[:, 5:6])
        yield
        kb.tt("dve", stat[:, 6:7], stat[:, 4:5], stat[:, 5:6], ALU.add)
        yield
        rstd(stat[:, 7:8], stat[:, 6:7], stat[:, 3:4], D)
        yield
        for hf in range(2):
            kb.stt("dve", xo[:, hf * 512:(hf + 1) * 512], py_views[hf], stat[:, 7:8], G[:, hf * 512:(hf + 1) * 512],
                   ALU.mult, ALU.mult)
        yield
        kb.tt("pool", xo.all(), xo.all(), xt_v, ALU.add)
        yield
        kb.dma("sp", dst[t * 128:(t + 1) * 128, :], xo.all())
        yield

    def postnorm_tile(*a):
        for _ in g_postnorm(*a):
            pass

    def multi_chain(gens):
        gens = list(gens)
        while gens:
            for g in list(gens):
                try:
                    next(g)
                except StopIteration:
                    gens.remove(g)
            yield

    def run_tasks(tasks, width=4):
        n = len(tasks)
        done = [False] * n; started = [False] * n; gens = [None] * n
        active = []
        first_unstarted = 0
        while True:
            while len(active) < width:
                cand = None
                for j in range(first_unstarted, n):
                    if not started[j] and all(done[d] for d in tasks[j][1]):
                        cand = j; break
                if cand is None:
                    break
                started[cand] = True; gens[cand] = tasks[cand][0](); active.append(cand)
                while first_unstarted < n and started[first_unstarted]:
                    first_unstarted += 1
            if not active:
                break
            for j in list(active):
                try:
                    next(gens[j])
                except StopIteration:
                    done[j] = True; active.remove(j)
        assert all(done), "task graph stuck"

    def ffn_layout():
        A = Alloc(kb)
        WGU = A.take([128, 8, 2 * DFF], BF16)
        WDN = A.take([128, 22, D], BF16)
        return A, WGU, WDN

    def ffn_load(layer, which, lo=0, hi=1 << 30):
        _, WGU, WDN = ffn_layout()
        wg = ffn_w_gu; wd = ffn_w_down
        for k in range(8):
            for hf in range(2):
                dv = WGU[:, k, hf * DFF:(hf + 1) * DFF]
                if lo < dv.hi <= hi:
                    kb.dma("pool", dv, wg[layer, which, k * 128:(k + 1) * 128, hf * DFF:(hf + 1) * DFF])
        for m in range(0, 22, 2):
            base = ((layer * 2 + which) * DFF + m * 128) * D
            dv = WDN[:, m:m + 2, :]
            if lo < dv.hi <= hi:
                kb.dma("pool", dv,
                       wd.cv(wd.h[layer, which, m * 128:(m + 2) * 128, :].rearrange("(a p) n -> p a n", p=128),
                             base, base + 256 * D))

    def phase_ffn(name, layer, which, moff, gidx, preloaded=False):
        src, dst = route[name]
        A, WGU, WDN = ffn_layout()
        if not preloaded:
            ffn_load(layer, which)
        AT, BT, G = load_mod_vectors(A, moff, gvec_fm(norm_g, (layer, gidx, 0)), gvec_rep(norm_g, (layer, gidx, 1)), 0.5)
        xt = A.take([128, 2, D], F32)
        xr = A.take([128, 1, D], F32)
        hT = A.take([128, 2, 8, 512], BF16)
        actb = A.take([128, 22, 512], BF16)
        sg = A.take([128, 2, 512], F32)
        sc = make_scratch(A, 1)[0]

        def pre(g):
            for j in range(4):
                t = g * 4 + j
                prenorm_tile(src, t, xt[:, t % 2, :], sc, 0, hT[:, g % 2, :, j * 128:(j + 1) * 128], AT, BT)

        pre(0)
        for g in range(SEQ // 512):
            hs = g % 2
            for m in range(22):
                pg = kb.pbank(1 + (m % 2), [128, 512])
                pu = kb.pbank(3 + (m % 2), [128, 512])
                for k in range(8):
                    kb.mm(pg.all(), WGU[:, k, m * 128:(m + 1) * 128], hT[:, hs, k, :], k == 0, k == 7)
                for k in range(8):
                    kb.mm(pu.all(), WGU[:, k, DFF + m * 128:DFF + (m + 1) * 128], hT[:, hs, k, :], k == 0, k == 7)
                kb.act(sg[:, m % 2, :], pg.all(), AF.Silu)
                kb.tt("dve", actb[:, m, :], sg[:, m % 2, :], pu.all(), ALU.mult)
            if g + 1 < SEQ // 512:
                pre(g + 1)
            for j in range(4):
                t = g * 4 + j
                kb.dma("sp", xr[:, 0, :], src[t * 128:(t + 1) * 128, :])
                pys = []
                for hf in range(2):
                    py = kb.pbank(5 + ((2 * t + hf) % 3), [128, 512])
                    for m in range(22):
                        kb.mm(py.all(), actb[:, m, j * 128:(j + 1) * 128], WDN[:, m, hf * 512:(hf + 1) * 512], m == 0, m == 21)
                    pys.append(py.all())
                postnorm_tile(pys, xr[:, 0, :], sc, G, dst, t)


    def phase_gla():
        src, dst = route["gla"]
        A = Alloc(kb)
        WIN = A.take([128, 8, 3088], BF16)
        WOUT = A.take([128, 8, D], BF16)
        WG = A.take([16, 512], F32)
        BG = A.take([1, 512], F32)
        GO = A.take([128, 256], F32)
        AT, BT, G = load_mod_vectors(A, 3072, gvec_fm(norm_g, (0, 1, 0)), gvec_rep(norm_g, (0, 1, 1)), 1.0)
        S32 = A.take([128, 4, 256], F32)
        Sb = A.take([128, 3, 4, 256], BF16)
        xt = A.take([128, 2, D], F32)
        hT = A.take([128, 2, 8, 128], BF16)
        scs = make_scratch(A, 2)
        gT_sb = A.take([16, 128], F32)
        e1 = A.take([128, 512], F32)
        lsp = A.take([128, 512], F32)
        eb = A.take([128, 4, 128], F32)
        enb = A.take([128, 4, 128], F32)
        qd = A.take([128, 4, 128], BF16)
        qdA = A.take([128, 4, 128], BF16)
        qdB = A.take([128, 4, 128], BF16)
        kd = A.take([128, 4, 128], BF16)
        eu = A.take([128, 512], F32)
        ku = A.take([128, 512], BF16)
        vb = A.take([128, D], BF16)
        vm = A.take([128, 4, 2, 256], BF16)
        attT = A.take([128, 4, 128], BF16)
        sr = A.take([128, D], F32)
        on = A.take([128, D], F32)
        og = A.take([128, D], BF16)
        ogT = A.take([128, 8, 128], BF16)
        if "mods" in plan:
            mods_setup((A.off + 63) // 64 * 64)
        for k in range(8):
            kb.dma("pool", WIN[:, k, :], gla_w_in[k * 128:(k + 1) * 128, :])
        for k in range(0, 8, 2):
            kb.dma("pool", WOUT[:, k:k + 2, :],
                   gla_w_out.cv(gla_w_out.h[k * 128:(k + 2) * 128, :].rearrange("(a p) n -> p a n", p=128),
                                k * 128 * D, (k + 2) * 128 * D))
        kb.dma("sp", WG.all(), gla_w_gate_up.all())
        kb.dma("sp", BG.all(), gla_b_gate.all())
        kb.dma("sp", GO.all(), gla_g_out.cv(gla_g_out.h[0].partition_broadcast(128)))
        kb.memset("dve", S32.all(), 0.0)
        kb.memset("dve", Sb[:, 0, :, :], 0.0)
        kb.memset("pool", qdA.all(), 0.0)
        kb.memset("pool", qdB.all(), 0.0)
        qsc = 128.0 ** -0.5
        updb = [(1, 0), (1, 1024), (2, 0), (2, 1024), (3, 0), (3, 1024), (6, 0), (6, 1024)]
        stage = int(os.environ.get("GLA_STAGE", "99"))
        for t in range(int(os.environ.get("GLA_NT", str(NT)))):
            n = 2 * t
            hTt = lambda k: hT[:, t % 2, k, :]
            sc = scs[t % 2]; stat = sc["stat"]; junk = sc["junk"]
            prenorm_tile(src, t, xt[:, t % 2, :], sc, 0, hT[:, t % 2, :, :], AT, BT)
            qT_ps = kb.pbank(1, [128, 4, 128]); kT_ps = kb.pbank(2, [128, 4, 128])
            ktok_ps = kb.pbank(3, [128, 512])
            gT_ps = kb.pbank(7, [128, 128])
            for h in range(4):
                for k in range(8):
                    kb.mm(qT_ps[:, h, :], WIN[:, k, h * 128:(h + 1) * 128], hTt(k), k == 0, k == 7)
            for h in range(4):
                for k in range(8):
                    kb.mm(kT_ps[:, h, :], WIN[:, k, 512 + h * 128:512 + (h + 1) * 128], hTt(k), k == 0, k == 7)
            for k in range(8):
                kb.mm(ktok_ps.all(), hTt(k), WIN[:, k, 512:1024], k == 0, k == 7)
            for k in range(8):
                kb.mm(gT_ps[0:16, :], WIN[:, k, 2048:2064], hTt(k), k == 0, k == 7)
            v_ps = [kb.pbank(4, [128, 512]), kb.pbank(5, [128, 512])]
            for nh in range(2):
                for k in range(8):
                    kb.mm(v_ps[nh].all(), hTt(k), WIN[:, k, 1024 + nh * 512:1024 + (nh + 1) * 512], k == 0, k == 7)
            kb.cp("act", vb[:, 0:512], v_ps[0].all())
            kb.cp("act", vb[:, 512:1024], v_ps[1].all())
            for nh in range(2):
                for c in range(2):
                    dstv = vm.cv(vm.ap[:, 2 * nh:2 * nh + 2, c, :], 2 * nh * 512, (2 * nh + 2) * 512)
                    srcv = View(v_ps[nh].ap.rearrange("p (h e) -> p h e", h=2), v_ps[nh].r, 0, 2048)
                    kb.act(dstv, srcv, AF.Identity, scale=mab[:, c:c + 1])
            if stage < 2:
                kb.dma("sp", dst[t * 128:(t + 1) * 128, :], xt[:, t % 2, :]); continue
            kb.cp("dve", gT_sb.all(), gT_ps[0:16, :])
            z_ps = kb.pbank(6, [128, 512])
            kb.mm(z_ps.all(), gT_sb.all(), WG.all(), True, False)
            kb.mm(z_ps.all(), onef[0:1, 0:128], BG.all(), False, True)
            kb.act(e1.all(), z_ps.all(), AF.Exp, scale=-1.0)
            kb.act(lsp.all(), e1.all(), AF.Ln, bias=onef[:, 0:1])
            if stage < 3:
                kb.dma("sp", dst[t * 128:(t + 1) * 128, :], xt[:, t % 2, :]); continue
            r_ps = [kb.pbank(4, [128, 512]), kb.pbank(5, [128, 512])]
            for nh in range(2):
                for k in range(8):
                    kb.mm(r_ps[nh].all(), hTt(k), WIN[:, k, 2064 + nh * 512:2064 + (nh + 1) * 512], k == 0, k == 7)
            kb.act(sr[:, 0:512], r_ps[0].all(), AF.Silu)
            kb.act(sr[:, 512:1024], r_ps[1].all(), AF.Silu)
            if stage < 4:
                kb.dma("sp", dst[t * 128:(t + 1) * 128, :], xt[:, t % 2, :]); continue
            bT_ps = kb.pbank(6, [128, 4, 128])
            U_ps = kb.pbank(7, [128, 512])
            for h in range(4):
                kb.mm(bT_ps[:, h, :], lsp[:, h * 128:(h + 1) * 128], m2sf.all(), True, True)
            kb.mm(U_ps.all(), u2sf.all(), lsp.all(), True, True)
            kb.act(eb.all(), bT_ps.all(), AF.Exp)
            kb.act(enb.all(), bT_ps.all(), AF.Exp, scale=-1.0)
            kb.act(eu.all(), U_ps.all(), AF.Exp)
            kb.stt("dve", qd.all(), qT_ps.all(), qsc, eb.all(), ALU.mult, ALU.mult)
            kb.tt("dve", kd.all(), kT_ps.all(), enb.all(), ALU.mult)
            kb.cp("pool", qdA[:, :, 0:64], qd[:, :, 0:64])
            kb.cp("pool", qdB[:, :, 64:128], qd[:, :, 64:128])
            kb.tt("dve", ku.all(), ktok_ps.all(), eu.all(), ALU.mult)
            if stage < 5:
                kb.dma("sp", dst[t * 128:(t + 1) * 128, :], xt[:, t % 2, :]); continue
            sub = int(os.environ.get("GLA_SUB", "9"))
            att_ps = kb.pbank(0, [128, 4, 128])
            for h in range(4):
                kb.mm(att_ps[:, h, :], kd[:, h, :], qd[:, h, :], True, True)
            upd = []
            for h in range(4):
                u_ps = kb.pbank((1, 2, 3, 6)[h], [128, 2, 256])
                if sub >= 2:
                    kb.mm(u_ps.all(), ku[:, h * 128:(h + 1) * 128], vm[:, h, :, :], True, True)
                upd.append(u_ps[:, 0, :]); upd.append(u_ps[:, 1, :])
            if sub >= 4:
                for h in range(4):
                    kb.tt("dve", attT[:, h, :], att_ps[:, h, :], m2b.all(), ALU.mult)
            o_ps = [kb.pbank((4, 5, 7, 0)[h], [128, 256]) for h in range(4)]
            if sub >= 5:
                for h in range(4):
                    kb.mm(o_ps[h].all(), attT[:, h, :], vb[:, h * 256:(h + 1) * 256], True, False)
                    kb.mm(o_ps[h].all(), qdA[:, h, :], Sb[:, n % 3, h, :], False, sub < 6)
            if sub >= 6:
                for h in range(4):
                    dA = eb[:, h, 63:64]; dB = eb[:, h, 127:128]
                    kb.stt("dve", Sb[:, (n + 1) % 3, h, :], S32[:, h, :], dA, upd[2 * h], ALU.mult, ALU.add)
                    kb.stt("dve", S32[:, h, :], S32[:, h, :], dA, upd[2 * h], ALU.mult, ALU.add)
                for h in range(4):
                    kb.mm(o_ps[h].all(), qdB[:, h, :], Sb[:, (n + 1) % 3, h, :], False, True)
                for h in range(4):
                    dB = eb[:, h, 127:128]
                    kb.stt("dve", Sb[:, (n + 2) % 3, h, :], S32[:, h, :], dB, upd[2 * h + 1], ALU.mult, ALU.add)
                    kb.stt("dve", S32[:, h, :], S32[:, h, :], dB, upd[2 * h + 1], ALU.mult, ALU.add)
            if stage < 6:
                kb.dma("sp", dst[t * 128:(t + 1) * 128, :], xt[:, t % 2, :]); continue
            for h in range(4):
                kb.act(junk[:, h * 256:(h + 1) * 256], o_ps[h].all(), AF.Square, accum=stat[:, 8 + h:9 + h])
            kb.act(stat[:, 4:8], stat[:, 8:12], AF.Sqrt, bias=epsc[:, 0:1], scale=1.0 / 256)
            kb.recip(stat[:, 12:16], stat[:, 4:8])
            for h in range(4):
                kb.stt("dve", on[:, h * 256:(h + 1) * 256], o_ps[h].all(), stat[:, 12 + h:13 + h], GO.all(), ALU.mult, ALU.mult)
            kb.tt("pool", og.all(), on.all(), sr.all(), ALU.mult)
            pT = kb.pbank(6, [128, 8, 128], BF16)
            for k in range(8):
                kb.tr(pT[:, k, :], og[:, k * 128:(k + 1) * 128], identb.all())
            kb.cp("act", ogT.all(), pT.all())
            pys = [kb.pbank(1, [128, 512]), kb.pbank(2, [128, 512])]
            for hf in range(2):
                for k in range(8):
                    kb.mm(pys[hf].all(), ogT[:, k, :], WOUT[:, k, hf * 512:(hf + 1) * 512], k == 0, k == 7)
            postnorm_tile([p.all() for p in pys], xt[:, t % 2, :], sc, G, dst, t)
            if "mods" in plan:
                mods_job((0,))


    def phase_gla2():
        src, dst = route["gla"]
        A = Alloc(kb)
        WIN = A.take([128, 8, 3088], BF16)
        WOUT = A.take([128, 8, D], BF16)
        WG = A.take([16, 512], F32)
        BG = A.take([1, 512], F32)
        GO = A.take([128, 256], F32)
        AT, BT, G = load_mod_vectors(A, 3072, gvec_fm(norm_g, (0, 1, 0)), gvec_rep(norm_g, (0, 1, 1)), 1.0)
        S32 = A.take([128, 4, 256], F32)
        Sb = A.take([128, 3, 4, 256], BF16)
        xt = A.take([128, 2, D], F32)
        hT = A.take([128, 2, 8, 128], BF16)
        scA = make_scratch(A, 2, with_xo=False)
        scB = [dict(xo=A.take([128, D], F32), junk=A.take([128, D], BF16), stat=A.take([128, 16], F32)) for _ in range(2)]
        qd = [A.take([128, 4, 128], BF16) for _ in range(2)]
        qdA = [A.take([128, 4, 128], BF16) for _ in range(2)]
        qdB = [A.take([128, 4, 128], BF16) for _ in range(2)]
        kd = [A.take([128, 4, 128], BF16) for _ in range(2)]
        ku = [A.take([128, 512], BF16) for _ in range(2)]
        vb = [A.take([128, D], BF16) for _ in range(2)]
        vm = [A.take([128, 4, 2, 256], BF16) for _ in range(2)]
        eb = [A.take([128, 4, 128], F32) for _ in range(2)]
        sr = [A.take([128, 4, 256], F32) for _ in range(2)]
        gT_sb = A.take([16, 128], F32)
        lsp = A.take([128, 512], F32)
        enb = A.take([128, 4, 128], F32)
        eu = A.take([128, 512], F32)
        attT = A.take([128, 4, 128], BF16)
        og = A.take([128, D], BF16)
        ogT = A.take([128, 8, 128], BF16)
        if "mods" in plan:
            mods_setup((A.off + 63) // 64 * 64)
        for k in range(8):
            kb.dma("pool", WIN[:, k, :], gla_w_in[k * 128:(k + 1) * 128, :])
        for k in range(0, 8, 2):
            kb.dma("pool", WOUT[:, k:k + 2, :],
                   gla_w_out.cv(gla_w_out.h[k * 128:(k + 2) * 128, :].rearrange("(a p) n -> p a n", p=128),
                                k * 128 * D, (k + 2) * 128 * D))
        kb.dma("sp", WG.all(), gla_w_gate_up.all())
        kb.dma("sp", BG.all(), gla_b_gate.all())
        kb.dma("sp", GO.all(), gla_g_out.cv(gla_g_out.h[0].partition_broadcast(128)))
        kb.memset("dve", S32.all(), 0.0)
        kb.memset("dve", Sb[:, 0, :, :], 0.0)
        for p in range(2):
            kb.memset("pool", qdA[p].all(), 0.0)
            kb.memset("pool", qdB[p].all(), 0.0)
        qsc = 128.0 ** -0.5
        ntl = int(os.environ.get("GLA_NT", str(NT)))

        def chain_a(t):
            p = t % 2
            hTt = lambda k: hT[:, p, k, :]
            yield from g_prenorm(src, t, xt[:, p, :], scA[p], 0, hT[:, p, :, :], AT, BT)
            qT_ps = kb.pbank(1, [128, 4, 128]); kT_ps = kb.pbank(2, [128, 4, 128])
            ktok_ps = kb.pbank(3, [128, 512])
            gT_ps = kb.pbank(4, [128, 128])
            for h in range(4):
                for k in range(8):
                    kb.mm(qT_ps[:, h, :], WIN[:, k, h * 128:(h + 1) * 128], hTt(k), k == 0, k == 7)
            yield
            for h in range(4):
                for k in range(8):
                    kb.mm(kT_ps[:, h, :], WIN[:, k, 512 + h * 128:512 + (h + 1) * 128], hTt(k), k == 0, k == 7)
            yield
            for k in range(8):
                kb.mm(gT_ps[0:16, :], WIN[:, k, 2048:2064], hTt(k), k == 0, k == 7)
            for k in range(8):
                kb.mm(ktok_ps.all(), hTt(k), WIN[:, k, 512:1024], k == 0, k == 7)
            yield
            kb.cp("dve", gT_sb.all(), gT_ps[0:16, :])
            sbank = [0, 4]
            v0 = kb.pbank(0, [128, 512])
            for k in range(8):
                kb.mm(v0.all(), hTt(k), WIN[:, k, 1024:1536], k == 0, k == 7)
            yield
            z_ps = kb.pbank(4, [128, 512])
            kb.mm(z_ps.all(), gT_sb.all(), WG.all(), True, False)
            kb.mm(z_ps.all(), onef[0:1, 0:128], BG.all(), False, True)
            yield

            def v_evac(v_ps, nh):
                kb.cp("act", vb[p][:, nh * 512:(nh + 1) * 512], v_ps.all())
                for c in range(2):
                    dstv = vm[p].cv(vm[p].ap[:, 2 * nh:2 * nh + 2, c, :], 2 * nh * 512, (2 * nh + 2) * 512)
                    srcv = View(v_ps.ap.rearrange("p (h e) -> p h e", h=2), v_ps.r, 0, 2048)
                    kb.act(dstv, srcv, AF.Identity, scale=mab[:, c:c + 1])
            v_evac(v0, 0)
            yield
            kb.act(lsp.all(), z_ps.all(), AF.Exp, scale=-1.0)
            yield
            v1 = kb.pbank(0, [128, 512])
            for k in range(8):
                kb.mm(v1.all(), hTt(k), WIN[:, k, 1536:2048], k == 0, k == 7)
            kb.act(lsp.all(), lsp.all(), AF.Ln, bias=onef[:, 0:1])
            yield
            v_evac(v1, 1)
            yield
            bT_ps = kb.pbank(4, [128, 4, 128])
            for h in range(4):
                kb.mm(bT_ps[:, h, :], lsp[:, h * 128:(h + 1) * 128], m2sf.all(), True, True)
            yield
            U_ps = kb.pbank(0, [128, 512])
            kb.mm(U_ps.all(), u2sf.all(), lsp.all(), True, True)
            yield
            kb.act(eb[p].all(), bT_ps.all(), AF.Exp)
            kb.act(enb.all(), bT_ps.all(), AF.Exp, scale=-1.0)
            yield
            kb.act(eu.all(), U_ps.all(), AF.Exp)
            yield
            r0 = kb.pbank(4, [128, 512]); r1 = kb.pbank(0, [128, 512])
            for k in range(8):
                kb.mm(r0.all(), hTt(k), WIN[:, k, 2064:2576], k == 0, k == 7)
            yield
            kb.stt("dve", qd[p].all(), qT_ps.all(), qsc, eb[p].all(), ALU.mult, ALU.mult)
            kb.tt("dve", kd[p].all(), kT_ps.all(), enb.all(), ALU.mult)
            yield
            for k in range(8):
                kb.mm(r1.all(), hTt(k), WIN[:, k, 2576:3088], k == 0, k == 7)
            kb.tt("dve", ku[p].all(), ktok_ps.all(), eu.all(), ALU.mult)
            yield
            kb.cp("pool", qdA[p][:, :, 0:64], qd[p][:, :, 0:64])
            kb.cp("pool", qdB[p][:, :, 64:128], qd[p][:, :, 64:128])
            srf = sr[p].cv(sr[p].ap.rearrange("p h e -> p (h e)"))
            kb.act(View(srf.ap[:, 0:512], srf.r, srf.lo, srf.hi), r0.all(), AF.Silu)
            yield
            kb.act(View(srf.ap[:, 512:1024], srf.r, srf.lo, srf.hi), r1.all(), AF.Silu)
            yield
            gob = View(GO.all().ap.unsqueeze(1).broadcast_to([128, 4, 256]), GO.r, GO.off, GO.off + GO.nbytes)
            kb.tt("pool", sr[p].all(), sr[p].all(), gob, ALU.mult)
            yield

        def chain_b(t):
            p = t % 2
            n = 2 * t
            sc = scB[p]; stat = sc["stat"]; junk = sc["junk"]
            att_ps = kb.pbank(5, [128, 4, 128])
            for h in range(4):
                kb.mm(att_ps[:, h, :], kd[p][:, h, :], qd[p][:, h, :], True, True)
            yield
            for h in range(4):
                kb.tt("dve", attT[:, h, :], att_ps[:, h, :], m2b.all(), ALU.mult)
            yield
            for h in range(4):
                u_ps = kb.pbank(6, [128, 2, 256])
                o_ps = kb.pbank(5 if h % 2 == 0 else 7, [128, 256])
                kb.mm(u_ps.all(), ku[p][:, h * 128:(h + 1) * 128], vm[p][:, h, :, :], True, True)
                kb.mm(o_ps.all(), attT[:, h, :], vb[p][:, h * 256:(h + 1) * 256], True, False)
                kb.mm(o_ps.all(), qdA[p][:, h, :], Sb[:, n % 3, h, :], False, False)
                yield
                dA = eb[p][:, h, 63:64]
                kb.stt("dve", Sb[:, (n + 1) % 3, h, :], S32[:, h, :], dA, u_ps[:, 0, :], ALU.mult, ALU.add)
                kb.stt("dve", S32[:, h, :], S32[:, h, :], dA, u_ps[:, 0, :], ALU.mult, ALU.add)
                yield
                kb.mm(o_ps.all(), qdB[p][:, h, :], Sb[:, (n + 1) % 3, h, :], False, True)
                yield
                dB = eb[p][:, h, 127:128]
                kb.stt("dve", Sb[:, (n + 2) % 3, h, :], S32[:, h, :], dB, u_ps[:, 1, :], ALU.mult, ALU.add)
                kb.stt("dve", S32[:, h, :], S32[:, h, :], dB, u_ps[:, 1, :], ALU.mult, ALU.add)
                kb.act(junk[:, h * 256:(h + 1) * 256], o_ps.all(), AF.Square, accum=stat[:, 8 + h:9 + h])
                yield
                kb.act(stat[:, 4 + h:5 + h], stat[:, 8 + h:9 + h], AF.Sqrt, bias=epsc[:, 0:1], scale=1.0 / 256)
                yield
                kb.recip(stat[:, 12 + h:13 + h], stat[:, 4 + h:5 + h])
                yield
                kb.stt("dve", og[:, h * 256:(h + 1) * 256], o_ps.all(), stat[:, 12 + h:13 + h], sr[p][:, h, :],
                       ALU.mult, ALU.mult)
                yield
            pT = kb.pbank(6, [128, 8, 128], BF16)
            for k in range(8):
                kb.tr(pT[:, k, :], og[:, k * 128:(k + 1) * 128], identb.all())
            yield
            kb.cp("act", ogT.all(), pT.all())
            yield
            pys = [kb.pbank(5, [128, 512]), kb.pbank(7, [128, 512])]
            for hf in range(2):
                for k in range(8):
                    kb.mm(pys[hf].all(), ogT[:, k, :], WOUT[:, k, hf * 512:(hf + 1) * 512], k == 0, k == 7)
            yield
            yield from g_postnorm([q.all() for q in pys], xt[:, p, :], sc, G, dst, t)
            if "mods" in plan:
                mods_job((6,))
                yield

        tasks = []
        idx_a = {}; idx_b = {}
        for t in range(ntl):
            if t == 0:
                idx_a[0] = len(tasks); tasks.append((lambda: chain_a(0), []))
            if t + 1 < ntl:
                deps = [idx_a[t]] + ([idx_b[t - 1]] if t - 1 >= 0 else [])
                idx_a[t + 1] = len(tasks); tasks.append((lambda tt=t + 1: chain_a(tt), deps))
            deps = [idx_a[t]] + ([idx_b[t - 1]] if t >= 1 else [])
            idx_b[t] = len(tasks); tasks.append((lambda tt=t: chain_b(tt), deps))
        run_tasks(tasks, 2)
        if "mods" in plan:
            while mods_job((6,)):
                pass

    def g_rope(cols, scale, R, par):
        c0, c1 = cols
        posi = R["posi"]; posf = R["posf"]; u = R["u"]; w = R["w"]; wi = R["wi"]; wf = R["wf"]; g1 = R["g1"]
        kb.dma("sp", posi.all(), pos_in.cv(pos_in.h[0, c0:c1].partition_broadcast(64), c0, c1))
        yield
        kb.cp("dve", posf.all(), posi.all())
        yield
        kb.tsc("dve", u.all(), posf.all(), invf[0:64, 0:1], 1.0 / (2.0 * math.pi), ALU.mult, ALU.mult)
        yield
        for tab, shift in ((R["sinT"][par], 0.0), (R["cosT"][par], 0.25)):
            kb.tsc("dve", w.all(), u.all(), shift, None, ALU.add)
            yield
            kb.cp("dve", wi.all(), w.all())
            yield
            kb.cp("dve", wf.all(), wi.all())
            yield
            kb.tt("dve", w.all(), w.all(), wf.all(), ALU.subtract)
            yield
            kb.tsc("dve", g1.all(), w.all(), 0.5, None, ALU.is_gt)
            yield
            kb.tt("dve", w.all(), w.all(), g1.all(), ALU.subtract)
            yield
            kb.tsc("dve", g1.all(), w.all(), -0.5, None, ALU.is_lt)
            yield
            kb.tt("dve", w.all(), w.all(), g1.all(), ALU.add)
            yield
            kb.act(tab.all(), w.all(), AF.Sin, scale=2.0 * math.pi)
            yield
            if scale != 1.0:
                kb.tsc("dve", tab.all(), tab.all(), float(scale), None, ALU.mult)
                yield

    def rope_alloc(A):
        return dict(cosT=[A.take([64, 512], F32) for _ in range(8)], sinT=[A.take([64, 512], F32) for _ in range(8)],
                    posi=A.take([64, 512], I32), posf=A.take([64, 512], F32), u=A.take([64, 512], F32),
                    w=A.take([64, 512], F32), wi=A.take([64, 512], I32), wf=A.take([64, 512], F32),
                    g1=A.take([64, 512], F32))

    def phase_kv():
        A = Alloc(kb, LIGHT_BASE)
        WKA = A.take([128, 8, 320], BF16)
        WKR = A.take([128, 8, 64], BF16)
        WKB = A.take([128, 2, 2048], BF16)
        WV = A.take([128, 2, 1024], BF16)
        GKV = A.take([128, 256], F32)
        AT, BT, _ = load_mod_vectors(A, 18432, gvec_fm(kv_g_in, 0), None, 0.0)
        R = rope_alloc(A)
        NS = 4
        xt = A.take([128, NS, D], F32)
        hT = A.take([128, 2, 8, 512], BF16)
        scs = make_scratch(A, NS, with_xo=False)
        cknN = [A.take([128, 256], BF16) for _ in range(NS)]
        ckvT = A.take([128, 2, 2, 512], BF16)
        vsb = A.take([128, NS, D], BF16)
        t1 = A.take([64, 512], F32); t2 = A.take([64, 512], F32)
        krs = A.take([64, 2, 512], BF16)
        ksb = A.take([128, 2, 512], BF16)
        for k in range(0, 8, 4):
            kb.dma("pool", WKA[:, k:k + 4, :],
                   mla_w_kv_a.cv(mla_w_kv_a.h[k * 128:(k + 4) * 128, :].rearrange("(a p) n -> p a n", p=128),
                                 k * 128 * 320, (k + 4) * 128 * 320))
        kb.dma("pool", WKB.all(), mla_w_kv_b.cv(mla_w_kv_b.h[:, :].rearrange("(a p) n -> p a n", p=128)))
        kb.dma("sp", GKV.all(), mla_g_kv.cv(mla_g_kv.h[0].partition_broadcast(128)))
        kb.S.op("act", lambda e: e.mul(out=WKR[:, :, 0:32].ap, in_=WKA[:, :, 288:320].ap, mul=-1.0),
                reads=[WKA[:, :, 288:320]], writes=[WKR[:, :, 0:32]])
        kb.cp("act", WKR[:, :, 32:64], WKA[:, :, 256:288])
        for c in range(2):
            srcv = WKB.cv(WKB.ap[:, c, :].rearrange("p (h e) -> p h e", h=8)[:, :, 128:256], c * 2048, (c + 1) * 2048)
            dstv = WV.cv(WV.ap[:, c, :].rearrange("p (h e) -> p h e", h=8), c * 1024, (c + 1) * 1024)
            kb.cp("dve", dstv, srcv)

        def tile_chain(t):
            bq, j = divmod(t, 4)
            hs = bq % 2
            jc = slice(j * 128, (j + 1) * 128)
            sc = scs[t % NS]; stat = sc["stat"]; junk = sc["junk"]; ckn = cknN[t % NS]
            yield from g_prenorm(kv_src, t, xt[:, t % NS, :], sc, t % 4, hT[:, hs, :, jc], AT, BT)
            ck_ps = kb.pbank(t % 4, [128, 256])
            for k in range(8):
                kb.mm(ck_ps.all(), hT[:, hs, k, jc], WKA[:, k, 0:256], k == 0, k == 7)
            yield
            kb.act(junk[:, 0:256], ck_ps.all(), AF.Square, accum=stat[:, 8:9])
            yield
            rstd(stat[:, 10:11], stat[:, 8:9], stat[:, 9:10], 256)
            yield
            kb.stt("dve", ckn.all(), ck_ps.all(), stat[:, 10:11], GKV.all(), ALU.mult, ALU.mult)
            yield
            pT2 = kb.pbank(t % 4, [128, 2, 128], BF16)
            for c in range(2):
                kb.tr(pT2[:, c, :], ckn[:, c * 128:(c + 1) * 128], identb.all())
            yield
            kb.cp("act", ckvT[:, hs, :, jc], pT2.all())
            yield
            for nh in range(2):
                v_ps = kb.pbank(t % 4, [128, 512])
                for c in range(2):
                    kb.mm(v_ps.all(), ckvT[:, hs, c, jc], WV[:, c, nh * 512:(nh + 1) * 512], c == 0, c == 1)
                yield
                kb.cp("act" if nh == 0 else "dve", vsb[:, t % NS, nh * 512:(nh + 1) * 512], v_ps.all())
                yield
            kb.dma("sp", v_d[t * 128:(t + 1) * 128, :], vsb[:, t % NS, :])
            yield

        def block_chain(bq):
            hs = bq % 2
            cols = (bq * 512, (bq + 1) * 512)
            kp_ps = kb.pbank(4, [64, 512]); kr_ps = kb.pbank(5, [64, 512])
            for k in range(8):
                kb.mm(kp_ps.all(), WKA[:, k, 256:320], hT[:, hs, k, :], k == 0, k == 7)
            for k in range(8):
                kb.mm(kr_ps.all(), WKR[:, k, :], hT[:, hs, k, :], k == 0, k == 7)
            yield
            kb.tt("dve", t1.all(), kp_ps.all(), R["cosT"][bq].all(), ALU.mult)
            kb.tt("dve", t2.all(), kr_ps.all(), R["sinT"][bq].all(), ALU.mult)
            yield
            kb.tt("pool", krs[:, hs, :], t1.all(), t2.all(), ALU.add)
            yield
            kb.dma("sp", krT_d[:, cols[0]:cols[1]], krs[:, hs, :])
            yield
            for h in range(8):
                kn_ps = kb.pbank((6, 7)[h % 2], [128, 512])
                for c in range(2):
                    kb.mm(kn_ps.all(), WKB[:, c, h * 256:h * 256 + 128], ckvT[:, hs, c, :], c == 0, c == 1)
                yield
                kb.cp("act" if h % 2 == 0 else "dve", ksb[:, h % 2, :], kn_ps.all())
                yield
                kb.dma("sp", kT_d[h, :, cols[0]:cols[1]], ksb[:, h % 2, :])
                yield

        tasks = []
        bidx = {}; ridx = {}
        for bq in range(SEQ // 512):
            base = len(tasks)
            if bq == 0:
                rope0 = len(tasks)
                tasks.append((lambda: g_rope((0, 512), 1.0, R, 0), []))
            dq = ([bidx[bq - 2]] if bq >= 2 else []) + ([ridx[bq - 1]] if bq >= 1 else [])
            ridx[bq] = len(tasks)
            tasks.append((lambda bq=bq: multi_chain([tile_chain(bq * 4 + j) for j in range(4)]), dq))
            if bq == 0:
                def rope_rest():
                    for b2 in range(1, SEQ // 512):
                        yield from g_rope((b2 * 512, (b2 + 1) * 512), 1.0, R, b2)
                roperest = len(tasks)
                tasks.append((rope_rest, [rope0]))
            bidx[bq] = len(tasks)
            tasks.append((lambda bq=bq: block_chain(bq), [ridx[bq], rope0] + ([roperest] if bq >= 1 else []) +
                          ([bidx[bq - 1]] if bq >= 1 else [])))
        run_tasks(tasks, int(os.environ.get("LIGHT_W", "2")))

    def phase_mlaq():
        src = route["mla"][0]
        A = Alloc(kb, LIGHT_BASE)
        WDQ = A.take([128, 8, 384], BF16)
        WUQ = A.take([128, 3, 1536], BF16)
        WQR = A.take([128, 3, 8, 64], BF16)
        GQ = A.take([128, 384], F32)
        AT, BT, _ = load_mod_vectors(A, 9216 + 3072, gvec_fm(norm_g, (1, 1, 0)), None, 0.0)
        R = rope_alloc(A)
        NS = 4
        xt = A.take([128, NS, D], F32)
        hT = A.take([128, 2, 8, 512], BF16)
        scs = make_scratch(A, NS, with_xo=False)
        cqnN = [A.take([128, 384], BF16) for _ in range(NS)]
        cqT = A.take([128, 2, 3, 512], BF16)
        qsb = A.take([128, 2, 512], BF16)
        t1 = A.take([64, 512], F32); t2 = A.take([64, 512], F32)
        qrs = A.take([64, 2, 512], BF16)
        qs = 192.0 ** -0.5
        for k in range(0, 8, 4):
            kb.dma("pool", WDQ[:, k:k + 4, :],
                   mla_w_dq.cv(mla_w_dq.h[k * 128:(k + 4) * 128, :].rearrange("(a p) n -> p a n", p=128),
                               k * 128 * 384, (k + 4) * 128 * 384))
        kb.dma("pool", WUQ.all(), mla_w_uq.cv(mla_w_uq.h[:, :].rearrange("(a p) n -> p a n", p=128)))
        kb.dma("sp", GQ.all(), mla_g_q.cv(mla_g_q.h[0].partition_broadcast(128)))
        for c in range(3):
            w4 = WUQ.ap[:, c, :].rearrange("p (h e) -> p h e", h=8)
            src_x2 = WUQ.cv(w4[:, :, 160:192], c * 1536, (c + 1) * 1536)
            src_x1 = WUQ.cv(w4[:, :, 128:160], c * 1536, (c + 1) * 1536)
            kb.S.op("act", lambda e, c=c, src_x2=src_x2: e.mul(out=WQR[:, c, :, 0:32].ap, in_=src_x2.ap, mul=-1.0),
                    reads=[src_x2], writes=[WQR[:, c, :, 0:32]])
            kb.cp("act", WQR[:, c, :, 32:64], src_x1)

        def tile_chain(t):
            bq, j = divmod(t, 4)
            hs = bq % 2
            jc = slice(j * 128, (j + 1) * 128)
            sc = scs[t % NS]; stat = sc["stat"]; junk = sc["junk"]; cqn = cqnN[t % NS]
            yield from g_prenorm(src, t, xt[:, t % NS, :], sc, t % 4, hT[:, hs, :, jc], AT, BT)
            cq_ps = kb.pbank(t % 4, [128, 384])
            for k in range(8):
                kb.mm(cq_ps.all(), hT[:, hs, k, jc], WDQ[:, k, :], k == 0, k == 7)
            yield
            kb.act(junk[:, 0:384], cq_ps.all(), AF.Square, accum=stat[:, 8:9])
            yield
            rstd(stat[:, 10:11], stat[:, 8:9], stat[:, 9:10], 384)
            yield
            kb.stt("dve", cqn.all(), cq_ps.all(), stat[:, 10:11], GQ.all(), ALU.mult, ALU.mult)
            yield
            pT3 = kb.pbank(t % 4, [128, 3, 128], BF16)
            for c in range(3):
                kb.tr(pT3[:, c, :], cqn[:, c * 128:(c + 1) * 128], identb.all())
            yield
            kb.cp("act", cqT[:, hs, :, jc], pT3.all())
            yield

        def block_chain(bq):
            hs = bq % 2
            cols = (bq * 512, (bq + 1) * 512)
            for h in range(8):
                qn_ps = kb.pbank(4 + h % 2, [128, 512])
                qp_ps = kb.pbank(6, [64, 512])
                qr_ps = kb.pbank(7, [64, 512])
                for c in range(3):
                    kb.mm(qn_ps.all(), WUQ[:, c, h * 192:h * 192 + 128], cqT[:, hs, c, :], c == 0, c == 2)
                for c in range(3):
                    kb.mm(qp_ps.all(), WUQ[:, c, h * 192 + 128:h * 192 + 192], cqT[:, hs, c, :], c == 0, c == 2)
                for c in range(3):
                    kb.mm(qr_ps.all(), WQR[:, c, h, :], cqT[:, hs, c, :], c == 0, c == 2)
                yield
                kb.act(qsb[:, h % 2, :], qn_ps.all(), AF.Identity, scale=qs)
                kb.tt("dve", t1.all(), qp_ps.all(), R["cosT"][bq].all(), ALU.mult)
                kb.tt("dve", t2.all(), qr_ps.all(), R["sinT"][bq].all(), ALU.mult)
                yield
                kb.dma("sp", qT_d[h, 0:128, cols[0]:cols[1]], qsb[:, h % 2, :])
                kb.tt("pool", qrs[:, h % 2, :], t1.all(), t2.all(), ALU.add)
                yield
                kb.dma("sp", qT_d[h, 128:192, cols[0]:cols[1]], qrs[:, h % 2, :])
                yield

        tasks = []
        bidx = {}; ridx = {}
        for bq in range(SEQ // 512):
            base = len(tasks)
            if bq == 0:
                rope0 = len(tasks)
                tasks.append((lambda: g_rope((0, 512), qs, R, 0), []))
            dq = ([bidx[bq - 2]] if bq >= 2 else []) + ([ridx[bq - 1]] if bq >= 1 else [])
            ridx[bq] = len(tasks)
            tasks.append((lambda bq=bq: multi_chain([tile_chain(bq * 4 + j) for j in range(4)]), dq))
            if bq == 0:
                def rope_rest():
                    for b2 in range(1, SEQ // 512):
                        yield from g_rope((b2 * 512, (b2 + 1) * 512), qs, R, b2)
                roperest = len(tasks)
                tasks.append((rope_rest, [rope0]))
            bidx[bq] = len(tasks)
            tasks.append((lambda bq=bq: block_chain(bq), [ridx[bq], rope0] + ([roperest] if bq >= 1 else []) +
                          ([bidx[bq - 1]] if bq >= 1 else [])))
        run_tasks(tasks, int(os.environ.get("LIGHT_W", "2")))

    def phase_attn():
        A = Alloc(kb, ATTN_BASE)
        KR = A.take([64, SEQ], BF16)
        KN = A.take([128, 2, SEQ], BF16)
        V = A.take([128, 2, NT, 128], BF16)
        QN = A.take([128, 2, 512], BF16)
        QR = A.take([64, 2, 512], BF16)
        NPT = 6
        PT = A.take([128, NPT, 512], BF16)
        RL = A.take([128, 2, 512], F32)
        OT = A.take([128, 2, 512], BF16)
        ACC = A.take([128, 2, 512], F32)
        kb.dma("sp", KR.all(), krT_d.all())
        SK = int(os.environ.get("ATTN_SKEW", "3"))
        blocks = []
        it = 0
        for h in range(8):
            for Q in range(SEQ // 512):
                nkt = 4 * Q + 4
                for kj in range(nkt):
                    blocks.append((h, Q, kj, nkt, it % 2))
                it += 1

        def s_block(i):
            h, Q, kj, nkt, par = blocks[i]
            hs = h % 2
            if kj == 0:
                cols = (Q * 512, (Q + 1) * 512)
                if Q == 0:
                    kb.dma("sp", KN[:, hs, :], kT_d[h, :, :])
                    for tq in range(0, NT, 8):
                        base = tq * 128 * D
                        kb.dma("sp", V[:, hs, tq:tq + 8, :],
                               v_d.cv(v_d.h[tq * 128:(tq + 8) * 128, h * 128:(h + 1) * 128].rearrange("(t p) v -> p t v", p=128),
                                      base, base + 8 * 128 * D))
                kb.dma("sp", QN[:, par, :], qT_d[h, 0:128, cols[0]:cols[1]])
                kb.dma("sp", QR[:, par, :], qT_d[h, 128:192, cols[0]:cols[1]])
                kb.memset("pool", ACC[:, par, :], 0.0)
            c0 = max(0, kj - 4 * Q) * 128
            ps = kb.pbank(i % 4, [128, 512])
            kb.mm(ps[:, c0:512], KN[:, hs, kj * 128:(kj + 1) * 128], QN[:, par, c0:512], True, False)
            if kj >= 4 * Q:
                kb.mm(ps[:, c0:c0 + 128], identb.all(), negb.all(), False, False)
            kb.mm(ps[:, c0:512], KR[:, kj * 128:(kj + 1) * 128], QR[:, par, c0:512], False, True)
            kb.act(PT[:, i % NPT, c0:512], ps[:, c0:512], AF.Exp)

        def pv_block(i):
            h, Q, kj, nkt, par = blocks[i]
            hs = h % 2
            c0 = max(0, kj - 4 * Q) * 128
            po = kb.pbank(4 + par, [128, 512])
            pl = kb.pbank(6 + par, [128, 512])
            kb.mm(po[:, c0:512], V[:, hs, kj, :], PT[:, i % NPT, c0:512], kj == 0, kj == nkt - 1)
            if kj % 2 == 0:
                kb.mm(pl[:, c0:512], oneb.all(), PT[:, i % NPT, c0:512], kj == 0, False)
            else:
                kb.tt("dve", ACC[:, par, c0:512], ACC[:, par, c0:512], PT[:, i % NPT, c0:512], ALU.add)
            if kj == nkt - 1:
                cols = (Q * 512, (Q + 1) * 512)
                kb.mm(pl.all(), onef.all(), ACC[:, par, :], False, True)
                kb.recip(RL[:, par, :], pl.all())
                kb.tt("dve", OT[:, par, :], po.all(), RL[:, par, :], ALU.mult)
                kb.dma("sp", oT_d[h, :, cols[0]:cols[1]], OT[:, par, :])

        nb = len(blocks)
        for i in range(min(SK, nb)):
            s_block(i)
        for i in range(nb):
            if i + SK < nb:
                s_block(i + SK)
            pv_block(i)

    def phase_mlao():
        src, dst = route["mla"]
        A = Alloc(kb, MLAO_BASE)
        WO = A.take([128, 8, D], BF16)
        _, _, G = load_mod_vectors(A, 9216 + 3072, None, gvec_rep(norm_g, (1, 1, 1)), 1.0)
        NS = int(os.environ.get("MLAO_NS", "4"))
        OTt = A.take([128, NS, 8, 128], BF16)
        xr = A.take([128, NS, D], F32)
        scs = make_scratch(A, NS)
        for k in range(0, 8, 2):
            kb.dma("pool", WO[:, k:k + 2, :],
                   mla_w_out.cv(mla_w_out.h[k * 128:(k + 2) * 128, :].rearrange("(a p) n -> p a n", p=128),
                                k * 128 * D, (k + 2) * 128 * D))

        def tile_chain(t):
            kb.dma("sp", OTt[:, t % NS, :, :],
                   oT_d.cv(oT_d.h[:, :, t * 128:(t + 1) * 128].rearrange("h v t -> v h t")))
            kb.dma("sp", xr[:, t % NS, :], src[t * 128:(t + 1) * 128, :])
            yield
            pys = [kb.pbank((2 * t) % 8, [128, 512]), kb.pbank((2 * t + 1) % 8, [128, 512])]
            for hf in range(2):
                for h in range(8):
                    kb.mm(pys[hf].all(), OTt[:, t % NS, h, :], WO[:, h, hf * 512:(hf + 1) * 512], h == 0, h == 7)
            yield
            yield from g_postnorm([p.all() for p in pys], xr[:, t % NS, :], scs[t % NS], G, dst, t)

        grp = int(os.environ.get("MLAO_G", "2"))
        run_tasks([(lambda t0=t0: multi_chain([tile_chain(t) for t in range(t0, min(NT, t0 + grp))]), []) for t0 in range(0, NT, grp)],
                  int(os.environ.get("MLAO_W", "2")))

    pre_ffn1a = "ffn1a" in plan and "kv" in plan and plan.index("kv") < plan.index("ffn1a")
    pre_ffn1b = "ffn1b" in plan and "mlaq" in plan
    pre_ffn0a = "ffn0a" in plan and "mods" in plan
    for p in plan:
        if p == "mods":
            if pre_ffn0a:
                ffn_load(0, 0)
            mods_setup(140032)
            for _ in range(12 if "gla" in plan else 40):
                mods_job()
        elif p == "ffn0a":
            phase_ffn("ffn0a", 0, 0, 0, 0, preloaded=pre_ffn0a)
        elif p == "ffn0b":
            phase_ffn("ffn0b", 0, 1, 6 * 1024, 2)
        elif p == "ffn1a":
            if pre_ffn1a:
                ffn_load(1, 0, lo=LIGHT_BASE)
            phase_ffn("ffn1a", 1, 0, 9216, 0, preloaded=pre_ffn1a)
        elif p == "ffn1b":
            phase_ffn("ffn1b", 1, 1, 9216 + 6 * 1024, 2, preloaded=pre_ffn1b)
        elif p == "gla":
            if os.environ.get("GLA_V1"):
                phase_gla()
            else:
                phase_gla2()
        elif p == "kv":
            if pre_ffn1a:
                ffn_load(1, 0, hi=LIGHT_BASE)
            phase_kv()
        elif p == "mlaq":
            if pre_ffn1b:
                ffn_load(1, 1, hi=LIGHT_BASE)
            phase_mlaq()
        elif p == "attn":
            if pre_ffn1b:
                ffn_load(1, 1, lo=LIGHT_BASE, hi=MLAO_BASE)
            phase_attn()
        elif p == "mlao":
            phase_mlao()
            if pre_ffn1b:
                ffn_load(1, 1, lo=MLAO_BASE)
        else:
            raise ValueError(p)
    while "mods" in plan and mods_state.get("jobs") and mods_state["next"] < len(mods_state["jobs"]):
        raise RuntimeError("mods jobs left unissued")
    if debug_out == "mods":
        pass
    S.finish()
    return kb


def host_inputs(inputs, b):
    f = lambda a: np.ascontiguousarray(np.asarray(a), dtype=np.float32)
    m = {
        "x": f(inputs["x"][b]),
        "c_fm": np.ascontiguousarray(f(inputs["c"][b]).reshape(8, 128).T),
        "pos": np.ascontiguousarray(np.asarray(inputs["positions"][b], dtype=np.int32).reshape(1, SEQ)),
        "consts": make_consts(),
        "cond_w": f(inputs["cond_w"]), "cond_b": f(inputs["cond_b"]), "norm_g": f(inputs["norm_g"]),
        "ffn_w_gu": f(inputs["ffn_w_gu"]), "ffn_w_down": f(inputs["ffn_w_down"]),
        "gla_w_in": f(inputs["gla_w_in"][0]), "gla_w_gate_up": f(inputs["gla_w_gate_up"][0]),
        "gla_b_gate": f(inputs["gla_b_gate"]).reshape(1, 512), "gla_g_out": f(inputs["gla_g_out"]).reshape(1, 256),
        "gla_w_out": f(inputs["gla_w_out"][0]),
        "kv_g_in": f(inputs["kv_g_in"]).reshape(1, D), "kv_cond_w": f(inputs["kv_cond_w"]),
        "kv_cond_b": f(inputs["kv_cond_b"]).reshape(1, 2 * D),
        "mla_w_kv_a": f(inputs["mla_w_kv_a"]), "mla_g_kv": f(inputs["mla_g_kv"]).reshape(1, 256),
        "mla_w_kv_b": f(inputs["mla_w_kv_b"]),
        "mla_w_dq": f(inputs["mla_w_dq"][0]), "mla_g_q": f(inputs["mla_g_q"]).reshape(1, 384),
        "mla_w_uq": f(inputs["mla_w_uq"][0]), "mla_w_out": f(inputs["mla_w_out"][0]),

    }
    return m


FULL_PLAN = ["mods", "ffn0a", "gla", "ffn0b", "kv", "ffn1a", "mlaq", "attn", "mlao", "ffn1b"]


def kernel(**inputs):
    kb = build(FULL_PLAN)
    with contextlib.ExitStack() as st:
        kb.S.emit(st)
        in_maps = [{k: v for k, v in host_inputs(inputs, b).items() if k in kb.din} for b in range(8)]
        res = run_bass_kernel_spmd(kb.nc, in_maps, core_ids=list(range(8)))
    return np.stack([np.asarray(r["y"], dtype=np.float32) for r in res.results], axis=0)
```

```python
import contextlib
import math
import os
import numpy as np
import concourse.bass as bass
import concourse.mybir as mybir
from concourse.bass_utils import run_bass_kernel_spmd

F32 = mybir.dt.float32
BF16 = mybir.dt.bfloat16
I32 = mybir.dt.int32
AF = mybir.ActivationFunctionType
ALU = mybir.AluOpType
AX = mybir.AxisListType
ESZ = {F32: 4, BF16: 2, I32: 4}

ENGS = ("pe", "act", "dve", "pool", "sp")
EIDX = {e: i for i, e in enumerate(ENGS)}
EPOCH = 16000
RING = 14

D = 1024
SEQ = 4096
NT = SEQ // 128
DFF = 2816
EPS = 1e-6
ARENA_BYTES = 212480
STRICT = os.environ.get("MK_STRICT", "0") == "1"


class Region:
    def __init__(self, size, bucket=2048):
        self.size = size
        self.B = bucket
        self.b = {}


class View:
    __slots__ = ("ap", "r", "lo", "hi")

    def __init__(self, ap, r, lo, hi):
        self.ap = ap; self.r = r; self.lo = lo; self.hi = hi


class Sub:
    def __init__(self, ap, region, off, esz, shape, track_from):
        self.ap = ap; self.r = region; self.off = off; self.esz = esz
        self.shape = tuple(shape); self.skip = track_from
        st = []; acc = 1
        for d in reversed(self.shape[track_from:]):
            st.append(acc); acc *= d
        self.strides = tuple(reversed(st))
        self.nbytes = acc * esz
        self.whole = False

    def __getitem__(self, idx):
        if self.whole:
            if not isinstance(idx, tuple):
                idx = (idx,)
            return View(self.ap[idx], self.r, 0, self.r.size)
        if not isinstance(idx, tuple):
            idx = (idx,)
        idx = tuple(idx) + (slice(None),) * (len(self.shape) - len(idx))
        lo = 0; hi = 0
        for d in range(self.skip, len(self.shape)):
            i = idx[d]; s = self.strides[d - self.skip]
            if isinstance(i, slice):
                a = 0 if i.start is None else i.start
                b = self.shape[d] if i.stop is None else i.stop
                assert i.step in (None, 1) and 0 <= a < b <= self.shape[d], (idx, self.shape)
            else:
                a = i; b = i + 1
                assert 0 <= a < self.shape[d], (idx, self.shape)
            lo += a * s; hi += (b - 1) * s
        return View(self.ap[idx], self.r, self.off + lo * self.esz, self.off + (hi + 1) * self.esz)

    def all(self):
        if self.whole:
            return View(self.ap, self.r, 0, self.r.size)
        return View(self.ap, self.r, self.off, self.off + self.nbytes)

    def cv(self, ap, lo_e=0, hi_e=None):
        hi_e = self.nbytes // self.esz if hi_e is None else hi_e
        return View(ap, self.r, self.off + lo_e * self.esz, self.off + hi_e * self.esz)


class Op:
    __slots__ = ("eng", "fn", "dma", "pos", "waits", "signal", "clock", "slot", "val", "sidx")


class Sched:
    def __init__(self, nc):
        self.nc = nc
        self.ops = []
        self.pos = {e: 0 for e in ENGS}
        self.known = {e: [0] * len(ENGS) for e in ENGS}
        self.known_dma = {e: {} for e in ENGS}
        self.ring_next = {e: 0 for e in ENGS}
        self.ring_last = {}
        self.last_op = {e: None for e in ENGS}
        self.n_waits = 0

    def _need(self, X, Y, raw):
        if Y is X:
            return
        E = X.eng
        if Y.dma:
            key = (Y.eng, Y.slot)
            if self.known_dma[E].get(key, 0) >= Y.val:
                return
            self.known_dma[E][key] = Y.val
            X.waits.append(Y)
            self._merge(E, Y.clock)
            return
        if Y.eng == E and not X.dma:
            if E == "pe" or (not raw and not STRICT):
                return
        pi = EIDX[Y.eng]
        if self.known[E][pi] >= Y.pos:
            return
        Y.signal = True
        X.waits.append(Y)
        self._merge(E, Y.clock)

    def _merge(self, E, clock):
        k = self.known[E]
        for i, v in enumerate(clock):
            if v > k[i]:
                k[i] = v

    def op(self, eng, fn, reads=(), writes=(), dma=False):
        X = Op()
        X.eng = eng; X.fn = fn; X.dma = dma; X.waits = []; X.signal = False
        X.slot = None; X.val = 0; X.sidx = 0
        self.pos[eng] += 1
        X.pos = self.pos[eng]
        deps = {}
        for v in reads:
            rb = v.r.b; B = v.r.B
            for bi in range(v.lo // B, (v.hi - 1) // B + 1):
                for r in rb.get(bi, ()):
                    if r[3] and r[0] < v.hi and v.lo < r[1]:
                        deps[id(r[2])] = (r[2], True)
        for v in writes:
            rb = v.r.b; B = v.r.B
            for bi in range(v.lo // B, (v.hi - 1) // B + 1):
                for r in rb.get(bi, ()):
                    if r[0] < v.hi and v.lo < r[1]:
                        k = id(r[2])
                        if k not in deps:
                            deps[k] = (r[2], r[3] and False)
        for Y, raw in sorted(deps.values(), key=lambda t: -t[0].pos):
            self._need(X, Y, raw)
        if dma:
            s = self.ring_next[eng]; self.ring_next[eng] = (s + 1) % RING
            prev = self.ring_last.get((eng, s))
            if prev is not None:
                self._need(X, prev, False)
                X.val = prev.val + 16
            else:
                X.val = 16
            X.slot = s
            self.ring_last[(eng, s)] = X
        ck = list(self.known[eng])
        if not dma:
            ck[EIDX[eng]] = X.pos
        X.clock = ck
        for v in reads:
            rb = v.r.b; B = v.r.B
            rec = (v.lo, v.hi, X, False)
            for bi in range(v.lo // B, (v.hi - 1) // B + 1):
                lst = rb.get(bi)
                if lst is None:
                    rb[bi] = [rec]; continue
                if not dma:
                    lst[:] = [r for r in lst if not ((not r[3]) and (not r[2].dma) and r[2].eng == eng
                                                     and v.lo <= r[0] and r[1] <= v.hi)]
                lst.append(rec)
        for v in writes:
            rb = v.r.b; B = v.r.B
            rec = (v.lo, v.hi, X, True)
            for bi in range(v.lo // B, (v.hi - 1) // B + 1):
                lst = rb.get(bi)
                if lst is None:
                    rb[bi] = [rec]; continue
                lst[:] = [r for r in lst if not (v.lo <= r[0] and r[1] <= v.hi)]
                lst.append(rec)
        self.ops.append(X)
        if not dma:
            self.last_op[eng] = X
        return X

    def finish(self):
        X = Op()
        X.eng = "sp"; X.fn = None; X.dma = False; X.waits = []; X.signal = False
        X.slot = None; X.val = 0; X.sidx = 0
        self.pos["sp"] += 1; X.pos = self.pos["sp"]
        for (e, s), Dm in self.ring_last.items():
            self._need(X, Dm, False)
        for e in ENGS:
            if e != "sp" and self.last_op[e] is not None:
                Y = self.last_op[e]
                if self.known["sp"][EIDX[e]] < Y.pos:
                    Y.signal = True; X.waits.append(Y)
        X.clock = list(self.known["sp"])
        self.ops.append(X)

    def emit(self, stack):
        nc = self.nc
        engobj = {"pe": nc.tensor, "act": nc.scalar, "dve": nc.vector, "pool": nc.gpsimd, "sp": nc.sync}
        nsig = {e: 0 for e in ENGS}
        for X in self.ops:
            if X.signal:
                nsig[X.eng] += 1
        sems = {}
        for e in ENGS:
            n = (nsig[e] + EPOCH - 1) // EPOCH
            sems[e] = [stack.enter_context(nc.semaphore(f"s_{e}_{i}")) for i in range(n)]
        rings = {}
        for (e, s) in self.ring_last:
            rings[(e, s)] = stack.enter_context(nc.semaphore(f"r_{e}_{s}"))
        cnt = {e: 0 for e in ENGS}
        nw = 0
        for X in self.ops:
            eo = engobj[X.eng]
            for Y in X.waits:
                if Y.dma:
                    eo.wait_ge(rings[(Y.eng, Y.slot)], Y.val)
                else:
                    n = Y.sidx
                    assert n > 0
                    eo.wait_ge(sems[Y.eng][(n - 1) // EPOCH], (n - 1) % EPOCH + 1)
                nw += 1
            if X.fn is None:
                continue
            ins = X.fn(eo)
            if X.dma:
                ins.then_inc(rings[(X.eng, X.slot)], 16)
            if X.signal:
                cnt[X.eng] += 1
                X.sidx = cnt[X.eng]
                ins.then_inc(sems[X.eng][(X.sidx - 1) // EPOCH], 1)
        self.n_waits = nw
        return nsig


class Alloc:
    def __init__(self, K, base=0):
        self.K = K; self.off = base

    def take(self, shape, dt):
        n = 1
        for d in shape[1:]:
            n *= d
        nb = n * ESZ[dt]
        off = (self.off + 31) // 32 * 32
        assert off + nb <= self.K.arena_limit, ("arena overflow", off + nb, self.K.arena_limit)
        self.off = off + nb
        return self.K.carve(off, shape, dt)


class K:
    def __init__(self):
        self.nc = bass.Bass("TRN2", target_bir_lowering=False)
        self.S = Sched(self.nc)
        self.arena_h = self.nc.alloc_sbuf_tensor("arena", [128, ARENA_BYTES // 2], BF16)
        self.arena_r = Region(ARENA_BYTES)
        self.arena_limit = ARENA_BYTES
        self.banks = []
        for i in range(8):
            h = self.nc.alloc_psum_tensor(f"bank{i}", [128, 512], F32)
            self.banks.append((h, Region(2048, 256)))
        self.dr = {}

    def carve(self, off, shape, dt, parts=128):
        esz = ESZ[dt]
        n = 1
        for d in shape[1:]:
            n *= d
        assert off % 2 == 0
        ap = self.arena_h[0:shape[0], off // 2: off // 2 + n * esz // 2]
        if dt != BF16:
            ap = ap.bitcast(dt)
        if len(shape) == 3:
            ap = ap.rearrange("p (a b) -> p a b", a=shape[1])
        elif len(shape) == 4:
            ap = ap.rearrange("p (a b c) -> p a b c", a=shape[1], b=shape[2])
        return Sub(ap, self.arena_r, off, esz, shape, 1)

    def pbank(self, i, shape, dt=F32, coff=0):
        h, r = self.banks[i]
        esz = ESZ[dt]
        n = 1
        for d in shape[1:]:
            n *= d
        ap = h[0:shape[0], coff // 4: coff // 4 + n * esz // 4]
        if dt != F32:
            ap = ap.bitcast(dt)
        if len(shape) == 3:
            ap = ap.rearrange("p (a b) -> p a b", a=shape[1])
        sb = Sub(ap, r, coff, esz, shape, 1)
        sb.whole = True
        return sb

    def dram(self, name, shape, dt, kind="Internal"):
        h = self.nc.dram_tensor(name, list(shape), dt, kind=kind)
        n = 1
        for d in shape:
            n *= d
        r = Region(n * ESZ[dt], max(4096, n * ESZ[dt] // 512))
        s = Sub(h[tuple(slice(None) for _ in shape)], r, 0, ESZ[dt], shape, 0)
        s.h = h
        self.dr[name] = s
        return s

    def mm(self, out, lhsT, rhs, start, stop):
        self.S.op("pe", lambda e: e.matmul(out.ap, lhsT=lhsT.ap, rhs=rhs.ap, start=start, stop=stop),
                  reads=[lhsT, rhs], writes=[out])

    def tr(self, out, in_, ident):
        self.S.op("pe", lambda e: e.transpose(out=out.ap, in_=in_.ap, identity=ident.ap),
                  reads=[in_, ident], writes=[out])

    def act(self, out, in_, func, bias=None, scale=None, accum=None):
        kw = {}
        rd = [in_]; wr = [out]
        if bias is not None:
            if isinstance(bias, View):
                kw["bias"] = bias.ap; rd.append(bias)
            else:
                kw["bias"] = bias
        if scale is not None:
            if isinstance(scale, View):
                kw["scale"] = scale.ap; rd.append(scale)
            else:
                kw["scale"] = scale
        if accum is not None:
            kw["accum_out"] = accum.ap; wr.append(accum)
        self.S.op("act", lambda e: e.activation(out=out.ap, in_=in_.ap, func=func, **kw), reads=rd, writes=wr)

    def tsc(self, eng, out, in0, s1, s2, op0, op1=None):
        rd = [in0]
        a1 = s1.ap if isinstance(s1, View) else s1
        a2 = s2.ap if isinstance(s2, View) else s2
        if isinstance(s1, View):
            rd.append(s1)
        if isinstance(s2, View):
            rd.append(s2)
        if op1 is None:
            self.S.op(eng, lambda e: e.tensor_scalar(out=out.ap, in0=in0.ap, scalar1=a1, scalar2=None, op0=op0),
                      reads=rd, writes=[out])
        else:
            self.S.op(eng, lambda e: e.tensor_scalar(out=out.ap, in0=in0.ap, scalar1=a1, scalar2=a2, op0=op0, op1=op1),
                      reads=rd, writes=[out])

    def stt(self, eng, out, in0, sc, in1, op0, op1):
        rd = [in0, in1]
        a = sc.ap if isinstance(sc, View) else sc
        if isinstance(sc, View):
            rd.append(sc)
        self.S.op(eng, lambda e: e.scalar_tensor_tensor(out=out.ap, in0=in0.ap, scalar=a, in1=in1.ap, op0=op0, op1=op1),
                  reads=rd, writes=[out])

    def tt(self, eng, out, in0, in1, op):
        self.S.op(eng, lambda e: e.tensor_tensor(out=out.ap, in0=in0.ap, in1=in1.ap, op=op),
                  reads=[in0, in1], writes=[out])

    def cp(self, eng, out, in_):
        if eng == "act":
            self.S.op("act", lambda e: e.copy(out=out.ap, in_=in_.ap), reads=[in_], writes=[out])
        else:
            self.S.op(eng, lambda e: e.tensor_copy(out=out.ap, in_=in_.ap), reads=[in_], writes=[out])

    def memset(self, eng, out, val):
        self.S.op(eng, lambda e: e.memset(out.ap, val), reads=[], writes=[out])

    def recip(self, out, in_):
        self.S.op("dve", lambda e: e.reciprocal(out=out.ap, in_=in_.ap), reads=[in_], writes=[out])

    def dma(self, eng, out, in_, slow=False):
        if slow:
            self.S.op(eng, lambda e: e.dma_start(out=out.ap, in_=in_.ap, allow_slow_non_contiguous=True),
                      reads=[in_], writes=[out], dma=True)
        else:
            self.S.op(eng, lambda e: e.dma_start(out=out.ap, in_=in_.ap), reads=[in_], writes=[out], dma=True)


def bcast_last(v, sub, n):
    ap = v.ap.unsqueeze(2).broadcast_to([v.ap.shape[0], v.ap.shape[1], n])
    return View(ap, v.r, v.lo, v.hi)


C_ID, C_M2, C_M2S, C_U2S, C_NEG, C_ONE, C_INVF = 0, 128, 256, 384, 512, 640, 768
NCONST = 776


def make_consts():
    c = np.zeros((128, NCONST), np.float32)
    c[:, C_ID:C_ID + 128] = np.eye(128, dtype=np.float32)
    j = np.arange(128)[:, None]; i = np.arange(128)[None, :]
    same = (j // 64) == (i // 64)
    c[:, C_M2:C_M2 + 128] = (same & (j <= i)).astype(np.float32)
    c[:, C_M2S:C_M2S + 128] = -(same & (j <= i)).astype(np.float32) / 16.0
    c[:, C_U2S:C_U2S + 128] = -(same & (j > i)).astype(np.float32) / 16.0
    c[:, C_NEG:C_NEG + 128] = np.where(j <= i, 0.0, -30000.0).astype(np.float32)
    c[:, C_ONE:C_ONE + 128] = 1.0
    invf = (np.float32(10000.0) ** (-np.arange(0, 64, 2, dtype=np.float32) / np.float32(64))).astype(np.float32)
    c[0:32, C_INVF] = invf; c[32:64, C_INVF] = invf
    return c


LIGHT_BASE = 49152
ATTN_BASE = 143744
MLAO_BASE = 108544
CONST_BYTES = 3200


def build(plan, debug_out=None):
    kb = K()
    S = kb.S
    nc = kb.nc
    kb.arena_limit = ARENA_BYTES - CONST_BYTES
    din = {}
    def inp(name, shape, dt=F32):
        din[name] = kb.dram(name, shape, dt, kind="ExternalInput")
        return din[name]
    x_in = inp("x", [SEQ, D])
    c_in = inp("c_fm", [128, 8])
    pos_in = inp("pos", [1, SEQ], I32)
    consts_in = inp("consts", [128, NCONST])
    cond_w = inp("cond_w", [2, D, 9 * D]); cond_b = inp("cond_b", [2, 9 * D])
    norm_g = inp("norm_g", [2, 3, 2, D])
    ffn_w_gu = inp("ffn_w_gu", [2, 2, D, 2 * DFF]); ffn_w_down = inp("ffn_w_down", [2, 2, DFF, D])
    gla_w_in = inp("gla_w_in", [D, 3088]); gla_w_gate_up = inp("gla_w_gate_up", [16, 512])
    gla_b_gate = inp("gla_b_gate", [1, 512]); gla_g_out = inp("gla_g_out", [1, 256])
    gla_w_out = inp("gla_w_out", [D, D])
    kv_g_in = inp("kv_g_in", [1, D]); kv_cond_w = inp("kv_cond_w", [D, 2 * D]); kv_cond_b = inp("kv_cond_b", [1, 2 * D])
    mla_w_kv_a = inp("mla_w_kv_a", [D, 320]); mla_g_kv = inp("mla_g_kv", [1, 256])
    mla_w_kv_b = inp("mla_w_kv_b", [256, 2048])
    mla_w_dq = inp("mla_w_dq", [D, 384]); mla_g_q = inp("mla_g_q", [1, 384])
    mla_w_uq = inp("mla_w_uq", [384, 1536]); mla_w_out = inp("mla_w_out", [D, D])
    mods_in = inp("mods_dbg", [1, 20480]) if "mods" not in plan else None
    xkv_in = inp("x_kv", [SEQ, D]) if ("kv" in plan and "ffn0b" not in plan) else None
    kb.din = din
    y_out = kb.dram("y", [SEQ, D], F32, kind="ExternalOutput")
    mods_d = kb.dram("mods_d", [1, 20480], F32)
    xs = [kb.dram(f"xs{i}", [SEQ, D], F32) for i in range(5)]
    qT_d = kb.dram("qT_d", [8, 192, SEQ], BF16)
    kT_d = kb.dram("kT_d", [8, 128, SEQ], BF16)
    krT_d = kb.dram("krT_d", [64, SEQ], BF16)
    v_d = kb.dram("v_d", [SEQ, D], BF16)
    oT_d = kb.dram("oT_d", [8, 128, SEQ], BF16)
    dbg = {}

    cbase = ARENA_BYTES - CONST_BYTES
    identf = kb.carve(cbase, [128, 128], F32)
    identb = kb.carve(cbase + 512, [128, 128], BF16)
    m2b = kb.carve(cbase + 768, [128, 128], BF16)
    negb = kb.carve(cbase + 1024, [128, 128], BF16)
    oneb = kb.carve(cbase + 1280, [128, 128], BF16)
    m2sf = kb.carve(cbase + 1536, [128, 128], F32)
    u2sf = kb.carve(cbase + 2048, [128, 128], F32)
    invf = kb.carve(cbase + 2560, [128, 8], F32)
    onef = kb.carve(cbase + 2592, [128, 128], F32)
    epsc = kb.carve(cbase + 3104, [128, 8], F32)
    mab = kb.carve(cbase + 3136, [128, 2], F32)
    kb.memset("dve", epsc.all(), EPS)
    cstage = kb.carve(0, [128, 640], F32)
    kb.dma("sp", identf.all(), consts_in[:, C_ID:C_ID + 128])
    kb.dma("sp", m2sf.all(), consts_in[:, C_M2S:C_M2S + 128])
    kb.dma("sp", u2sf.all(), consts_in[:, C_U2S:C_U2S + 128])
    kb.dma("sp", invf.all(), consts_in[:, C_INVF:C_INVF + 8])
    kb.dma("sp", onef.all(), consts_in[:, C_ONE:C_ONE + 128])
    kb.dma("sp", cstage[:, 0:128], consts_in[:, C_ID:C_ID + 128])
    kb.dma("sp", cstage[:, 128:256], consts_in[:, C_M2:C_M2 + 128])
    kb.dma("sp", cstage[:, 256:384], consts_in[:, C_NEG:C_NEG + 128])
    kb.dma("sp", cstage[:, 384:512], consts_in[:, C_ONE:C_ONE + 128])
    kb.cp("dve", identb.all(), cstage[:, 0:128])
    kb.cp("dve", m2b.all(), cstage[:, 128:256])
    kb.cp("dve", negb.all(), cstage[:, 256:384])
    kb.cp("dve", oneb.all(), cstage[:, 384:512])
    kb.cp("dve", mab[:, 0:1], cstage[:, 128 + 63:128 + 64])
    kb.cp("dve", mab[:, 1:2], cstage[:, 128 + 127:128 + 128])

    plan_s = ["mla" if p == "mlaq" else p for p in plan]
    stream_phases = [p for p in plan_s if p in ("ffn0a", "gla", "ffn0b", "ffn1a", "mla", "ffn1b")]
    route = {"ffn0a": (x_in, xs[0]), "gla": (xs[0], xs[1]), "ffn0b": (xs[1], xs[2]), "ffn1a": (xs[2], xs[3]),
             "mla": (xs[3], xs[4]), "ffn1b": (xs[4], y_out)}
    if stream_phases:
        route[stream_phases[0]] = (x_in, route[stream_phases[0]][1])
        route[stream_phases[-1]] = (route[stream_phases[-1]][0], y_out)
    kv_src = xs[2]
    if "kv" in plan and "ffn0b" not in plan:
        kv_src = xkv_in
    mods_src = mods_d if "mods" in plan else mods_in

    mods_state = {}

    def mods_setup(base):
        A = Alloc(kb, base)
        cfm = A.take([128, 8], F32)
        cact = A.take([128, 8], F32)
        wt = [A.take([128, 8, 512], BF16) for _ in range(2)]
        cactb = A.take([128, 8], BF16)
        brow = [A.take([1, 512], F32) for _ in range(3)]
        orow = [A.take([1, 512], F32) for _ in range(3)]
        kb.dma("sp", cfm.all(), c_in.all())
        kb.act(cact.all(), cfm.all(), AF.Silu)
        kb.cp("dve", cactb.all(), cact.all())
        jobs = []
        for n in range(18):
            jobs.append((cond_w, 0, cond_b, n, n * 512))
        for n in range(4):
            jobs.append((kv_cond_w, None, kv_cond_b, n, 18432 + n * 512))
        for n in range(18):
            jobs.append((cond_w, 1, cond_b, n, 9216 + n * 512))
        mods_state.update(cact=cactb, wt=wt, brow=brow, orow=orow, jobs=jobs, next=mods_state.get("next", 0))
        mods_state["next_load"] = mods_state["next"]
        mods_load(); mods_load()

    def mods_load():
        jl = mods_state.get("next_load", 0)
        jl = max(jl, mods_state["next"])
        if jl >= len(mods_state["jobs"]) or jl - mods_state["next"] >= 2:
            mods_state["next_load"] = jl
            return
        wd, l, bd, n, off = mods_state["jobs"][jl]
        w_ = mods_state["wt"][jl % 2]; b_ = mods_state["brow"][jl % 3]
        if l is None:
            src = wd.cv(wd.h[:, n * 512:(n + 1) * 512].rearrange("(k p) n -> p k n", p=128))
            bsrc = bd[0:1, n * 512:(n + 1) * 512]
        else:
            src = wd.cv(wd.h[l, :, n * 512:(n + 1) * 512].rearrange("(k p) n -> p k n", p=128),
                        l * D * 9216, (l + 1) * D * 9216)
            bsrc = bd[l:l + 1, n * 512:(n + 1) * 512]
        kb.dma("pool", w_.all(), src)
        kb.dma("sp", b_.all(), bsrc)
        mods_state["next_load"] = jl + 1

    def mods_job(pbanks=(0, 1)):
        ji = mods_state["next"]
        if ji >= len(mods_state["jobs"]):
            return False
        if mods_state.get("next_load", 0) <= ji:
            mods_state["next_load"] = ji
            mods_load()
        mods_state["next"] = ji + 1
        wd, l, bd, n, off = mods_state["jobs"][ji]
        w_ = mods_state["wt"][ji % 2]; b_ = mods_state["brow"][ji % 3]; o_ = mods_state["orow"][ji % 3]
        cact = mods_state["cact"]
        pm = kb.pbank(pbanks[ji % len(pbanks)], [128, 512])
        for k in range(8):
            kb.mm(pm[0:1, :], cact[:, k:k + 1], w_[:, k, :], k == 0, k == 7)
        kb.tt("dve", o_.all(), pm[0:1, :], b_.all(), ALU.add)
        kb.dma("sp", mods_d[0:1, off:off + 512], o_.all())
        mods_load()
        return True

    def load_mod_vectors(A, moff, g_pre_v, g_post_v, rw):
        AT = A.take([128, 8], F32); BT = A.take([128, 8], F32)
        tmp = A.take([128, 8], F32)
        G = None
        if g_pre_v is not None:
            kb.dma("sp", BT.all(), mods_src.cv(mods_src.h[0, moff:moff + 1024].rearrange("(k p) -> p k", p=128),
                                               moff, moff + 1024), slow=True)
            kb.dma("sp", tmp.all(), mods_src.cv(mods_src.h[0, moff + 1024:moff + 2048].rearrange("(k p) -> p k", p=128),
                                                moff + 1024, moff + 2048), slow=True)
            kb.dma("sp", AT.all(), g_pre_v, slow=True)
            kb.stt("dve", AT.all(), tmp.all(), 1.0, AT.all(), ALU.add, ALU.mult)
        if g_post_v is not None:
            G = A.take([128, 1024], F32)
            gt = A.take([128, 1024], F32)
            kb.dma("sp", G.all(), mods_src.cv(mods_src.h[0, moff + 2048:moff + 3072].partition_broadcast(128),
                                              moff + 2048, moff + 3072))
            kb.dma("sp", gt.all(), g_post_v)
            kb.stt("dve", G.all(), G.all(), float(rw), gt.all(), ALU.mult, ALU.mult)
            A.off -= 4096
        return AT, BT, G

    def rstd(out, ss, tmp, n):
        kb.act(tmp, ss, AF.Sqrt, bias=epsc[:, 0:1], scale=1.0 / n)
        kb.recip(out, tmp)

    def gvec_fm(t, idx):
        ap = t.h[idx].rearrange("(k p) -> p k", p=128)
        return t.cv(ap)

    def gvec_rep(t, idx, n=1024):
        ap = t.h[idx].partition_broadcast(128)
        return t.cv(ap)

    def make_scratch(A, n, with_xo=True):
        out = []
        for _ in range(n):
            d = dict(hb=A.take([128, D], BF16), tmpm=A.take([128, D], F32), junk=A.take([128, D], BF16),
                     stat=A.take([128, 16], F32))
            if with_xo:
                d["xo"] = A.take([128, D], F32)
            out.append(d)
        return out

    def g_prenorm(src, t, xt_v, sc, ptb, hT_dst, AT, BT):
        hb = sc["hb"]; stat = sc["stat"]; tmpm = sc["tmpm"]
        kb.dma("sp", xt_v, src[t * 128:(t + 1) * 128, :])
        yield
        kb.act(sc["junk"].all(), xt_v, AF.Square, accum=stat[:, 0:1])
        yield
        rstd(stat[:, 2:3], stat[:, 0:1], stat[:, 1:2], D)
        yield
        kb.act(hb.all(), xt_v, AF.Identity, scale=stat[:, 2:3])
        yield
        pT = kb.pbank(ptb, [128, 8, 128], BF16)
        for k in range(8):
            kb.tr(pT[:, k, :], hb[:, k * 128:(k + 1) * 128], identb.all())
        yield
        t3 = tmpm.cv(tmpm.ap.rearrange("p (a b) -> p a b", a=8))
        kb.tt("dve", t3, pT.all(), bcast_last(AT.all(), AT, 128), ALU.mult)
        yield
        kb.tt("pool", hT_dst, t3, bcast_last(BT.all(), BT, 128), ALU.add)
        yield

    def prenorm_tile(*a):
        for _ in g_prenorm(*a):
            pass

    def g_postnorm(py_views, xt_v, sc, G, dst, t):
        stat = sc["stat"]; xo = sc["xo"]; junk = sc["junk"]
        kb.act(junk[:, 0:512], py_views[0], AF.Square, accum=stat[:, 4:5])
        kb.act(junk[:, 512:1024], py_views[1], AF.Square, accum=stat[:, 5:6])
        yield
        kb.tt("dve", stat[:, 6:7], stat[:, 4:5], stat[:, 5:6], ALU.add)
        yield
        rstd(stat[:, 7:8], stat[:, 6:7], stat[:, 3:4], D)
        yield
        for hf in range(2):
            kb.stt("dve", xo[:, hf * 512:(hf + 1) * 512], py_views[hf], stat[:, 7:8], G[:, hf * 512:(hf + 1) * 512],
                   ALU.mult, ALU.mult)
        yield
        kb.tt("pool", xo.all(), xo.all(), xt_v, ALU.add)
        yield
        kb.dma("sp", dst[t * 128:(t + 1) * 128, :], xo.all())
        yield

    def postnorm_tile(*a):
        for _ in g_postnorm(*a):
            pass

    def multi_chain(gens):
        gens = list(gens)
        while gens:
            for g in list(gens):
                try:
                    next(g)
                except StopIteration:
                    gens.remove(g)
            yield

    def run_tasks(tasks, width=4):
        n = len(tasks)
        done = [False] * n; started = [False] * n; gens = [None] * n
        active = []
        first_unstarted = 0
        while True:
            while len(active) < width:
                cand = None
                for j in range(first_unstarted, n):
                    if not started[j] and all(done[d] for d in tasks[j][1]):
                        cand = j; break
                if cand is None:
                    break
                started[cand] = True; gens[cand] = tasks[cand][0](); active.append(cand)
                while first_unstarted < n and started[first_unstarted]:
                    first_unstarted += 1
            if not active:
                break
            for j in list(active):
                try:
                    next(gens[j])
                except StopIteration:
                    done[j] = True; active.remove(j)
        assert all(done), "task graph stuck"

    def ffn_layout():
        A = Alloc(kb)
        WGU = A.take([128, 8, 2 * DFF], BF16)
        WDN = A.take([128, 22, D], BF16)
        return A, WGU, WDN

    def ffn_load(layer, which, lo=0, hi=1 << 30):
        _, WGU, WDN = ffn_layout()
        wg = ffn_w_gu; wd = ffn_w_down
        for k in range(8):
            for hf in range(2):
                dv = WGU[:, k, hf * DFF:(hf + 1) * DFF]
                if lo < dv.hi <= hi:
                    kb.dma("pool", dv, wg[layer, which, k * 128:(k + 1) * 128, hf * DFF:(hf + 1) * DFF])
        for m in range(0, 22, 2):
            base = ((layer * 2 + which) * DFF + m * 128) * D
            dv = WDN[:, m:m + 2, :]
            if lo < dv.hi <= hi:
                kb.dma("pool", dv,
                       wd.cv(wd.h[layer, which, m * 128:(m + 2) * 128, :].rearrange("(a p) n -> p a n", p=128),
                             base, base + 256 * D))

    def phase_ffn(name, layer, which, moff, gidx, preloaded=False):
        src, dst = route[name]
        A, WGU, WDN = ffn_layout()
        if not preloaded:
            ffn_load(layer, which)
        AT, BT, G = load_mod_vectors(A, moff, gvec_fm(norm_g, (layer, gidx, 0)), gvec_rep(norm_g, (layer, gidx, 1)), 0.5)
        xt = A.take([128, 2, D], F32)
        xr = A.take([128, 1, D], F32)
        hT = A.take([128, 2, 8, 512], BF16)
        actb = A.take([128, 22, 512], BF16)
        sg = A.take([128, 2, 512], F32)
        sc = make_scratch(A, 1)[0]

        def pre(g):
            for j in range(4):
                t = g * 4 + j
                prenorm_tile(src, t, xt[:, t % 2, :], sc, 0, hT[:, g % 2, :, j * 128:(j + 1) * 128], AT, BT)

        pre(0)
        for g in range(SEQ // 512):
            hs = g % 2
            for m in range(22):
                pg = kb.pbank(1 + (m % 2), [128, 512])
                pu = kb.pbank(3 + (m % 2), [128, 512])
                for k in range(8):
                    kb.mm(pg.all(), WGU[:, k, m * 128:(m + 1) * 128], hT[:, hs, k, :], k == 0, k == 7)
                for k in range(8):
                    kb.mm(pu.all(), WGU[:, k, DFF + m * 128:DFF + (m + 1) * 128], hT[:, hs, k, :], k == 0, k == 7)
                kb.act(sg[:, m % 2, :], pg.all(), AF.Silu)
                kb.tt("dve", actb[:, m, :], sg[:, m % 2, :], pu.all(), ALU.mult)
            if g + 1 < SEQ // 512:
                pre(g + 1)
            for j in range(4):
                t = g * 4 + j
                kb.dma("sp", xr[:, 0, :], src[t * 128:(t + 1) * 128, :])
                pys = []
                for hf in range(2):
                    py = kb.pbank(5 + ((2 * t + hf) % 3), [128, 512])
                    for m in range(22):
                        kb.mm(py.all(), actb[:, m, j * 128:(j + 1) * 128], WDN[:, m, hf * 512:(hf + 1) * 512], m == 0, m == 21)
                    pys.append(py.all())
                postnorm_tile(pys, xr[:, 0, :], sc, G, dst, t)


    def phase_gla():
        src, dst = route["gla"]
        A = Alloc(kb)
        WIN = A.take([128, 8, 3088], BF16)
        WOUT = A.take([128, 8, D], BF16)
        WG = A.take([16, 512], F32)
        BG = A.take([1, 512], F32)
        GO = A.take([128, 256], F32)
        AT, BT, G = load_mod_vectors(A, 3072, gvec_fm(norm_g, (0, 1, 0)), gvec_rep(norm_g, (0, 1, 1)), 1.0)
        S32 = A.take([128, 4, 256], F32)
        Sb = A.take([128, 3, 4, 256], BF16)
        xt = A.take([128, 2, D], F32)
        hT = A.take([128, 2, 8, 128], BF16)
        scs = make_scratch(A, 2)
        gT_sb = A.take([16, 128], F32)
        e1 = A.take([128, 512], F32)
        lsp = A.take([128, 512], F32)
        eb = A.take([128, 4, 128], F32)
        enb = A.take([128, 4, 128], F32)
        qd = A.take([128, 4, 128], BF16)
        qdA = A.take([128, 4, 128], BF16)
        qdB = A.take([128, 4, 128], BF16)
        kd = A.take([128, 4, 128], BF16)
        eu = A.take([128, 512], F32)
        ku = A.take([128, 512], BF16)
        vb = A.take([128, D], BF16)
        vm = A.take([128, 4, 2, 256], BF16)
        attT = A.take([128, 4, 128], BF16)
        sr = A.take([128, D], F32)
        on = A.take([128, D], F32)
        og = A.take([128, D], BF16)
        ogT = A.take([128, 8, 128], BF16)
        if "mods" in plan:
            mods_setup((A.off + 63) // 64 * 64)
        for k in range(8):
            kb.dma("pool", WIN[:, k, :], gla_w_in[k * 128:(k + 1) * 128, :])
        for k in range(0, 8, 2):
            kb.dma("pool", WOUT[:, k:k + 2, :],
                   gla_w_out.cv(gla_w_out.h[k * 128:(k + 2) * 128, :].rearrange("(a p) n -> p a n", p=128),
                                k * 128 * D, (k + 2) * 128 * D))
        kb.dma("sp", WG.all(), gla_w_gate_up.all())
        kb.dma("sp", BG.all(), gla_b_gate.all())
        kb.dma("sp", GO.all(), gla_g_out.cv(gla_g_out.h[0].partition_broadcast(128)))
        kb.memset("dve", S32.all(), 0.0)
        kb.memset("dve", Sb[:, 0, :, :], 0.0)
        kb.memset("pool", qdA.all(), 0.0)
        kb.memset("pool", qdB.all(), 0.0)
        qsc = 128.0 ** -0.5
        updb = [(1, 0), (1, 1024), (2, 0), (2, 1024), (3, 0), (3, 1024), (6, 0), (6, 1024)]
        stage = int(os.environ.get("GLA_STAGE", "99"))
        for t in range(int(os.environ.get("GLA_NT", str(NT)))):
            n = 2 * t
            hTt = lambda k: hT[:, t % 2, k, :]
            sc = scs[t % 2]; stat = sc["stat"]; junk = sc["junk"]
            prenorm_tile(src, t, xt[:, t % 2, :], sc, 0, hT[:, t % 2, :, :], AT, BT)
            qT_ps = kb.pbank(1, [128, 4, 128]); kT_ps = kb.pbank(2, [128, 4, 128])
            ktok_ps = kb.pbank(3, [128, 512])
            gT_ps = kb.pbank(7, [128, 128])
            for h in range(4):
                for k in range(8):
                    kb.mm(qT_ps[:, h, :], WIN[:, k, h * 128:(h + 1) * 128], hTt(k), k == 0, k == 7)
            for h in range(4):
                for k in range(8):
                    kb.mm(kT_ps[:, h, :], WIN[:, k, 512 + h * 128:512 + (h + 1) * 128], hTt(k), k == 0, k == 7)
            for k in range(8):
                kb.mm(ktok_ps.all(), hTt(k), WIN[:, k, 512:1024], k == 0, k == 7)
            for k in range(8):
                kb.mm(gT_ps[0:16, :], WIN[:, k, 2048:2064], hTt(k), k == 0, k == 7)
            v_ps = [kb.pbank(4, [128, 512]), kb.pbank(5, [128, 512])]
            for nh in range(2):
                for k in range(8):
                    kb.mm(v_ps[nh].all(), hTt(k), WIN[:, k, 1024 + nh * 512:1024 + (nh + 1) * 512], k == 0, k == 7)
            kb.cp("act", vb[:, 0:512], v_ps[0].all())
            kb.cp("act", vb[:, 512:1024], v_ps[1].all())
            for nh in range(2):
                for c in range(2):
                    dstv = vm.cv(vm.ap[:, 2 * nh:2 * nh + 2, c, :], 2 * nh * 512, (2 * nh + 2) * 512)
                    srcv = View(v_ps[nh].ap.rearrange("p (h e) -> p h e", h=2), v_ps[nh].r, 0, 2048)
                    kb.act(dstv, srcv, AF.Identity, scale=mab[:, c:c + 1])
            if stage < 2:
                kb.dma("sp", dst[t * 128:(t + 1) * 128, :], xt[:, t % 2, :]); continue
            kb.cp("dve", gT_sb.all(), gT_ps[0:16, :])
            z_ps = kb.pbank(6, [128, 512])
            kb.mm(z_ps.all(), gT_sb.all(), WG.all(), True, False)
            kb.mm(z_ps.all(), onef[0:1, 0:128], BG.all(), False, True)
            kb.act(e1.all(), z_ps.all(), AF.Exp, scale=-1.0)
            kb.act(lsp.all(), e1.all(), AF.Ln, bias=onef[:, 0:1])
            if stage < 3:
                kb.dma("sp", dst[t * 128:(t + 1) * 128, :], xt[:, t % 2, :]); continue
            r_ps = [kb.pbank(4, [128, 512]), kb.pbank(5, [128, 512])]
            for nh in range(2):
                for k in range(8):
                    kb.mm(r_ps[nh].all(), hTt(k), WIN[:, k, 2064 + nh * 512:2064 + (nh + 1) * 512], k == 0, k == 7)
            kb.act(sr[:, 0:512], r_ps[0].all(), AF.Silu)
            kb.act(sr[:, 512:1024], r_ps[1].all(), AF.Silu)
            if stage < 4:
                kb.dma("sp", dst[t * 128:(t + 1) * 128, :], xt[:, t % 2, :]); continue
            bT_ps = kb.pbank(6, [128, 4, 128])
            U_ps = kb.pbank(7, [128, 512])
            for h in range(4):
                kb.mm(bT_ps[:, h, :], lsp[:, h * 128:(h + 1) * 128], m2sf.all(), True, True)
            kb.mm(U_ps.all(), u2sf.all(), lsp.all(), True, True)
            kb.act(eb.all(), bT_ps.all(), AF.Exp)
            kb.act(enb.all(), bT_ps.all(), AF.Exp, scale=-1.0)
            kb.act(eu.all(), U_ps.all(), AF.Exp)
            kb.stt("dve", qd.all(), qT_ps.all(), qsc, eb.all(), ALU.mult, ALU.mult)
            kb.tt("dve", kd.all(), kT_ps.all(), enb.all(), ALU.mult)
            kb.cp("pool", qdA[:, :, 0:64], qd[:, :, 0:64])
            kb.cp("pool", qdB[:, :, 64:128], qd[:, :, 64:128])
            kb.tt("dve", ku.all(), ktok_ps.all(), eu.all(), ALU.mult)
            if stage < 5:
                kb.dma("sp", dst[t * 128:(t + 1) * 128, :], xt[:, t % 2, :]); continue
            sub = int(os.environ.get("GLA_SUB", "9"))
            att_ps = kb.pbank(0, [128, 4, 128])
            for h in range(4):
                kb.mm(att_ps[:, h, :], kd[:, h, :], qd[:, h, :], True, True)
            upd = []
            for h in range(4):
                u_ps = kb.pbank((1, 2, 3, 6)[h], [128, 2, 256])
                if sub >= 2:
                    kb.mm(u_ps.all(), ku[:, h * 128:(h + 1) * 128], vm[:, h, :, :], True, True)
                upd.append(u_ps[:, 0, :]); upd.append(u_ps[:, 1, :])
            if sub >= 4:
                for h in range(4):
                    kb.tt("dve", attT[:, h, :], att_ps[:, h, :], m2b.all(), ALU.mult)
            o_ps = [kb.pbank((4, 5, 7, 0)[h], [128, 256]) for h in range(4)]
            if sub >= 5:
                for h in range(4):
                    kb.mm(o_ps[h].all(), attT[:, h, :], vb[:, h * 256:(h + 1) * 256], True, False)
                    kb.mm(o_ps[h].all(), qdA[:, h, :], Sb[:, n % 3, h, :], False, sub < 6)
            if sub >= 6:
                for h in range(4):
                    dA = eb[:, h, 63:64]; dB = eb[:, h, 127:128]
                    kb.stt("dve", Sb[:, (n + 1) % 3, h, :], S32[:, h, :], dA, upd[2 * h], ALU.mult, ALU.add)
                    kb.stt("dve", S32[:, h, :], S32[:, h, :], dA, upd[2 * h], ALU.mult, ALU.add)
                for h in range(4):
                    kb.mm(o_ps[h].all(), qdB[:, h, :], Sb[:, (n + 1) % 3, h, :], False, True)
                for h in range(4):
                    dB = eb[:, h, 127:128]
                    kb.stt("dve", Sb[:, (n + 2) % 3, h, :], S32[:, h, :], dB, upd[2 * h + 1], ALU.mult, ALU.add)
                    kb.stt("dve", S32[:, h, :], S32[:, h, :], dB, upd[2 * h + 1], ALU.mult, ALU.add)
            if stage < 6:
                kb.dma("sp", dst[t * 128:(t + 1) * 128, :], xt[:, t % 2, :]); continue
            for h in range(4):
                kb.act(junk[:, h * 256:(h + 1) * 256], o_ps[h].all(), AF.Square, accum=stat[:, 8 + h:9 + h])
            kb.act(stat[:, 4:8], stat[:, 8:12], AF.Sqrt, bias=epsc[:, 0:1], scale=1.0 / 256)
            kb.recip(stat[:, 12:16], stat[:, 4:8])
            for h in range(4):
                kb.stt("dve", on[:, h * 256:(h + 1) * 256], o_ps[h].all(), stat[:, 12 + h:13 + h], GO.all(), ALU.mult, ALU.mult)
            kb.tt("pool", og.all(), on.all(), sr.all(), ALU.mult)
            pT = kb.pbank(6, [128, 8, 128], BF16)
            for k in range(8):
                kb.tr(pT[:, k, :], og[:, k * 128:(k + 1) * 128], identb.all())
            kb.cp("act", ogT.all(), pT.all())
            pys = [kb.pbank(1, [128, 512]), kb.pbank(2, [128, 512])]
            for hf in range(2):
                for k in range(8):
                    kb.mm(pys[hf].all(), ogT[:, k, :], WOUT[:, k, hf * 512:(hf + 1) * 512], k == 0, k == 7)
            postnorm_tile([p.all() for p in pys], xt[:, t % 2, :], sc, G, dst, t)
            if "mods" in plan:
                mods_job((0,))


    def phase_gla2():
        src, dst = route["gla"]
        A = Alloc(kb)
        WIN = A.take([128, 8, 3088], BF16)
        WOUT = A.take([128, 8, D], BF16)
        WG = A.take([16, 512], F32)
        BG = A.take([1, 512], F32)
        GO = A.take([128, 256], F32)
        AT, BT, G = load_mod_vectors(A, 3072, gvec_fm(norm_g, (0, 1, 0)), gvec_rep(norm_g, (0, 1, 1)), 1.0)
        S32 = A.take([128, 4, 256], F32)
        Sb = A.take([128, 3, 4, 256], BF16)
        xt = A.take([128, 2, D], F32)
        hT = A.take([128, 2, 8, 128], BF16)
        scA = make_scratch(A, 2, with_xo=False)
        scB = [dict(xo=A.take([128, D], F32), junk=A.take([128, D], BF16), stat=A.take([128, 16], F32)) for _ in range(2)]
        qd = [A.take([128, 4, 128], BF16) for _ in range(2)]
        qdA = [A.take([128, 4, 128], BF16) for _ in range(2)]
        qdB = [A.take([128, 4, 128], BF16) for _ in range(2)]
        kd = [A.take([128, 4, 128], BF16) for _ in range(2)]
        ku = [A.take([128, 512], BF16) for _ in range(2)]
        vb = [A.take([128, D], BF16) for _ in range(2)]
        vm = [A.take([128, 4, 2, 256], BF16) for _ in range(2)]
        eb = [A.take([128, 4, 128], F32) for _ in range(2)]
        sr = [A.take([128, 4, 256], F32) for _ in range(2)]
        gT_sb = A.take([16, 128], F32)
        lsp = A.take([128, 512], F32)
        enb = A.take([128, 4, 128], F32)
        eu = A.take([128, 512], F32)
        attT = A.take([128, 4, 128], BF16)
        og = A.take([128, D], BF16)
        ogT = A.take([128, 8, 128], BF16)
        if "mods" in plan:
            mods_setup((A.off + 63) // 64 * 64)
        for k in range(8):
            kb.dma("pool", WIN[:, k, :], gla_w_in[k * 128:(k + 1) * 128, :])
        for k in range(0, 8, 2):
            kb.dma("pool", WOUT[:, k:k + 2, :],
                   gla_w_out.cv(gla_w_out.h[k * 128:(k + 2) * 128, :].rearrange("(a p) n -> p a n", p=128),
                                k * 128 * D, (k + 2) * 128 * D))
        kb.dma("sp", WG.all(), gla_w_gate_up.all())
        kb.dma("sp", BG.all(), gla_b_gate.all())
        kb.dma("sp", GO.all(), gla_g_out.cv(gla_g_out.h[0].partition_broadcast(128)))
        kb.memset("dve", S32.all(), 0.0)
        kb.memset("dve", Sb[:, 0, :, :], 0.0)
        for p in range(2):
            kb.memset("pool", qdA[p].all(), 0.0)
            kb.memset("pool", qdB[p].all(), 0.0)
        qsc = 128.0 ** -0.5
        ntl = int(os.environ.get("GLA_NT", str(NT)))

        def chain_a(t):
            p = t % 2
            hTt = lambda k: hT[:, p, k, :]
            yield from g_prenorm(src, t, xt[:, p, :], scA[p], 0, hT[:, p, :, :], AT, BT)
            qT_ps = kb.pbank(1, [128, 4, 128]); kT_ps = kb.pbank(2, [128, 4, 128])
            ktok_ps = kb.pbank(3, [128, 512])
            gT_ps = kb.pbank(4, [128, 128])
            for h in range(4):
                for k in range(8):
                    kb.mm(qT_ps[:, h, :], WIN[:, k, h * 128:(h + 1) * 128], hTt(k), k == 0, k == 7)
            yield
            for h in range(4):
                for k in range(8):
                    kb.mm(kT_ps[:, h, :], WIN[:, k, 512 + h * 128:512 + (h + 1) * 128], hTt(k), k == 0, k == 7)
            yield
            for k in range(8):
                kb.mm(gT_ps[0:16, :], WIN[:, k, 2048:2064], hTt(k), k == 0, k == 7)
            for k in range(8):
                kb.mm(ktok_ps.all(), hTt(k), WIN[:, k, 512:1024], k == 0, k == 7)
            yield
            kb.cp("dve", gT_sb.all(), gT_ps[0:16, :])
            sbank = [0, 4]
            v0 = kb.pbank(0, [128, 512])
            for k in range(8):
                kb.mm(v0.all(), hTt(k), WIN[:, k, 1024:1536], k == 0, k == 7)
            yield
            z_ps = kb.pbank(4, [128, 512])
            kb.mm(z_ps.all(), gT_sb.all(), WG.all(), True, False)
            kb.mm(z_ps.all(), onef[0:1, 0:128], BG.all(), False, True)
            yield

            def v_evac(v_ps, nh):
                kb.cp("act", vb[p][:, nh * 512:(nh + 1) * 512], v_ps.all())
                for c in range(2):
                    dstv = vm[p].cv(vm[p].ap[:, 2 * nh:2 * nh + 2, c, :], 2 * nh * 512, (2 * nh + 2) * 512)
                    srcv = View(v_ps.ap.rearrange("p (h e) -> p h e", h=2), v_ps.r, 0, 2048)
                    kb.act(dstv, srcv, AF.Identity, scale=mab[:, c:c + 1])
            v_evac(v0, 0)
            yield
            kb.act(lsp.all(), z_ps.all(), AF.Exp, scale=-1.0)
            yield
            v1 = kb.pbank(0, [128, 512])
            for k in range(8):
                kb.mm(v1.all(), hTt(k), WIN[:, k, 1536:2048], k == 0, k == 7)
            kb.act(lsp.all(), lsp.all(), AF.Ln, bias=onef[:, 0:1])
            yield
            v_evac(v1, 1)
            yield
            bT_ps = kb.pbank(4, [128, 4, 128])
            for h in range(4):
                kb.mm(bT_ps[:, h, :], lsp[:, h * 128:(h + 1) * 128], m2sf.all(), True, True)
            yield
            U_ps = kb.pbank(0, [128, 512])
            kb.mm(U_ps.all(), u2sf.all(), lsp.all(), True, True)
            yield
            kb.act(eb[p].all(), bT_ps.all(), AF.Exp)
            kb.act(enb.all(), bT_ps.all(), AF.Exp, scale=-1.0)
            yield
            kb.act(eu.all(), U_ps.all(), AF.Exp)
            yield
            r0 = kb.pbank(4, [128, 512]); r1 = kb.pbank(0, [128, 512])
            for k in range(8):
                kb.mm(r0.all(), hTt(k), WIN[:, k, 2064:2576], k == 0, k == 7)
            yield
            kb.stt("dve", qd[p].all(), qT_ps.all(), qsc, eb[p].all(), ALU.mult, ALU.mult)
            kb.tt("dve", kd[p].all(), kT_ps.all(), enb.all(), ALU.mult)
            yield
            for k in range(8):
                kb.mm(r1.all(), hTt(k), WIN[:, k, 2576:3088], k == 0, k == 7)
            kb.tt("dve", ku[p].all(), ktok_ps.all(), eu.all(), ALU.mult)
            yield
            kb.cp("pool", qdA[p][:, :, 0:64], qd[p][:, :, 0:64])
            kb.cp("pool", qdB[p][:, :, 64:128], qd[p][:, :, 64:128])
            srf = sr[p].cv(sr[p].ap.rearrange("p h e -> p (h e)"))
            kb.act(View(srf.ap[:, 0:512], srf.r, srf.lo, srf.hi), r0.all(), AF.Silu)
            yield
            kb.act(View(srf.ap[:, 512:1024], srf.r, srf.lo, srf.hi), r1.all(), AF.Silu)
            yield
            gob = View(GO.all().ap.unsqueeze(1).broadcast_to([128, 4, 256]), GO.r, GO.off, GO.off + GO.nbytes)
            kb.tt("pool", sr[p].all(), sr[p].all(), gob, ALU.mult)
            yield

        def chain_b(t):
            p = t % 2
            n = 2 * t
            sc = scB[p]; stat = sc["stat"]; junk = sc["junk"]
            att_ps = kb.pbank(5, [128, 4, 128])
            for h in range(4):
                kb.mm(att_ps[:, h, :], kd[p][:, h, :], qd[p][:, h, :], True, True)
            yield
            for h in range(4):
                kb.tt("dve", attT[:, h, :], att_ps[:, h, :], m2b.all(), ALU.mult)
            yield
            o_list = [kb.pbank(5 if h % 2 == 0 else 7, [128, 256]) for h in range(4)]

            def evac(h):
                o_ps = o_list[h]
                kb.act(junk[:, h * 256:(h + 1) * 256], o_ps.all(), AF.Square, accum=stat[:, 8 + h:9 + h])
                yield
                kb.act(stat[:, 4 + h:5 + h], stat[:, 8 + h:9 + h], AF.Sqrt, bias=epsc[:, 0:1], scale=1.0 / 256)
                yield
                kb.recip(stat[:, 12 + h:13 + h], stat[:, 4 + h:5 + h])
                yield
                kb.stt("dve", og[:, h * 256:(h + 1) * 256], o_ps.all(), stat[:, 12 + h:13 + h], sr[p][:, h, :],
                       ALU.mult, ALU.mult)
                yield

            for h in range(4):
                u_ps = kb.pbank(6, [128, 2, 256])
                o_ps = o_list[h]
                if h >= 2:
                    yield from evac(h - 2)
                kb.mm(u_ps.all(), ku[p][:, h * 128:(h + 1) * 128], vm[p][:, h, :, :], True, True)
                kb.mm(o_ps.all(), attT[:, h, :], vb[p][:, h * 256:(h + 1) * 256], True, False)
                kb.mm(o_ps.all(), qdA[p][:, h, :], Sb[:, n % 3, h, :], False, False)
                if h >= 1:
                    kb.mm(o_list[h - 1].all(), qdB[p][:, h - 1, :], Sb[:, (n + 1) % 3, h - 1, :], False, True)
                yield
                dA = eb[p][:, h, 63:64]; dB = eb[p][:, h, 127:128]
                kb.stt("dve", Sb[:, (n + 1) % 3, h, :], S32[:, h, :], dA, u_ps[:, 0, :], ALU.mult, ALU.add)
                kb.stt("dve", S32[:, h, :], S32[:, h, :], dA, u_ps[:, 0, :], ALU.mult, ALU.add)
                kb.stt("dve", Sb[:, (n + 2) % 3, h, :], S32[:, h, :], dB, u_ps[:, 1, :], ALU.mult, ALU.add)
                kb.stt("dve", S32[:, h, :], S32[:, h, :], dB, u_ps[:, 1, :], ALU.mult, ALU.add)
                yield
            kb.mm(o_list[3].all(), qdB[p][:, 3, :], Sb[:, (n + 1) % 3, 3, :], False, True)
            yield
            yield from evac(2)
            yield from evac(3)
            pT = kb.pbank(6, [128, 8, 128], BF16)
            for k in range(8):
                kb.tr(pT[:, k, :], og[:, k * 128:(k + 1) * 128], identb.all())
            yield
            kb.cp("act", ogT.all(), pT.all())
            yield
            pys = [kb.pbank(5, [128, 512]), kb.pbank(7, [128, 512])]
            for hf in range(2):
                for k in range(8):
                    kb.mm(pys[hf].all(), ogT[:, k, :], WOUT[:, k, hf * 512:(hf + 1) * 512], k == 0, k == 7)
            yield
            yield from g_postnorm([q.all() for q in pys], xt[:, p, :], sc, G, dst, t)
            if "mods" in plan:
                mods_job((6,))
                yield

        tasks = []
        idx_a = {}; idx_b = {}
        for t in range(ntl):
            if t == 0:
                idx_a[0] = len(tasks); tasks.append((lambda: chain_a(0), []))
            if t + 1 < ntl:
                deps = [idx_a[t]] + ([idx_b[t - 1]] if t - 1 >= 0 else [])
                idx_a[t + 1] = len(tasks); tasks.append((lambda tt=t + 1: chain_a(tt), deps))
            deps = [idx_a[t]] + ([idx_b[t - 1]] if t >= 1 else [])
            idx_b[t] = len(tasks); tasks.append((lambda tt=t: chain_b(tt), deps))
        run_tasks(tasks, 2)
        if "mods" in plan:
            while mods_job((6,)):
                pass

    def g_rope(cols, scale, R, par):
        c0, c1 = cols
        posi = R["posi"]; posf = R["posf"]; u = R["u"]; w = R["w"]; wi = R["wi"]; wf = R["wf"]; g1 = R["g1"]
        kb.dma("sp", posi.all(), pos_in.cv(pos_in.h[0, c0:c1].partition_broadcast(64), c0, c1))
        yield
        kb.cp("dve", posf.all(), posi.all())
        yield
        kb.tsc("dve", u.all(), posf.all(), invf[0:64, 0:1], 1.0 / (2.0 * math.pi), ALU.mult, ALU.mult)
        yield
        for tab, shift in ((R["sinT"][par], 0.0), (R["cosT"][par], 0.25)):
            kb.tsc("dve", w.all(), u.all(), shift, None, ALU.add)
            yield
            kb.cp("dve", wi.all(), w.all())
            yield
            kb.cp("dve", wf.all(), wi.all())
            yield
            kb.tt("dve", w.all(), w.all(), wf.all(), ALU.subtract)
            yield
            kb.tsc("dve", g1.all(), w.all(), 0.5, None, ALU.is_gt)
            yield
            kb.tt("dve", w.all(), w.all(), g1.all(), ALU.subtract)
            yield
            kb.tsc("dve", g1.all(), w.all(), -0.5, None, ALU.is_lt)
            yield
            kb.tt("dve", w.all(), w.all(), g1.all(), ALU.add)
            yield
            kb.act(tab.all(), w.all(), AF.Sin, scale=2.0 * math.pi)
            yield
            if scale != 1.0:
                kb.tsc("dve", tab.all(), tab.all(), float(scale), None, ALU.mult)
                yield

    def rope_alloc(A):
        return dict(cosT=[A.take([64, 512], F32) for _ in range(8)], sinT=[A.take([64, 512], F32) for _ in range(8)],
                    posi=A.take([64, 512], I32), posf=A.take([64, 512], F32), u=A.take([64, 512], F32),
                    w=A.take([64, 512], F32), wi=A.take([64, 512], I32), wf=A.take([64, 512], F32),
                    g1=A.take([64, 512], F32))

    def phase_kv():
        A = Alloc(kb, LIGHT_BASE)
        WKA = A.take([128, 8, 320], BF16)
        WKR = A.take([128, 8, 64], BF16)
        WKB = A.take([128, 2, 2048], BF16)
        WV = A.take([128, 2, 1024], BF16)
        GKV = A.take([128, 256], F32)
        AT, BT, _ = load_mod_vectors(A, 18432, gvec_fm(kv_g_in, 0), None, 0.0)
        R = rope_alloc(A)
        NS = 4
        xt = A.take([128, NS, D], F32)
        hT = A.take([128, 2, 8, 512], BF16)
        scs = make_scratch(A, NS, with_xo=False)
        cknN = [A.take([128, 256], BF16) for _ in range(NS)]
        ckvT = A.take([128, 2, 2, 512], BF16)
        vsb = A.take([128, NS, D], BF16)
        t1 = A.take([64, 512], F32); t2 = A.take([64, 512], F32)
        krs = A.take([64, 2, 512], BF16)
        ksb = A.take([128, 2, 512], BF16)
        for k in range(0, 8, 4):
            kb.dma("pool", WKA[:, k:k + 4, :],
                   mla_w_kv_a.cv(mla_w_kv_a.h[k * 128:(k + 4) * 128, :].rearrange("(a p) n -> p a n", p=128),
                                 k * 128 * 320, (k + 4) * 128 * 320))
        kb.dma("pool", WKB.all(), mla_w_kv_b.cv(mla_w_kv_b.h[:, :].rearrange("(a p) n -> p a n", p=128)))
        kb.dma("sp", GKV.all(), mla_g_kv.cv(mla_g_kv.h[0].partition_broadcast(128)))
        kb.S.op("act", lambda e: e.mul(out=WKR[:, :, 0:32].ap, in_=WKA[:, :, 288:320].ap, mul=-1.0),
                reads=[WKA[:, :, 288:320]], writes=[WKR[:, :, 0:32]])
        kb.cp("act", WKR[:, :, 32:64], WKA[:, :, 256:288])
        for c in range(2):
            srcv = WKB.cv(WKB.ap[:, c, :].rearrange("p (h e) -> p h e", h=8)[:, :, 128:256], c * 2048, (c + 1) * 2048)
            dstv = WV.cv(WV.ap[:, c, :].rearrange("p (h e) -> p h e", h=8), c * 1024, (c + 1) * 1024)
            kb.cp("dve", dstv, srcv)

        def tile_chain(t):
            bq, j = divmod(t, 4)
            hs = bq % 2
            jc = slice(j * 128, (j + 1) * 128)
            sc = scs[t % NS]; stat = sc["stat"]; junk = sc["junk"]; ckn = cknN[t % NS]
            yield from g_prenorm(kv_src, t, xt[:, t % NS, :], sc, t % 4, hT[:, hs, :, jc], AT, BT)
            ck_ps = kb.pbank(t % 4, [128, 256])
            for k in range(8):
                kb.mm(ck_ps.all(), hT[:, hs, k, jc], WKA[:, k, 0:256], k == 0, k == 7)
            yield
            kb.act(junk[:, 0:256], ck_ps.all(), AF.Square, accum=stat[:, 8:9])
            yield
            rstd(stat[:, 10:11], stat[:, 8:9], stat[:, 9:10], 256)
            yield
            kb.stt("dve", ckn.all(), ck_ps.all(), stat[:, 10:11], GKV.all(), ALU.mult, ALU.mult)
            yield
            pT2 = kb.pbank(t % 4, [128, 2, 128], BF16)
            for c in range(2):
                kb.tr(pT2[:, c, :], ckn[:, c * 128:(c + 1) * 128], identb.all())
            yield
            kb.cp("act", ckvT[:, hs, :, jc], pT2.all())
            yield
            for nh in range(2):
                v_ps = kb.pbank(t % 4, [128, 512])
                for c in range(2):
                    kb.mm(v_ps.all(), ckvT[:, hs, c, jc], WV[:, c, nh * 512:(nh + 1) * 512], c == 0, c == 1)
                yield
                kb.cp("act" if nh == 0 else "dve", vsb[:, t % NS, nh * 512:(nh + 1) * 512], v_ps.all())
                yield
            kb.dma("sp", v_d[t * 128:(t + 1) * 128, :], vsb[:, t % NS, :])
            yield

        def block_chain(bq):
            hs = bq % 2
            cols = (bq * 512, (bq + 1) * 512)
            kp_ps = kb.pbank(4, [64, 512]); kr_ps = kb.pbank(5, [64, 512])
            for k in range(8):
                kb.mm(kp_ps.all(), WKA[:, k, 256:320], hT[:, hs, k, :], k == 0, k == 7)
            for k in range(8):
                kb.mm(kr_ps.all(), WKR[:, k, :], hT[:, hs, k, :], k == 0, k == 7)
            yield
            kb.tt("dve", t1.all(), kp_ps.all(), R["cosT"][bq].all(), ALU.mult)
            kb.tt("dve", t2.all(), kr_ps.all(), R["sinT"][bq].all(), ALU.mult)
            yield
            kb.tt("pool", krs[:, hs, :], t1.all(), t2.all(), ALU.add)
            yield
            kb.dma("sp", krT_d[:, cols[0]:cols[1]], krs[:, hs, :])
            yield
            for h in range(8):
                kn_ps = kb.pbank((6, 7)[h % 2], [128, 512])
                for c in range(2):
                    kb.mm(kn_ps.all(), WKB[:, c, h * 256:h * 256 + 128], ckvT[:, hs, c, :], c == 0, c == 1)
                yield
                kb.cp("act" if h % 2 == 0 else "dve", ksb[:, h % 2, :], kn_ps.all())
                yield
                kb.dma("sp", kT_d[h, :, cols[0]:cols[1]], ksb[:, h % 2, :])
                yield

        tasks = []
        bidx = {}; ridx = {}
        for bq in range(SEQ // 512):
            base = len(tasks)
            if bq == 0:
                rope0 = len(tasks)
                tasks.append((lambda: g_rope((0, 512), 1.0, R, 0), []))
            dq = ([bidx[bq - 2]] if bq >= 2 else []) + ([ridx[bq - 1]] if bq >= 1 else [])
            ridx[bq] = len(tasks)
            tasks.append((lambda bq=bq: multi_chain([tile_chain(bq * 4 + j) for j in range(4)]), dq))
            if bq == 0:
                def rope_rest():
                    for b2 in range(1, SEQ // 512):
                        yield from g_rope((b2 * 512, (b2 + 1) * 512), 1.0, R, b2)
                roperest = len(tasks)
                tasks.append((rope_rest, [rope0]))
            bidx[bq] = len(tasks)
            tasks.append((lambda bq=bq: block_chain(bq), [ridx[bq], rope0] + ([roperest] if bq >= 1 else []) +
                          ([bidx[bq - 1]] if bq >= 1 else [])))
        run_tasks(tasks, int(os.environ.get("LIGHT_W", "2")))

    def phase_mlaq():
        src = route["mla"][0]
        A = Alloc(kb, LIGHT_BASE)
        WDQ = A.take([128, 8, 384], BF16)
        WUQ = A.take([128, 3, 1536], BF16)
        WQR = A.take([128, 3, 8, 64], BF16)
        GQ = A.take([128, 384], F32)
        AT, BT, _ = load_mod_vectors(A, 9216 + 3072, gvec_fm(norm_g, (1, 1, 0)), None, 0.0)
        R = rope_alloc(A)
        NS = 4
        xt = A.take([128, NS, D], F32)
        hT = A.take([128, 2, 8, 512], BF16)
        scs = make_scratch(A, NS, with_xo=False)
        cqnN = [A.take([128, 384], BF16) for _ in range(NS)]
        cqT = A.take([128, 2, 3, 512], BF16)
        qsb = A.take([128, 2, 512], BF16)
        t1 = A.take([64, 512], F32); t2 = A.take([64, 512], F32)
        qrs = A.take([64, 2, 512], BF16)
        qs = 192.0 ** -0.5
        for k in range(0, 8, 4):
            kb.dma("pool", WDQ[:, k:k + 4, :],
                   mla_w_dq.cv(mla_w_dq.h[k * 128:(k + 4) * 128, :].rearrange("(a p) n -> p a n", p=128),
                               k * 128 * 384, (k + 4) * 128 * 384))
        kb.dma("pool", WUQ.all(), mla_w_uq.cv(mla_w_uq.h[:, :].rearrange("(a p) n -> p a n", p=128)))
        kb.dma("sp", GQ.all(), mla_g_q.cv(mla_g_q.h[0].partition_broadcast(128)))
        for c in range(3):
            w4 = WUQ.ap[:, c, :].rearrange("p (h e) -> p h e", h=8)
            src_x2 = WUQ.cv(w4[:, :, 160:192], c * 1536, (c + 1) * 1536)
            src_x1 = WUQ.cv(w4[:, :, 128:160], c * 1536, (c + 1) * 1536)
            kb.S.op("act", lambda e, c=c, src_x2=src_x2: e.mul(out=WQR[:, c, :, 0:32].ap, in_=src_x2.ap, mul=-1.0),
                    reads=[src_x2], writes=[WQR[:, c, :, 0:32]])
            kb.cp("act", WQR[:, c, :, 32:64], src_x1)

        def tile_chain(t):
            bq, j = divmod(t, 4)
            hs = bq % 2
            jc = slice(j * 128, (j + 1) * 128)
            sc = scs[t % NS]; stat = sc["stat"]; junk = sc["junk"]; cqn = cqnN[t % NS]
            yield from g_prenorm(src, t, xt[:, t % NS, :], sc, t % 4, hT[:, hs, :, jc], AT, BT)
            cq_ps = kb.pbank(t % 4, [128, 384])
            for k in range(8):
                kb.mm(cq_ps.all(), hT[:, hs, k, jc], WDQ[:, k, :], k == 0, k == 7)
            yield
            kb.act(junk[:, 0:384], cq_ps.all(), AF.Square, accum=stat[:, 8:9])
            yield
            rstd(stat[:, 10:11], stat[:, 8:9], stat[:, 9:10], 384)
            yield
            kb.stt("dve", cqn.all(), cq_ps.all(), stat[:, 10:11], GQ.all(), ALU.mult, ALU.mult)
            yield
            pT3 = kb.pbank(t % 4, [128, 3, 128], BF16)
            for c in range(3):
                kb.tr(pT3[:, c, :], cqn[:, c * 128:(c + 1) * 128], identb.all())
            yield
            kb.cp("act", cqT[:, hs, :, jc], pT3.all())
            yield

        def block_chain(bq):
            hs = bq % 2
            cols = (bq * 512, (bq + 1) * 512)
            for h in range(8):
                qn_ps = kb.pbank(4 + h % 2, [128, 512])
                qp_ps = kb.pbank(6, [64, 512])
                qr_ps = kb.pbank(7, [64, 512])
                for c in range(3):
                    kb.mm(qn_ps.all(), WUQ[:, c, h * 192:h * 192 + 128], cqT[:, hs, c, :], c == 0, c == 2)
                for c in range(3):
                    kb.mm(qp_ps.all(), WUQ[:, c, h * 192 + 128:h * 192 + 192], cqT[:, hs, c, :], c == 0, c == 2)
                for c in range(3):
                    kb.mm(qr_ps.all(), WQR[:, c, h, :], cqT[:, hs, c, :], c == 0, c == 2)
                yield
                kb.act(qsb[:, h % 2, :], qn_ps.all(), AF.Identity, scale=qs)
                kb.tt("dve", t1.all(), qp_ps.all(), R["cosT"][bq].all(), ALU.mult)
                kb.tt("dve", t2.all(), qr_ps.all(), R["sinT"][bq].all(), ALU.mult)
                yield
                kb.dma("sp", qT_d[h, 0:128, cols[0]:cols[1]], qsb[:, h % 2, :])
                kb.tt("pool", qrs[:, h % 2, :], t1.all(), t2.all(), ALU.add)
                yield
                kb.dma("sp", qT_d[h, 128:192, cols[0]:cols[1]], qrs[:, h % 2, :])
                yield

        tasks = []
        bidx = {}; ridx = {}
        for bq in range(SEQ // 512):
            base = len(tasks)
            if bq == 0:
                rope0 = len(tasks)
                tasks.append((lambda: g_rope((0, 512), qs, R, 0), []))
            dq = ([bidx[bq - 2]] if bq >= 2 else []) + ([ridx[bq - 1]] if bq >= 1 else [])
            ridx[bq] = len(tasks)
            tasks.append((lambda bq=bq: multi_chain([tile_chain(bq * 4 + j) for j in range(4)]), dq))
            if bq == 0:
                def rope_rest():
                    for b2 in range(1, SEQ // 512):
                        yield from g_rope((b2 * 512, (b2 + 1) * 512), qs, R, b2)
                roperest = len(tasks)
                tasks.append((rope_rest, [rope0]))
            bidx[bq] = len(tasks)
            tasks.append((lambda bq=bq: block_chain(bq), [ridx[bq], rope0] + ([roperest] if bq >= 1 else []) +
                          ([bidx[bq - 1]] if bq >= 1 else [])))
        run_tasks(tasks, int(os.environ.get("LIGHT_W", "2")))

    def phase_attn():
        A = Alloc(kb, ATTN_BASE)
        KR = A.take([64, SEQ], BF16)
        KN = A.take([128, 2, SEQ], BF16)
        V = A.take([128, 2, NT, 128], BF16)
        QN = A.take([128, 2, 512], BF16)
        QR = A.take([64, 2, 512], BF16)
        NPT = 6
        PT = A.take([128, NPT, 512], BF16)
        RL = A.take([128, 2, 512], F32)
        OT = A.take([128, 2, 512], BF16)
        ACC = A.take([128, 2, 512], F32)
        kb.dma("sp", KR.all(), krT_d.all())
        SK = int(os.environ.get("ATTN_SKEW", "3"))
        blocks = []
        it = 0
        for h in range(8):
            for Q in range(SEQ // 512):
                nkt = 4 * Q + 4
                for kj in range(nkt):
                    blocks.append((h, Q, kj, nkt, it % 2))
                it += 1

        def s_block(i):
            h, Q, kj, nkt, par = blocks[i]
            hs = h % 2
            if kj == 0:
                cols = (Q * 512, (Q + 1) * 512)
                if Q == 0:
                    kb.dma("sp", KN[:, hs, :], kT_d[h, :, :])
                    for tq in range(0, NT, 8):
                        base = tq * 128 * D
                        kb.dma("sp", V[:, hs, tq:tq + 8, :],
                               v_d.cv(v_d.h[tq * 128:(tq + 8) * 128, h * 128:(h + 1) * 128].rearrange("(t p) v -> p t v", p=128),
                                      base, base + 8 * 128 * D))
                kb.dma("sp", QN[:, par, :], qT_d[h, 0:128, cols[0]:cols[1]])
                kb.dma("sp", QR[:, par, :], qT_d[h, 128:192, cols[0]:cols[1]])
                kb.memset("pool", ACC[:, par, :], 0.0)
            c0 = max(0, kj - 4 * Q) * 128
            ps = kb.pbank(i % 4, [128, 512])
            kb.mm(ps[:, c0:512], KN[:, hs, kj * 128:(kj + 1) * 128], QN[:, par, c0:512], True, False)
            if kj >= 4 * Q:
                kb.mm(ps[:, c0:c0 + 128], identb.all(), negb.all(), False, False)
            kb.mm(ps[:, c0:512], KR[:, kj * 128:(kj + 1) * 128], QR[:, par, c0:512], False, True)
            kb.act(PT[:, i % NPT, c0:512], ps[:, c0:512], AF.Exp)

        def pv_block(i):
            h, Q, kj, nkt, par = blocks[i]
            hs = h % 2
            c0 = max(0, kj - 4 * Q) * 128
            po = kb.pbank(4 + par, [128, 512])
            pl = kb.pbank(6 + par, [128, 512])
            kb.mm(po[:, c0:512], V[:, hs, kj, :], PT[:, i % NPT, c0:512], kj == 0, kj == nkt - 1)
            if kj % 2 == 0:
                kb.mm(pl[:, c0:512], oneb.all(), PT[:, i % NPT, c0:512], kj == 0, False)
            else:
                kb.tt("dve", ACC[:, par, c0:512], ACC[:, par, c0:512], PT[:, i % NPT, c0:512], ALU.add)
            if kj == nkt - 1:
                cols = (Q * 512, (Q + 1) * 512)
                kb.mm(pl.all(), onef.all(), ACC[:, par, :], False, True)
                kb.recip(RL[:, par, :], pl.all())
                kb.tt("dve", OT[:, par, :], po.all(), RL[:, par, :], ALU.mult)
                kb.dma("sp", oT_d[h, :, cols[0]:cols[1]], OT[:, par, :])

        nb = len(blocks)
        for i in range(min(SK, nb)):
            s_block(i)
        for i in range(nb):
            if i + SK < nb:
                s_block(i + SK)
            pv_block(i)

    def phase_mlao():
        src, dst = route["mla"]
        A = Alloc(kb, MLAO_BASE)
        WO = A.take([128, 8, D], BF16)
        _, _, G = load_mod_vectors(A, 9216 + 3072, None, gvec_rep(norm_g, (1, 1, 1)), 1.0)
        NS = int(os.environ.get("MLAO_NS", "4"))
        OTt = A.take([128, NS, 8, 128], BF16)
        xr = A.take([128, NS, D], F32)
        scs = make_scratch(A, NS)
        for k in range(0, 8, 2):
            kb.dma("pool", WO[:, k:k + 2, :],
                   mla_w_out.cv(mla_w_out.h[k * 128:(k + 2) * 128, :].rearrange("(a p) n -> p a n", p=128),
                                k * 128 * D, (k + 2) * 128 * D))

        def tile_chain(t):
            kb.dma("sp", OTt[:, t % NS, :, :],
                   oT_d.cv(oT_d.h[:, :, t * 128:(t + 1) * 128].rearrange("h v t -> v h t")))
            kb.dma("sp", xr[:, t % NS, :], src[t * 128:(t + 1) * 128, :])
            yield
            pys = [kb.pbank((2 * t) % 8, [128, 512]), kb.pbank((2 * t + 1) % 8, [128, 512])]
            for hf in range(2):
                for h in range(8):
                    kb.mm(pys[hf].all(), OTt[:, t % NS, h, :], WO[:, h, hf * 512:(hf + 1) * 512], h == 0, h == 7)
            yield
            yield from g_postnorm([p.all() for p in pys], xr[:, t % NS, :], scs[t % NS], G, dst, t)

        grp = int(os.environ.get("MLAO_G", "2"))
        run_tasks([(lambda t0=t0: multi_chain([tile_chain(t) for t in range(t0, min(NT, t0 + grp))]), []) for t0 in range(0, NT, grp)],
                  int(os.environ.get("MLAO_W", "2")))

    pre_ffn1a = "ffn1a" in plan and "kv" in plan and plan.index("kv") < plan.index("ffn1a")
    pre_ffn1b = "ffn1b" in plan and "mlaq" in plan
    pre_ffn0a = "ffn0a" in plan and "mods" in plan
    for p in plan:
        if p == "mods":
            if pre_ffn0a:
                ffn_load(0, 0)
            mods_setup(140032)
            for _ in range(12 if "gla" in plan else 40):
                mods_job()
        elif p == "ffn0a":
            phase_ffn("ffn0a", 0, 0, 0, 0, preloaded=pre_ffn0a)
        elif p == "ffn0b":
            phase_ffn("ffn0b", 0, 1, 6 * 1024, 2)
        elif p == "ffn1a":
            if pre_ffn1a:
                ffn_load(1, 0, lo=LIGHT_BASE)
            phase_ffn("ffn1a", 1, 0, 9216, 0, preloaded=pre_ffn1a)
        elif p == "ffn1b":
            phase_ffn("ffn1b", 1, 1, 9216 + 6 * 1024, 2, preloaded=pre_ffn1b)
        elif p == "gla":
            if os.environ.get("GLA_V1"):
                phase_gla()
            else:
                phase_gla2()
        elif p == "kv":
            if pre_ffn1a:
                ffn_load(1, 0, hi=LIGHT_BASE)
            phase_kv()
        elif p == "mlaq":
            if pre_ffn1b:
                ffn_load(1, 1, hi=LIGHT_BASE)
            phase_mlaq()
        elif p == "attn":
            if pre_ffn1b:
                ffn_load(1, 1, lo=LIGHT_BASE, hi=MLAO_BASE)
            phase_attn()
        elif p == "mlao":
            phase_mlao()
            if pre_ffn1b:
                ffn_load(1, 1, lo=MLAO_BASE)
        else:
            raise ValueError(p)
    while "mods" in plan and mods_state.get("jobs") and mods_state["next"] < len(mods_state["jobs"]):
        raise RuntimeError("mods jobs left unissued")
    if debug_out == "mods":
        pass
    S.finish()
    return kb


def host_inputs(inputs, b):
    f = lambda a: np.ascontiguousarray(np.asarray(a), dtype=np.float32)
    m = {
        "x": f(inputs["x"][b]),
        "c_fm": np.ascontiguousarray(f(inputs["c"][b]).reshape(8, 128).T),
        "pos": np.ascontiguousarray(np.asarray(inputs["positions"][b], dtype=np.int32).reshape(1, SEQ)),
        "consts": make_consts(),
        "cond_w": f(inputs["cond_w"]), "cond_b": f(inputs["cond_b"]), "norm_g": f(inputs["norm_g"]),
        "ffn_w_gu": f(inputs["ffn_w_gu"]), "ffn_w_down": f(inputs["ffn_w_down"]),
        "gla_w_in": f(inputs["gla_w_in"][0]), "gla_w_gate_up": f(inputs["gla_w_gate_up"][0]),
        "gla_b_gate": f(inputs["gla_b_gate"]).reshape(1, 512), "gla_g_out": f(inputs["gla_g_out"]).reshape(1, 256),
        "gla_w_out": f(inputs["gla_w_out"][0]),
        "kv_g_in": f(inputs["kv_g_in"]).reshape(1, D), "kv_cond_w": f(inputs["kv_cond_w"]),
        "kv_cond_b": f(inputs["kv_cond_b"]).reshape(1, 2 * D),
        "mla_w_kv_a": f(inputs["mla_w_kv_a"]), "mla_g_kv": f(inputs["mla_g_kv"]).reshape(1, 256),
        "mla_w_kv_b": f(inputs["mla_w_kv_b"]),
        "mla_w_dq": f(inputs["mla_w_dq"][0]), "mla_g_q": f(inputs["mla_g_q"]).reshape(1, 384),
        "mla_w_uq": f(inputs["mla_w_uq"][0]), "mla_w_out": f(inputs["mla_w_out"][0]),

    }
    return m


FULL_PLAN = ["mods", "ffn0a", "gla", "ffn0b", "kv", "ffn1a", "mlaq", "attn", "mlao", "ffn1b"]


def kernel(**inputs):
    kb = build(FULL_PLAN)
    with contextlib.ExitStack() as st:
        kb.S.emit(st)
        in_maps = [{k: v for k, v in host_inputs(inputs, b).items() if k in kb.din} for b in range(8)]
        res = run_bass_kernel_spmd(kb.nc, in_maps, core_ids=list(range(8)))
    return np.stack([np.asarray(r["y"], dtype=np.float32) for r in res.results], axis=0)
```

```python
import contextlib
import math
import os
import numpy as np
import concourse.bass as bass
import concourse.mybir as mybir
from concourse.bass_utils import run_bass_kernel_spmd

F32 = mybir.dt.float32
BF16 = mybir.dt.bfloat16
I32 = mybir.dt.int32
AF = mybir.ActivationFunctionType
ALU = mybir.AluOpType
AX = mybir.AxisListType
ESZ = {F32: 4, BF16: 2, I32: 4}

ENGS = ("pe", "act", "dve", "pool", "sp")
EIDX = {e: i for i, e in enumerate(ENGS)}
EPOCH = 16000
RING = 14

D = 1024
SEQ = 4096
NT = SEQ // 128
DFF = 2816
EPS = 1e-6
ARENA_BYTES = 212480
STRICT = os.environ.get("MK_STRICT", "1") == "1"


class Region:
    def __init__(self, size, bucket=2048):
        self.size = size
        self.B = bucket
        self.b = {}


class View:
    __slots__ = ("ap", "r", "lo", "hi")

    def __init__(self, ap, r, lo, hi):
        self.ap = ap; self.r = r; self.lo = lo; self.hi = hi


class Sub:
    def __init__(self, ap, region, off, esz, shape, track_from):
        self.ap = ap; self.r = region; self.off = off; self.esz = esz
        self.shape = tuple(shape); self.skip = track_from
        st = []; acc = 1
        for d in reversed(self.shape[track_from:]):
            st.append(acc); acc *= d
        self.strides = tuple(reversed(st))
        self.nbytes = acc * esz
        self.whole = False

    def __getitem__(self, idx):
        if self.whole:
            if not isinstance(idx, tuple):
                idx = (idx,)
            return View(self.ap[idx], self.r, 0, self.r.size)
        if not isinstance(idx, tuple):
            idx = (idx,)
        idx = tuple(idx) + (slice(None),) * (len(self.shape) - len(idx))
        lo = 0; hi = 0
        for d in range(self.skip, len(self.shape)):
            i = idx[d]; s = self.strides[d - self.skip]
            if isinstance(i, slice):
                a = 0 if i.start is None else i.start
                b = self.shape[d] if i.stop is None else i.stop
                assert i.step in (None, 1) and 0 <= a < b <= self.shape[d], (idx, self.shape)
            else:
                a = i; b = i + 1
                assert 0 <= a < self.shape[d], (idx, self.shape)
            lo += a * s; hi += (b - 1) * s
        return View(self.ap[idx], self.r, self.off + lo * self.esz, self.off + (hi + 1) * self.esz)

    def all(self):
        if self.whole:
            return View(self.ap, self.r, 0, self.r.size)
        return View(self.ap, self.r, self.off, self.off + self.nbytes)

    def cv(self, ap, lo_e=0, hi_e=None):
        hi_e = self.nbytes // self.esz if hi_e is None else hi_e
        return View(ap, self.r, self.off + lo_e * self.esz, self.off + hi_e * self.esz)


class Op:
    __slots__ = ("eng", "fn", "dma", "pos", "waits", "signal", "clock", "slot", "val", "sidx")


class Sched:
    def __init__(self, nc):
        self.nc = nc
        self.ops = []
        self.pos = {e: 0 for e in ENGS}
        self.known = {e: [0] * len(ENGS) for e in ENGS}
        self.known_dma = {e: {} for e in ENGS}
        self.ring_next = {e: 0 for e in ENGS}
        self.ring_last = {}
        self.last_op = {e: None for e in ENGS}
        self.n_waits = 0

    def _need(self, X, Y, raw):
        if Y is X:
            return
        E = X.eng
        if Y.dma:
            key = (Y.eng, Y.slot)
            if self.known_dma[E].get(key, 0) >= Y.val:
                return
            self.known_dma[E][key] = Y.val
            X.waits.append(Y)
            self._merge(E, Y.clock)
            return
        if Y.eng == E and not X.dma:
            if E == "pe" or (not raw and not STRICT):
                return
        pi = EIDX[Y.eng]
        if self.known[E][pi] >= Y.pos:
            return
        Y.signal = True
        X.waits.append(Y)
        self._merge(E, Y.clock)

    def _merge(self, E, clock):
        k = self.known[E]
        for i, v in enumerate(clock):
            if v > k[i]:
                k[i] = v

    def op(self, eng, fn, reads=(), writes=(), dma=False):
        X = Op()
        X.eng = eng; X.fn = fn; X.dma = dma; X.waits = []; X.signal = False
        X.slot = None; X.val = 0; X.sidx = 0
        self.pos[eng] += 1
        X.pos = self.pos[eng]
        deps = {}
        for v in reads:
            rb = v.r.b; B = v.r.B
            for bi in range(v.lo // B, (v.hi - 1) // B + 1):
                for r in rb.get(bi, ()):
                    if r[3] and r[0] < v.hi and v.lo < r[1]:
                        deps[id(r[2])] = (r[2], True)
        for v in writes:
            rb = v.r.b; B = v.r.B
            for bi in range(v.lo // B, (v.hi - 1) // B + 1):
                for r in rb.get(bi, ()):
                    if r[0] < v.hi and v.lo < r[1]:
                        k = id(r[2])
                        if k not in deps:
                            deps[k] = (r[2], r[3] and False)
        for Y, raw in sorted(deps.values(), key=lambda t: -t[0].pos):
            self._need(X, Y, raw)
        if dma:
            s = self.ring_next[eng]; self.ring_next[eng] = (s + 1) % RING
            prev = self.ring_last.get((eng, s))
            if prev is not None:
                self._need(X, prev, False)
                X.val = prev.val + 16
            else:
                X.val = 16
            X.slot = s
            self.ring_last[(eng, s)] = X
        ck = list(self.known[eng])
        if not dma:
            ck[EIDX[eng]] = X.pos
        X.clock = ck
        for v in reads:
            rb = v.r.b; B = v.r.B
            rec = (v.lo, v.hi, X, False)
            for bi in range(v.lo // B, (v.hi - 1) // B + 1):
                lst = rb.get(bi)
                if lst is None:
                    rb[bi] = [rec]; continue
                if not dma:
                    lst[:] = [r for r in lst if not ((not r[3]) and (not r[2].dma) and r[2].eng == eng
                                                     and v.lo <= r[0] and r[1] <= v.hi)]
                lst.append(rec)
        for v in writes:
            rb = v.r.b; B = v.r.B
            rec = (v.lo, v.hi, X, True)
            for bi in range(v.lo // B, (v.hi - 1) // B + 1):
                lst = rb.get(bi)
                if lst is None:
                    rb[bi] = [rec]; continue
                lst[:] = [r for r in lst if not (v.lo <= r[0] and r[1] <= v.hi)]
                lst.append(rec)
        self.ops.append(X)
        if not dma:
            self.last_op[eng] = X
        return X

    def finish(self):
        X = Op()
        X.eng = "sp"; X.fn = None; X.dma = False; X.waits = []; X.signal = False
        X.slot = None; X.val = 0; X.sidx = 0
        self.pos["sp"] += 1; X.pos = self.pos["sp"]
        for (e, s), Dm in self.ring_last.items():
            self._need(X, Dm, False)
        for e in ENGS:
            if e != "sp" and self.last_op[e] is not None:
                Y = self.last_op[e]
                if self.known["sp"][EIDX[e]] < Y.pos:
                    Y.signal = True; X.waits.append(Y)
        X.clock = list(self.known["sp"])
        self.ops.append(X)

    def emit(self, stack):
        nc = self.nc
        engobj = {"pe": nc.tensor, "act": nc.scalar, "dve": nc.vector, "pool": nc.gpsimd, "sp": nc.sync}
        nsig = {e: 0 for e in ENGS}
        for X in self.ops:
            if X.signal:
                nsig[X.eng] += 1
        sems = {}
        for e in ENGS:
            n = (nsig[e] + EPOCH - 1) // EPOCH
            sems[e] = [stack.enter_context(nc.semaphore(f"s_{e}_{i}")) for i in range(n)]
        rings = {}
        for (e, s) in self.ring_last:
            rings[(e, s)] = stack.enter_context(nc.semaphore(f"r_{e}_{s}"))
        cnt = {e: 0 for e in ENGS}
        nw = 0
        for X in self.ops:
            eo = engobj[X.eng]
            for Y in X.waits:
                if Y.dma:
                    eo.wait_ge(rings[(Y.eng, Y.slot)], Y.val)
                else:
                    n = Y.sidx
                    assert n > 0
                    eo.wait_ge(sems[Y.eng][(n - 1) // EPOCH], (n - 1) % EPOCH + 1)
                nw += 1
            if X.fn is None:
                continue
            ins = X.fn(eo)
            if X.dma:
                ins.then_inc(rings[(X.eng, X.slot)], 16)
            if X.signal:
                cnt[X.eng] += 1
                X.sidx = cnt[X.eng]
                ins.then_inc(sems[X.eng][(X.sidx - 1) // EPOCH], 1)
        self.n_waits = nw
        return nsig


class Alloc:
    def __init__(self, K, base=0):
        self.K = K; self.off = base

    def take(self, shape, dt):
        n = 1
        for d in shape[1:]:
            n *= d
        nb = n * ESZ[dt]
        off = (self.off + 31) // 32 * 32
        assert off + nb <= self.K.arena_limit, ("arena overflow", off + nb, self.K.arena_limit)
        self.off = off + nb
        return self.K.carve(off, shape, dt)


class K:
    def __init__(self):
        self.nc = bass.Bass("TRN2", target_bir_lowering=False)
        self.S = Sched(self.nc)
        self.arena_h = self.nc.alloc_sbuf_tensor("arena", [128, ARENA_BYTES // 2], BF16)
        self.arena_r = Region(ARENA_BYTES)
        self.arena_limit = ARENA_BYTES
        self.banks = []
        for i in range(8):
            h = self.nc.alloc_psum_tensor(f"bank{i}", [128, 512], F32)
            self.banks.append((h, Region(2048, 256)))
        self.dr = {}

    def carve(self, off, shape, dt, parts=128):
        esz = ESZ[dt]
        n = 1
        for d in shape[1:]:
            n *= d
        assert off % 2 == 0
        ap = self.arena_h[0:shape[0], off // 2: off // 2 + n * esz // 2]
        if dt != BF16:
            ap = ap.bitcast(dt)
        if len(shape) == 3:
            ap = ap.rearrange("p (a b) -> p a b", a=shape[1])
        elif len(shape) == 4:
            ap = ap.rearrange("p (a b c) -> p a b c", a=shape[1], b=shape[2])
        return Sub(ap, self.arena_r, off, esz, shape, 1)

    def pbank(self, i, shape, dt=F32, coff=0):
        h, r = self.banks[i]
        esz = ESZ[dt]
        n = 1
        for d in shape[1:]:
            n *= d
        ap = h[0:shape[0], coff // 4: coff // 4 + n * esz // 4]
        if dt != F32:
            ap = ap.bitcast(dt)
        if len(shape) == 3:
            ap = ap.rearrange("p (a b) -> p a b", a=shape[1])
        sb = Sub(ap, r, coff, esz, shape, 1)
        sb.whole = True
        return sb

    def dram(self, name, shape, dt, kind="Internal"):
        h = self.nc.dram_tensor(name, list(shape), dt, kind=kind)
        n = 1
        for d in shape:
            n *= d
        r = Region(n * ESZ[dt], max(4096, n * ESZ[dt] // 512))
        s = Sub(h[tuple(slice(None) for _ in shape)], r, 0, ESZ[dt], shape, 0)
        s.h = h
        self.dr[name] = s
        return s

    def mm(self, out, lhsT, rhs, start, stop):
        self.S.op("pe", lambda e: e.matmul(out.ap, lhsT=lhsT.ap, rhs=rhs.ap, start=start, stop=stop),
                  reads=[lhsT, rhs], writes=[out])

    def tr(self, out, in_, ident):
        self.S.op("pe", lambda e: e.transpose(out=out.ap, in_=in_.ap, identity=ident.ap),
                  reads=[in_, ident], writes=[out])

    def act(self, out, in_, func, bias=None, scale=None, accum=None):
        kw = {}
        rd = [in_]; wr = [out]
        if bias is not None:
            if isinstance(bias, View):
                kw["bias"] = bias.ap; rd.append(bias)
            else:
                kw["bias"] = bias
        if scale is not None:
            if isinstance(scale, View):
                kw["scale"] = scale.ap; rd.append(scale)
            else:
                kw["scale"] = scale
        if accum is not None:
            kw["accum_out"] = accum.ap; wr.append(accum)
        self.S.op("act", lambda e: e.activation(out=out.ap, in_=in_.ap, func=func, **kw), reads=rd, writes=wr)

    def tsc(self, eng, out, in0, s1, s2, op0, op1=None):
        rd = [in0]
        a1 = s1.ap if isinstance(s1, View) else s1
        a2 = s2.ap if isinstance(s2, View) else s2
        if isinstance(s1, View):
            rd.append(s1)
        if isinstance(s2, View):
            rd.append(s2)
        if op1 is None:
            self.S.op(eng, lambda e: e.tensor_scalar(out=out.ap, in0=in0.ap, scalar1=a1, scalar2=None, op0=op0),
                      reads=rd, writes=[out])
        else:
            self.S.op(eng, lambda e: e.tensor_scalar(out=out.ap, in0=in0.ap, scalar1=a1, scalar2=a2, op0=op0, op1=op1),
                      reads=rd, writes=[out])

    def stt(self, eng, out, in0, sc, in1, op0, op1):
        rd = [in0, in1]
        a = sc.ap if isinstance(sc, View) else sc
        if isinstance(sc, View):
            rd.append(sc)
        self.S.op(eng, lambda e: e.scalar_tensor_tensor(out=out.ap, in0=in0.ap, scalar=a, in1=in1.ap, op0=op0, op1=op1),
                  reads=rd, writes=[out])

    def tt(self, eng, out, in0, in1, op):
        self.S.op(eng, lambda e: e.tensor_tensor(out=out.ap, in0=in0.ap, in1=in1.ap, op=op),
                  reads=[in0, in1], writes=[out])

    def cp(self, eng, out, in_):
        if eng == "act":
            self.S.op("act", lambda e: e.copy(out=out.ap, in_=in_.ap), reads=[in_], writes=[out])
        else:
            self.S.op(eng, lambda e: e.tensor_copy(out=out.ap, in_=in_.ap), reads=[in_], writes=[out])

    def memset(self, eng, out, val):
        self.S.op(eng, lambda e: e.memset(out.ap, val), reads=[], writes=[out])

    def recip(self, out, in_):
        self.S.op("dve", lambda e: e.reciprocal(out=out.ap, in_=in_.ap), reads=[in_], writes=[out])

    def dma(self, eng, out, in_, slow=False):
        if slow:
            self.S.op(eng, lambda e: e.dma_start(out=out.ap, in_=in_.ap, allow_slow_non_contiguous=True),
                      reads=[in_], writes=[out], dma=True)
        else:
            self.S.op(eng, lambda e: e.dma_start(out=out.ap, in_=in_.ap), reads=[in_], writes=[out], dma=True)


def bcast_last(v, sub, n):
    ap = v.ap.unsqueeze(2).broadcast_to([v.ap.shape[0], v.ap.shape[1], n])
    return View(ap, v.r, v.lo, v.hi)


C_ID, C_M2, C_M2S, C_U2S, C_NEG, C_ONE, C_INVF = 0, 128, 256, 384, 512, 640, 768
NCONST = 776


def make_consts():
    c = np.zeros((128, NCONST), np.float32)
    c[:, C_ID:C_ID + 128] = np.eye(128, dtype=np.float32)
    j = np.arange(128)[:, None]; i = np.arange(128)[None, :]
    same = (j // 64) == (i // 64)
    c[:, C_M2:C_M2 + 128] = (same & (j <= i)).astype(np.float32)
    c[:, C_M2S:C_M2S + 128] = -(same & (j <= i)).astype(np.float32) / 16.0
    c[:, C_U2S:C_U2S + 128] = -(same & (j > i)).astype(np.float32) / 16.0
    c[:, C_NEG:C_NEG + 128] = np.where(j <= i, 0.0, -30000.0).astype(np.float32)
    c[:, C_ONE:C_ONE + 128] = 1.0
    invf = (np.float32(10000.0) ** (-np.arange(0, 64, 2, dtype=np.float32) / np.float32(64))).astype(np.float32)
    c[0:32, C_INVF] = invf; c[32:64, C_INVF] = invf
    return c


LIGHT_BASE = 73728
ATTN_BASE = 143744
MLAO_BASE = 108544
CONST_BYTES = 3200


def build(plan, debug_out=None):
    kb = K()
    S = kb.S
    nc = kb.nc
    kb.arena_limit = ARENA_BYTES - CONST_BYTES
    din = {}
    def inp(name, shape, dt=F32):
        din[name] = kb.dram(name, shape, dt, kind="ExternalInput")
        return din[name]
    x_in = inp("x", [SEQ, D])
    c_in = inp("c_fm", [128, 8])
    pos_in = inp("pos", [1, SEQ], I32)
    consts_in = inp("consts", [128, NCONST])
    cond_w = inp("cond_w", [2, D, 9 * D]); cond_b = inp("cond_b", [2, 9 * D])
    norm_g = inp("norm_g", [2, 3, 2, D])
    ffn_w_gu = inp("ffn_w_gu", [2, 2, D, 2 * DFF]); ffn_w_down = inp("ffn_w_down", [2, 2, DFF, D])
    gla_w_in = inp("gla_w_in", [D, 3088]); gla_w_gate_up = inp("gla_w_gate_up", [16, 512])
    gla_b_gate = inp("gla_b_gate", [1, 512]); gla_g_out = inp("gla_g_out", [1, 256])
    gla_w_out = inp("gla_w_out", [D, D])
    kv_g_in = inp("kv_g_in", [1, D]); kv_cond_w = inp("kv_cond_w", [D, 2 * D]); kv_cond_b = inp("kv_cond_b", [1, 2 * D])
    mla_w_kv_a = inp("mla_w_kv_a", [D, 320]); mla_g_kv = inp("mla_g_kv", [1, 256])
    mla_w_kv_b = inp("mla_w_kv_b", [256, 2048])
    mla_w_dq = inp("mla_w_dq", [D, 384]); mla_g_q = inp("mla_g_q", [1, 384])
    mla_w_uq = inp("mla_w_uq", [384, 1536]); mla_w_out = inp("mla_w_out", [D, D])
    mods_in = inp("mods_dbg", [1, 20480]) if "mods" not in plan else None
    xkv_in = inp("x_kv", [SEQ, D]) if ("kv" in plan and "ffn0b" not in plan) else None
    kb.din = din
    y_out = kb.dram("y", [SEQ, D], F32, kind="ExternalOutput")
    mods_d = kb.dram("mods_d", [1, 20480], F32)
    xs = [kb.dram(f"xs{i}", [SEQ, D], F32) for i in range(5)]
    qT_d = kb.dram("qT_d", [8, 192, SEQ], BF16)
    kT_d = kb.dram("kT_d", [8, 128, SEQ], BF16)
    krT_d = kb.dram("krT_d", [64, SEQ], BF16)
    v_d = kb.dram("v_d", [SEQ, D], BF16)
    oT_d = kb.dram("oT_d", [8, 128, SEQ], BF16)
    dbg = {}

    cbase = ARENA_BYTES - CONST_BYTES
    identf = kb.carve(cbase, [128, 128], F32)
    identb = kb.carve(cbase + 512, [128, 128], BF16)
    m2b = kb.carve(cbase + 768, [128, 128], BF16)
    negb = kb.carve(cbase + 1024, [128, 128], BF16)
    oneb = kb.carve(cbase + 1280, [128, 128], BF16)
    m2sf = kb.carve(cbase + 1536, [128, 128], F32)
    u2sf = kb.carve(cbase + 2048, [128, 128], F32)
    invf = kb.carve(cbase + 2560, [128, 8], F32)
    onef = kb.carve(cbase + 2592, [128, 128], F32)
    epsc = kb.carve(cbase + 3104, [128, 8], F32)
    mab = kb.carve(cbase + 3136, [128, 2], F32)
    kb.memset("dve", epsc.all(), EPS)
    cstage = kb.carve(0, [128, 640], F32)
    kb.dma("sp", identf.all(), consts_in[:, C_ID:C_ID + 128])
    kb.dma("sp", m2sf.all(), consts_in[:, C_M2S:C_M2S + 128])
    kb.dma("sp", u2sf.all(), consts_in[:, C_U2S:C_U2S + 128])
    kb.dma("sp", invf.all(), consts_in[:, C_INVF:C_INVF + 8])
    kb.dma("sp", onef.all(), consts_in[:, C_ONE:C_ONE + 128])
    kb.dma("sp", cstage[:, 0:128], consts_in[:, C_ID:C_ID + 128])
    kb.dma("sp", cstage[:, 128:256], consts_in[:, C_M2:C_M2 + 128])
    kb.dma("sp", cstage[:, 256:384], consts_in[:, C_NEG:C_NEG + 128])
    kb.dma("sp", cstage[:, 384:512], consts_in[:, C_ONE:C_ONE + 128])
    kb.cp("dve", identb.all(), cstage[:, 0:128])
    kb.cp("dve", m2b.all(), cstage[:, 128:256])
    kb.cp("dve", negb.all(), cstage[:, 256:384])
    kb.cp("dve", oneb.all(), cstage[:, 384:512])
    kb.cp("dve", mab[:, 0:1], cstage[:, 128 + 63:128 + 64])
    kb.cp("dve", mab[:, 1:2], cstage[:, 128 + 127:128 + 128])

    plan_s = ["mla" if p == "mlaq" else p for p in plan]
    stream_phases = [p for p in plan_s if p in ("ffn0a", "gla", "ffn0b", "ffn1a", "mla", "ffn1b")]
    route = {"ffn0a": (x_in, xs[0]), "gla": (xs[0], xs[1]), "ffn0b": (xs[1], xs[2]), "ffn1a": (xs[2], xs[3]),
             "mla": (xs[3], xs[4]), "ffn1b": (xs[4], y_out)}
    if stream_phases:
        route[stream_phases[0]] = (x_in, route[stream_phases[0]][1])
        route[stream_phases[-1]] = (route[stream_phases[-1]][0], y_out)
    kv_src = xs[2]
    if "kv" in plan and "ffn0b" not in plan:
        kv_src = xkv_in
    mods_src = mods_d if "mods" in plan else mods_in

    mods_state = {}

    def mods_setup(base):
        A = Alloc(kb, base)
        cfm = A.take([128, 8], F32)
        cact = A.take([128, 8], F32)
        wt = [A.take([128, 8, 512], BF16) for _ in range(2)]
        cactb = A.take([128, 8], BF16)
        brow = [A.take([1, 512], F32) for _ in range(3)]
        orow = [A.take([1, 512], F32) for _ in range(3)]
        kb.dma("sp", cfm.all(), c_in.all())
        kb.act(cact.all(), cfm.all(), AF.Silu)
        kb.cp("dve", cactb.all(), cact.all())
        jobs = []
        for n in range(18):
            jobs.append((cond_w, 0, cond_b, n, n * 512))
        for n in range(4):
            jobs.append((kv_cond_w, None, kv_cond_b, n, 18432 + n * 512))
        for n in range(18):
            jobs.append((cond_w, 1, cond_b, n, 9216 + n * 512))
        mods_state.update(cact=cactb, wt=wt, brow=brow, orow=orow, jobs=jobs, next=mods_state.get("next", 0))
        mods_state["next_load"] = mods_state["next"]
        mods_load(); mods_load()

    def mods_load():
        jl = mods_state.get("next_load", 0)
        jl = max(jl, mods_state["next"])
        if jl >= len(mods_state["jobs"]) or jl - mods_state["next"] >= 2:
            mods_state["next_load"] = jl
            return
        wd, l, bd, n, off = mods_state["jobs"][jl]
        w_ = mods_state["wt"][jl % 2]; b_ = mods_state["brow"][jl % 3]
        if l is None:
            src = wd.cv(wd.h[:, n * 512:(n + 1) * 512].rearrange("(k p) n -> p k n", p=128))
            bsrc = bd[0:1, n * 512:(n + 1) * 512]
        else:
            src = wd.cv(wd.h[l, :, n * 512:(n + 1) * 512].rearrange("(k p) n -> p k n", p=128),
                        l * D * 9216, (l + 1) * D * 9216)
            bsrc = bd[l:l + 1, n * 512:(n + 1) * 512]
        kb.dma("pool", w_.all(), src)
        kb.dma("sp", b_.all(), bsrc)
        mods_state["next_load"] = jl + 1

    def mods_job(pbanks=(0, 1)):
        ji = mods_state["next"]
        if ji >= len(mods_state["jobs"]):
            return False
        if mods_state.get("next_load", 0) <= ji:
            mods_state["next_load"] = ji
            mods_load()
        mods_state["next"] = ji + 1
        wd, l, bd, n, off = mods_state["jobs"][ji]
        w_ = mods_state["wt"][ji % 2]; b_ = mods_state["brow"][ji % 3]; o_ = mods_state["orow"][ji % 3]
        cact = mods_state["cact"]
        pm = kb.pbank(pbanks[ji % len(pbanks)], [128, 512])
        for k in range(8):
            kb.mm(pm[0:1, :], cact[:, k:k + 1], w_[:, k, :], k == 0, k == 7)
        kb.tt("dve", o_.all(), pm[0:1, :], b_.all(), ALU.add)
        kb.dma("sp", mods_d[0:1, off:off + 512], o_.all())
        mods_load()
        return True

    def load_mod_vectors(A, moff, g_pre_v, g_post_v, rw):
        AT = A.take([128, 8], F32); BT = A.take([128, 8], F32)
        tmp = A.take([128, 8], F32)
        G = None
        if g_pre_v is not None:
            kb.dma("sp", BT.all(), mods_src.cv(mods_src.h[0, moff:moff + 1024].rearrange("(k p) -> p k", p=128),
                                               moff, moff + 1024), slow=True)
            kb.dma("sp", tmp.all(), mods_src.cv(mods_src.h[0, moff + 1024:moff + 2048].rearrange("(k p) -> p k", p=128),
                                                moff + 1024, moff + 2048), slow=True)
            kb.dma("sp", AT.all(), g_pre_v, slow=True)
            kb.stt("dve", AT.all(), tmp.all(), 1.0, AT.all(), ALU.add, ALU.mult)
        if g_post_v is not None:
            G = A.take([128, 1024], F32)
            gt = A.take([128, 1024], F32)
            kb.dma("sp", G.all(), mods_src.cv(mods_src.h[0, moff + 2048:moff + 3072].partition_broadcast(128),
                                              moff + 2048, moff + 3072))
            kb.dma("sp", gt.all(), g_post_v)
            kb.stt("dve", G.all(), G.all(), float(rw), gt.all(), ALU.mult, ALU.mult)
            A.off -= 4096
        return AT, BT, G

    def rstd(out, ss, tmp, n):
        kb.act(tmp, ss, AF.Sqrt, bias=epsc[:, 0:1], scale=1.0 / n)
        kb.recip(out, tmp)

    def gvec_fm(t, idx):
        ap = t.h[idx].rearrange("(k p) -> p k", p=128)
        return t.cv(ap)

    def gvec_rep(t, idx, n=1024):
        ap = t.h[idx].partition_broadcast(128)
        return t.cv(ap)

    def make_scratch(A, n, with_xo=True):
        out = []
        for _ in range(n):
            d = dict(hb=A.take([128, D], BF16), tmpm=A.take([128, D], F32), junk=A.take([128, D], BF16),
                     stat=A.take([128, 16], F32))
            if with_xo:
                d["xo"] = A.take([128, D], F32)
            out.append(d)
        return out

    def g_prenorm(src, t, xt_v, sc, ptb, hT_dst, AT, BT):
        hb = sc["hb"]; stat = sc["stat"]; tmpm = sc["tmpm"]
        kb.dma("sp", xt_v, src[t * 128:(t + 1) * 128, :])
        yield
        kb.act(sc["junk"].all(), xt_v, AF.Square, accum=stat[:, 0:1])
        yield
        rstd(stat[:, 2:3], stat[:, 0:1], stat[:, 1:2], D)
        yield
        kb.act(hb.all(), xt_v, AF.Identity, scale=stat[:, 2:3])
        yield
        pT = kb.pbank(ptb, [128, 8, 128], BF16)
        for k in range(8):
            kb.tr(pT[:, k, :], hb[:, k * 128:(k + 1) * 128], identb.all())
        yield
        t3 = tmpm.cv(tmpm.ap.rearrange("p (a b) -> p a b", a=8))
        kb.tt("dve", t3, pT.all(), bcast_last(AT.all(), AT, 128), ALU.mult)
        yield
        kb.tt("pool", hT_dst, t3, bcast_last(BT.all(), BT, 128), ALU.add)
        yield

    def prenorm_tile(*a):
        for _ in g_prenorm(*a):
            pass

    def g_postnorm(py_views, xt_v, sc, G, dst, t):
        stat = sc["stat"]; xo = sc["xo"]; junk = sc["junk"]
        kb.act(junk[:, 0:512], py_views[0], AF.Square, accum=stat[:, 4:5])
        kb.act(junk[:, 512:1024], py_views[1], AF.Square, accum=stat[:, 5:6])
        yield
        kb.tt("dve", stat[:, 6:7], stat[:, 4:5], stat[:, 5:6], ALU.add)
        yield
        rstd(stat[:, 7:8], stat[:, 6:7], stat[:, 3:4], D)
        yield
        for hf in range(2):
            kb.stt("dve", xo[:, hf * 512:(hf + 1) * 512], py_views[hf], stat[:, 7:8], G[:, hf * 512:(hf + 1) * 512],
                   ALU.mult, ALU.mult)
        yield
        kb.tt("pool", xo.all(), xo.all(), xt_v, ALU.add)
        yield
        kb.dma("sp", dst[t * 128:(t + 1) * 128, :], xo.all())
        yield

    def postnorm_tile(*a):
        for _ in g_postnorm(*a):
            pass

    def multi_chain(gens):
        gens = list(gens)
        while gens:
            for g in list(gens):
                try:
                    next(g)
                except StopIteration:
                    gens.remove(g)
            yield

    def run_tasks(tasks, width=4):
        n = len(tasks)
        done = [False] * n; started = [False] * n; gens = [None] * n
        active = []
        first_unstarted = 0
        while True:
            while len(active) < width:
                cand = None
                for j in range(first_unstarted, n):
                    if not started[j] and all(done[d] for d in tasks[j][1]):
                        cand = j; break
                if cand is None:
                    break
                started[cand] = True; gens[cand] = tasks[cand][0](); active.append(cand)
                while first_unstarted < n and started[first_unstarted]:
                    first_unstarted += 1
            if not active:
                break
            for j in list(active):
                try:
                    next(gens[j])
                except StopIteration:
                    done[j] = True; active.remove(j)
        assert all(done), "task graph stuck"

    def ffn_layout():
        A = Alloc(kb)
        WGU = A.take([128, 8, 2 * DFF], BF16)
        WDN = A.take([128, 22, D], BF16)
        return A, WGU, WDN

    def ffn_load(layer, which, lo=0, hi=1 << 30):
        _, WGU, WDN = ffn_layout()
        wg = ffn_w_gu; wd = ffn_w_down
        for k in range(8):
            for hf in range(2):
                dv = WGU[:, k, hf * DFF:(hf + 1) * DFF]
                if dv.lo >= lo and dv.hi <= hi:
                    kb.dma("pool", dv, wg[layer, which, k * 128:(k + 1) * 128, hf * DFF:(hf + 1) * DFF])
        for m in range(0, 22, 2):
            base = ((layer * 2 + which) * DFF + m * 128) * D
            dv = WDN[:, m:m + 2, :]
            if dv.lo >= lo and dv.hi <= hi:
                kb.dma("pool", dv,
                       wd.cv(wd.h[layer, which, m * 128:(m + 2) * 128, :].rearrange("(a p) n -> p a n", p=128),
                             base, base + 256 * D))

    def phase_ffn(name, layer, which, moff, gidx, preloaded=False):
        src, dst = route[name]
        A, WGU, WDN = ffn_layout()
        if not preloaded:
            ffn_load(layer, which)
        AT, BT, G = load_mod_vectors(A, moff, gvec_fm(norm_g, (layer, gidx, 0)), gvec_rep(norm_g, (layer, gidx, 1)), 0.5)
        xt = A.take([128, 2, D], F32)
        xr = A.take([128, 1, D], F32)
        hT = A.take([128, 2, 8, 512], BF16)
        actb = A.take([128, 22, 512], BF16)
        sg = A.take([128, 2, 512], F32)
        sc = make_scratch(A, 1)[0]

        def pre(g):
            for j in range(4):
                t = g * 4 + j
                prenorm_tile(src, t, xt[:, t % 2, :], sc, 0, hT[:, g % 2, :, j * 128:(j + 1) * 128], AT, BT)

        pre(0)
        for g in range(SEQ // 512):
            hs = g % 2
            for m in range(22):
                pg = kb.pbank(1 + (m % 2), [128, 512])
                pu = kb.pbank(3 + (m % 2), [128, 512])
                for k in range(8):
                    kb.mm(pg.all(), WGU[:, k, m * 128:(m + 1) * 128], hT[:, hs, k, :], k == 0, k == 7)
                for k in range(8):
                    kb.mm(pu.all(), WGU[:, k, DFF + m * 128:DFF + (m + 1) * 128], hT[:, hs, k, :], k == 0, k == 7)
                kb.act(sg[:, m % 2, :], pg.all(), AF.Silu)
                kb.tt("dve", actb[:, m, :], sg[:, m % 2, :], pu.all(), ALU.mult)
            if g + 1 < SEQ // 512:
                pre(g + 1)
            for j in range(4):
                t = g * 4 + j
                kb.dma("sp", xr[:, 0, :], src[t * 128:(t + 1) * 128, :])
                pys = []
                for hf in range(2):
                    py = kb.pbank(5 + ((2 * t + hf) % 3), [128, 512])
                    for m in range(22):
                        kb.mm(py.all(), actb[:, m, j * 128:(j + 1) * 128], WDN[:, m, hf * 512:(hf + 1) * 512], m == 0, m == 21)
                    pys.append(py.all())
                postnorm_tile(pys, xr[:, 0, :], sc, G, dst, t)


    def phase_gla():
        src, dst = route["gla"]
        A = Alloc(kb)
        WIN = A.take([128, 8, 3088], BF16)
        WOUT = A.take([128, 8, D], BF16)
        WG = A.take([16, 512], F32)
        BG = A.take([1, 512], F32)
        GO = A.take([128, 256], F32)
        AT, BT, G = load_mod_vectors(A, 3072, gvec_fm(norm_g, (0, 1, 0)), gvec_rep(norm_g, (0, 1, 1)), 1.0)
        S32 = A.take([128, 4, 256], F32)
        Sb = A.take([128, 3, 4, 256], BF16)
        xt = A.take([128, 2, D], F32)
        hT = A.take([128, 2, 8, 128], BF16)
        scs = make_scratch(A, 2)
        gT_sb = A.take([16, 128], F32)
        e1 = A.take([128, 512], F32)
        lsp = A.take([128, 512], F32)
        eb = A.take([128, 4, 128], F32)
        enb = A.take([128, 4, 128], F32)
        qd = A.take([128, 4, 128], BF16)
        qdA = A.take([128, 4, 128], BF16)
        qdB = A.take([128, 4, 128], BF16)
        kd = A.take([128, 4, 128], BF16)
        eu = A.take([128, 512], F32)
        ku = A.take([128, 512], BF16)
        vb = A.take([128, D], BF16)
        vm = A.take([128, 4, 2, 256], BF16)
        attT = A.take([128, 4, 128], BF16)
        sr = A.take([128, D], F32)
        on = A.take([128, D], F32)
        og = A.take([128, D], BF16)
        ogT = A.take([128, 8, 128], BF16)
        if "mods" in plan:
            mods_setup((A.off + 63) // 64 * 64)
        for k in range(8):
            kb.dma("pool", WIN[:, k, :], gla_w_in[k * 128:(k + 1) * 128, :])
        for k in range(0, 8, 2):
            kb.dma("pool", WOUT[:, k:k + 2, :],
                   gla_w_out.cv(gla_w_out.h[k * 128:(k + 2) * 128, :].rearrange("(a p) n -> p a n", p=128),
                                k * 128 * D, (k + 2) * 128 * D))
        kb.dma("sp", WG.all(), gla_w_gate_up.all())
        kb.dma("sp", BG.all(), gla_b_gate.all())
        kb.dma("sp", GO.all(), gla_g_out.cv(gla_g_out.h[0].partition_broadcast(128)))
        kb.memset("dve", S32.all(), 0.0)
        kb.memset("dve", Sb[:, 0, :, :], 0.0)
        kb.memset("pool", qdA.all(), 0.0)
        kb.memset("pool", qdB.all(), 0.0)
        qsc = 128.0 ** -0.5
        updb = [(1, 0), (1, 1024), (2, 0), (2, 1024), (3, 0), (3, 1024), (6, 0), (6, 1024)]
        stage = int(os.environ.get("GLA_STAGE", "99"))
        for t in range(int(os.environ.get("GLA_NT", str(NT)))):
            n = 2 * t
            hTt = lambda k: hT[:, t % 2, k, :]
            sc = scs[t % 2]; stat = sc["stat"]; junk = sc["junk"]
            prenorm_tile(src, t, xt[:, t % 2, :], sc, 0, hT[:, t % 2, :, :], AT, BT)
            qT_ps = kb.pbank(1, [128, 4, 128]); kT_ps = kb.pbank(2, [128, 4, 128])
            ktok_ps = kb.pbank(3, [128, 512])
            gT_ps = kb.pbank(7, [128, 128])
            for h in range(4):
                for k in range(8):
                    kb.mm(qT_ps[:, h, :], WIN[:, k, h * 128:(h + 1) * 128], hTt(k), k == 0, k == 7)
            for h in range(4):
                for k in range(8):
                    kb.mm(kT_ps[:, h, :], WIN[:, k, 512 + h * 128:512 + (h + 1) * 128], hTt(k), k == 0, k == 7)
            for k in range(8):
                kb.mm(ktok_ps.all(), hTt(k), WIN[:, k, 512:1024], k == 0, k == 7)
            for k in range(8):
                kb.mm(gT_ps[0:16, :], WIN[:, k, 2048:2064], hTt(k), k == 0, k == 7)
            v_ps = [kb.pbank(4, [128, 512]), kb.pbank(5, [128, 512])]
            for nh in range(2):
                for k in range(8):
                    kb.mm(v_ps[nh].all(), hTt(k), WIN[:, k, 1024 + nh * 512:1024 + (nh + 1) * 512], k == 0, k == 7)
            kb.cp("act", vb[:, 0:512], v_ps[0].all())
            kb.cp("act", vb[:, 512:1024], v_ps[1].all())
            for nh in range(2):
                for c in range(2):
                    dstv = vm.cv(vm.ap[:, 2 * nh:2 * nh + 2, c, :], 2 * nh * 512, (2 * nh + 2) * 512)
                    srcv = View(v_ps[nh].ap.rearrange("p (h e) -> p h e", h=2), v_ps[nh].r, 0, 2048)
                    kb.act(dstv, srcv, AF.Identity, scale=mab[:, c:c + 1])
            if stage < 2:
                kb.dma("sp", dst[t * 128:(t + 1) * 128, :], xt[:, t % 2, :]); continue
            kb.cp("dve", gT_sb.all(), gT_ps[0:16, :])
            z_ps = kb.pbank(6, [128, 512])
            kb.mm(z_ps.all(), gT_sb.all(), WG.all(), True, False)
            kb.mm(z_ps.all(), onef[0:1, 0:128], BG.all(), False, True)
            kb.act(e1.all(), z_ps.all(), AF.Exp, scale=-1.0)
            kb.act(lsp.all(), e1.all(), AF.Ln, bias=onef[:, 0:1])
            if stage < 3:
                kb.dma("sp", dst[t * 128:(t + 1) * 128, :], xt[:, t % 2, :]); continue
            r_ps = [kb.pbank(4, [128, 512]), kb.pbank(5, [128, 512])]
            for nh in range(2):
                for k in range(8):
                    kb.mm(r_ps[nh].all(), hTt(k), WIN[:, k, 2064 + nh * 512:2064 + (nh + 1) * 512], k == 0, k == 7)
            kb.act(sr[:, 0:512], r_ps[0].all(), AF.Silu)
            kb.act(sr[:, 512:1024], r_ps[1].all(), AF.Silu)
            if stage < 4:
                kb.dma("sp", dst[t * 128:(t + 1) * 128, :], xt[:, t % 2, :]); continue
            bT_ps = kb.pbank(6, [128, 4, 128])
            U_ps = kb.pbank(7, [128, 512])
            for h in range(4):
                kb.mm(bT_ps[:, h, :], lsp[:, h * 128:(h + 1) * 128], m2sf.all(), True, True)
            kb.mm(U_ps.all(), u2sf.all(), lsp.all(), True, True)
            kb.act(eb.all(), bT_ps.all(), AF.Exp)
            kb.act(enb.all(), bT_ps.all(), AF.Exp, scale=-1.0)
            kb.act(eu.all(), U_ps.all(), AF.Exp)
            kb.stt("dve", qd.all(), qT_ps.all(), qsc, eb.all(), ALU.mult, ALU.mult)
            kb.tt("dve", kd.all(), kT_ps.all(), enb.all(), ALU.mult)
            kb.cp("pool", qdA[:, :, 0:64], qd[:, :, 0:64])
            kb.cp("pool", qdB[:, :, 64:128], qd[:, :, 64:128])
            kb.tt("dve", ku.all(), ktok_ps.all(), eu.all(), ALU.mult)
            if stage < 5:
                kb.dma("sp", dst[t * 128:(t + 1) * 128, :], xt[:, t % 2, :]); continue
            sub = int(os.environ.get("GLA_SUB", "9"))
            att_ps = kb.pbank(0, [128, 4, 128])
            for h in range(4):
                kb.mm(att_ps[:, h, :], kd[:, h, :], qd[:, h, :], True, True)
            upd = []
            for h in range(4):
                u_ps = kb.pbank((1, 2, 3, 6)[h], [128, 2, 256])
                if sub >= 2:
                    kb.mm(u_ps.all(), ku[:, h * 128:(h + 1) * 128], vm[:, h, :, :], True, True)
                upd.append(u_ps[:, 0, :]); upd.append(u_ps[:, 1, :])
            if sub >= 4:
                for h in range(4):
                    kb.tt("dve", attT[:, h, :], att_ps[:, h, :], m2b.all(), ALU.mult)
            o_ps = [kb.pbank((4, 5, 7, 0)[h], [128, 256]) for h in range(4)]
            if sub >= 5:
                for h in range(4):
                    kb.mm(o_ps[h].all(), attT[:, h, :], vb[:, h * 256:(h + 1) * 256], True, False)
                    kb.mm(o_ps[h].all(), qdA[:, h, :], Sb[:, n % 3, h, :], False, sub < 6)
            if sub >= 6:
                for h in range(4):
                    dA = eb[:, h, 63:64]; dB = eb[:, h, 127:128]
                    kb.stt("dve", Sb[:, (n + 1) % 3, h, :], S32[:, h, :], dA, upd[2 * h], ALU.mult, ALU.add)
                    kb.stt("dve", S32[:, h, :], S32[:, h, :], dA, upd[2 * h], ALU.mult, ALU.add)
                for h in range(4):
                    kb.mm(o_ps[h].all(), qdB[:, h, :], Sb[:, (n + 1) % 3, h, :], False, True)
                for h in range(4):
                    dB = eb[:, h, 127:128]
                    kb.stt("dve", Sb[:, (n + 2) % 3, h, :], S32[:, h, :], dB, upd[2 * h + 1], ALU.mult, ALU.add)
                    kb.stt("dve", S32[:, h, :], S32[:, h, :], dB, upd[2 * h + 1], ALU.mult, ALU.add)
            if stage < 6:
                kb.dma("sp", dst[t * 128:(t + 1) * 128, :], xt[:, t % 2, :]); continue
            for h in range(4):
                kb.act(junk[:, h * 256:(h + 1) * 256], o_ps[h].all(), AF.Square, accum=stat[:, 8 + h:9 + h])
            kb.act(stat[:, 4:8], stat[:, 8:12], AF.Sqrt, bias=epsc[:, 0:1], scale=1.0 / 256)
            kb.recip(stat[:, 12:16], stat[:, 4:8])
            for h in range(4):
                kb.stt("dve", on[:, h * 256:(h + 1) * 256], o_ps[h].all(), stat[:, 12 + h:13 + h], GO.all(), ALU.mult, ALU.mult)
            kb.tt("pool", og.all(), on.all(), sr.all(), ALU.mult)
            pT = kb.pbank(6, [128, 8, 128], BF16)
            for k in range(8):
                kb.tr(pT[:, k, :], og[:, k * 128:(k + 1) * 128], identb.all())
            kb.cp("act", ogT.all(), pT.all())
            pys = [kb.pbank(1, [128, 512]), kb.pbank(2, [128, 512])]
            for hf in range(2):
                for k in range(8):
                    kb.mm(pys[hf].all(), ogT[:, k, :], WOUT[:, k, hf * 512:(hf + 1) * 512], k == 0, k == 7)
            postnorm_tile([p.all() for p in pys], xt[:, t % 2, :], sc, G, dst, t)
            if "mods" in plan:
                mods_job((0,))


    def phase_gla2():
        src, dst = route["gla"]
        A = Alloc(kb)
        WIN = A.take([128, 8, 3088], BF16)
        WOUT = A.take([128, 8, D], BF16)
        WG = A.take([16, 512], F32)
        BG = A.take([1, 512], F32)
        GO = A.take([128, 256], F32)
        AT, BT, G = load_mod_vectors(A, 3072, gvec_fm(norm_g, (0, 1, 0)), gvec_rep(norm_g, (0, 1, 1)), 1.0)
        S32 = A.take([128, 4, 256], F32)
        Sb = A.take([128, 3, 4, 256], BF16)
        xt = A.take([128, 2, D], F32)
        hT = A.take([128, 2, 8, 128], BF16)
        scA = make_scratch(A, 2, with_xo=False)
        scB = [dict(xo=A.take([128, D], F32), junk=A.take([128, D], BF16), stat=A.take([128, 16], F32)) for _ in range(2)]
        qd = [A.take([128, 4, 128], BF16) for _ in range(2)]
        qdA = [A.take([128, 4, 128], BF16) for _ in range(2)]
        qdB = [A.take([128, 4, 128], BF16) for _ in range(2)]
        kd = [A.take([128, 4, 128], BF16) for _ in range(2)]
        ku = [A.take([128, 512], BF16) for _ in range(2)]
        vb = [A.take([128, D], BF16) for _ in range(2)]
        vm = [A.take([128, 4, 2, 256], BF16) for _ in range(2)]
        eb = [A.take([128, 4, 128], F32) for _ in range(2)]
        sr = [A.take([128, 4, 256], F32) for _ in range(2)]
        gT_sb = A.take([16, 128], F32)
        lsp = A.take([128, 512], F32)
        enb = A.take([128, 4, 128], F32)
        eu = A.take([128, 512], F32)
        attT = A.take([128, 4, 128], BF16)
        og = A.take([128, D], BF16)
        ogT = A.take([128, 8, 128], BF16)
        if "mods" in plan:
            mods_setup((A.off + 63) // 64 * 64)
        for k in range(8):
            kb.dma("pool", WIN[:, k, :], gla_w_in[k * 128:(k + 1) * 128, :])
        for k in range(0, 8, 2):
            kb.dma("pool", WOUT[:, k:k + 2, :],
                   gla_w_out.cv(gla_w_out.h[k * 128:(k + 2) * 128, :].rearrange("(a p) n -> p a n", p=128),
                                k * 128 * D, (k + 2) * 128 * D))
        kb.dma("sp", WG.all(), gla_w_gate_up.all())
        kb.dma("sp", BG.all(), gla_b_gate.all())
        kb.dma("sp", GO.all(), gla_g_out.cv(gla_g_out.h[0].partition_broadcast(128)))
        kb.memset("dve", S32.all(), 0.0)
        kb.memset("dve", Sb[:, 0, :, :], 0.0)
        for p in range(2):
            kb.memset("pool", qdA[p].all(), 0.0)
            kb.memset("pool", qdB[p].all(), 0.0)
        qsc = 128.0 ** -0.5
        ntl = int(os.environ.get("GLA_NT", str(NT)))

        def chain_a(t):
            p = t % 2
            hTt = lambda k: hT[:, p, k, :]
            yield from g_prenorm(src, t, xt[:, p, :], scA[p], 0, hT[:, p, :, :], AT, BT)
            qT_ps = kb.pbank(1, [128, 4, 128]); kT_ps = kb.pbank(2, [128, 4, 128])
            ktok_ps = kb.pbank(3, [128, 512])
            gT_ps = kb.pbank(4, [128, 128])
            for h in range(4):
                for k in range(8):
                    kb.mm(qT_ps[:, h, :], WIN[:, k, h * 128:(h + 1) * 128], hTt(k), k == 0, k == 7)
            yield
            for h in range(4):
                for k in range(8):
                    kb.mm(kT_ps[:, h, :], WIN[:, k, 512 + h * 128:512 + (h + 1) * 128], hTt(k), k == 0, k == 7)
            yield
            for k in range(8):
                kb.mm(gT_ps[0:16, :], WIN[:, k, 2048:2064], hTt(k), k == 0, k == 7)
            for k in range(8):
                kb.mm(ktok_ps.all(), hTt(k), WIN[:, k, 512:1024], k == 0, k == 7)
            yield
            kb.cp("dve", gT_sb.all(), gT_ps[0:16, :])
            sbank = [0, 4]
            v0 = kb.pbank(0, [128, 512])
            for k in range(8):
                kb.mm(v0.all(), hTt(k), WIN[:, k, 1024:1536], k == 0, k == 7)
            yield
            z_ps = kb.pbank(4, [128, 512])
            kb.mm(z_ps.all(), gT_sb.all(), WG.all(), True, False)
            kb.mm(z_ps.all(), onef[0:1, 0:128], BG.all(), False, True)
            yield

            def v_evac(v_ps, nh):
                kb.cp("act", vb[p][:, nh * 512:(nh + 1) * 512], v_ps.all())
                for c in range(2):
                    dstv = vm[p].cv(vm[p].ap[:, 2 * nh:2 * nh + 2, c, :], 2 * nh * 512, (2 * nh + 2) * 512)
                    srcv = View(v_ps.ap.rearrange("p (h e) -> p h e", h=2), v_ps.r, 0, 2048)
                    kb.act(dstv, srcv, AF.Identity, scale=mab[:, c:c + 1])
            v_evac(v0, 0)
            yield
            kb.act(lsp.all(), z_ps.all(), AF.Exp, scale=-1.0)
            yield
            v1 = kb.pbank(0, [128, 512])
            for k in range(8):
                kb.mm(v1.all(), hTt(k), WIN[:, k, 1536:2048], k == 0, k == 7)
            kb.act(lsp.all(), lsp.all(), AF.Ln, bias=onef[:, 0:1])
            yield
            v_evac(v1, 1)
            yield
            bT_ps = kb.pbank(4, [128, 4, 128])
            for h in range(4):
                kb.mm(bT_ps[:, h, :], lsp[:, h * 128:(h + 1) * 128], m2sf.all(), True, True)
            yield
            U_ps = kb.pbank(0, [128, 512])
            kb.mm(U_ps.all(), u2sf.all(), lsp.all(), True, True)
            yield
            kb.act(eb[p].all(), bT_ps.all(), AF.Exp)
            kb.act(enb.all(), bT_ps.all(), AF.Exp, scale=-1.0)
            yield
            kb.act(eu.all(), U_ps.all(), AF.Exp)
            yield
            r0 = kb.pbank(4, [128, 512]); r1 = kb.pbank(0, [128, 512])
            for k in range(8):
                kb.mm(r0.all(), hTt(k), WIN[:, k, 2064:2576], k == 0, k == 7)
            yield
            kb.stt("dve", qd[p].all(), qT_ps.all(), qsc, eb[p].all(), ALU.mult, ALU.mult)
            kb.tt("dve", kd[p].all(), kT_ps.all(), enb.all(), ALU.mult)
            yield
            for k in range(8):
                kb.mm(r1.all(), hTt(k), WIN[:, k, 2576:3088], k == 0, k == 7)
            kb.tt("dve", ku[p].all(), ktok_ps.all(), eu.all(), ALU.mult)
            yield
            kb.cp("pool", qdA[p][:, :, 0:64], qd[p][:, :, 0:64])
            kb.cp("pool", qdB[p][:, :, 64:128], qd[p][:, :, 64:128])
            srf = sr[p].cv(sr[p].ap.rearrange("p h e -> p (h e)"))
            kb.act(View(srf.ap[:, 0:512], srf.r, srf.lo, srf.hi), r0.all(), AF.Silu)
            yield
            kb.act(View(srf.ap[:, 512:1024], srf.r, srf.lo, srf.hi), r1.all(), AF.Silu)
            yield
            gob = View(GO.all().ap.unsqueeze(1).broadcast_to([128, 4, 256]), GO.r, GO.off, GO.off + GO.nbytes)
            kb.tt("pool", sr[p].all(), sr[p].all(), gob, ALU.mult)
            yield

        def chain_b(t):
            p = t % 2
            n = 2 * t
            sc = scB[p]; stat = sc["stat"]; junk = sc["junk"]
            att_ps = kb.pbank(5, [128, 4, 128])
            for h in range(4):
                kb.mm(att_ps[:, h, :], kd[p][:, h, :], qd[p][:, h, :], True, True)
            yield
            for h in range(4):
                kb.tt("dve", attT[:, h, :], att_ps[:, h, :], m2b.all(), ALU.mult)
            yield
            for h in range(4):
                u_ps = kb.pbank(6, [128, 2, 256])
                o_ps = kb.pbank(5 if h % 2 == 0 else 7, [128, 256])
                kb.mm(u_ps.all(), ku[p][:, h * 128:(h + 1) * 128], vm[p][:, h, :, :], True, True)
                kb.mm(o_ps.all(), attT[:, h, :], vb[p][:, h * 256:(h + 1) * 256], True, False)
                kb.mm(o_ps.all(), qdA[p][:, h, :], Sb[:, n % 3, h, :], False, False)
                yield
                dA = eb[p][:, h, 63:64]
                kb.stt("dve", Sb[:, (n + 1) % 3, h, :], S32[:, h, :], dA, u_ps[:, 0, :], ALU.mult, ALU.add)
                kb.stt("dve", S32[:, h, :], S32[:, h, :], dA, u_ps[:, 0, :], ALU.mult, ALU.add)
                yield
                kb.mm(o_ps.all(), qdB[p][:, h, :], Sb[:, (n + 1) % 3, h, :], False, True)
                yield
                dB = eb[p][:, h, 127:128]
                kb.stt("dve", Sb[:, (n + 2) % 3, h, :], S32[:, h, :], dB, u_ps[:, 1, :], ALU.mult, ALU.add)
                kb.stt("dve", S32[:, h, :], S32[:, h, :], dB, u_ps[:, 1, :], ALU.mult, ALU.add)
                kb.act(junk[:, h * 256:(h + 1) * 256], o_ps.all(), AF.Square, accum=stat[:, 8 + h:9 + h])
                yield
                kb.act(stat[:, 4 + h:5 + h], stat[:, 8 + h:9 + h], AF.Sqrt, bias=epsc[:, 0:1], scale=1.0 / 256)
                yield
                kb.recip(stat[:, 12 + h:13 + h], stat[:, 4 + h:5 + h])
                yield
                kb.stt("dve", og[:, h * 256:(h + 1) * 256], o_ps.all(), stat[:, 12 + h:13 + h], sr[p][:, h, :],
                       ALU.mult, ALU.mult)
                yield
            pT = kb.pbank(6, [128, 8, 128], BF16)
            for k in range(8):
                kb.tr(pT[:, k, :], og[:, k * 128:(k + 1) * 128], identb.all())
            yield
            kb.cp("act", ogT.all(), pT.all())
            yield
            pys = [kb.pbank(5, [128, 512]), kb.pbank(7, [128, 512])]
            for hf in range(2):
                for k in range(8):
                    kb.mm(pys[hf].all(), ogT[:, k, :], WOUT[:, k, hf * 512:(hf + 1) * 512], k == 0, k == 7)
            yield
            yield from g_postnorm([q.all() for q in pys], xt[:, p, :], sc, G, dst, t)
            if "mods" in plan:
                mods_job((6,))
                yield

        tasks = []
        idx_a = {}; idx_b = {}
        for t in range(ntl):
            if t == 0:
                idx_a[0] = len(tasks); tasks.append((lambda: chain_a(0), []))
            if t + 1 < ntl:
                deps = [idx_a[t]] + ([idx_b[t - 1]] if t - 1 >= 0 else [])
                idx_a[t + 1] = len(tasks); tasks.append((lambda tt=t + 1: chain_a(tt), deps))
            deps = [idx_a[t]] + ([idx_b[t - 1]] if t >= 1 else [])
            idx_b[t] = len(tasks); tasks.append((lambda tt=t: chain_b(tt), deps))
        run_tasks(tasks, 2)
        if "mods" in plan:
            while mods_job((6,)):
                pass

    def g_rope(cols, scale, R, par):
        c0, c1 = cols
        posi = R["posi"]; posf = R["posf"]; u = R["u"]; w = R["w"]; wi = R["wi"]; wf = R["wf"]; g1 = R["g1"]
        kb.dma("sp", posi.all(), pos_in.cv(pos_in.h[0, c0:c1].partition_broadcast(64), c0, c1))
        yield
        kb.cp("dve", posf.all(), posi.all())
        yield
        kb.tsc("dve", u.all(), posf.all(), invf[0:64, 0:1], 1.0 / (2.0 * math.pi), ALU.mult, ALU.mult)
        yield
        for tab, shift in ((R["sinT"][par], 0.0), (R["cosT"][par], 0.25)):
            kb.tsc("dve", w.all(), u.all(), shift, None, ALU.add)
            yield
            kb.cp("dve", wi.all(), w.all())
            yield
            kb.cp("dve", wf.all(), wi.all())
            yield
            kb.tt("dve", w.all(), w.all(), wf.all(), ALU.subtract)
            yield
            kb.tsc("dve", g1.all(), w.all(), 0.5, None, ALU.is_gt)
            yield
            kb.tt("dve", w.all(), w.all(), g1.all(), ALU.subtract)
            yield
            kb.tsc("dve", g1.all(), w.all(), -0.5, None, ALU.is_lt)
            yield
            kb.tt("dve", w.all(), w.all(), g1.all(), ALU.add)
            yield
            kb.act(tab.all(), w.all(), AF.Sin, scale=2.0 * math.pi)
            yield
            if scale != 1.0:
                kb.tsc("dve", tab.all(), tab.all(), float(scale), None, ALU.mult)
                yield

    def rope_alloc(A):
        return dict(cosT=[A.take([64, 512], F32) for _ in range(2)], sinT=[A.take([64, 512], F32) for _ in range(2)],
                    posi=A.take([64, 512], I32), posf=A.take([64, 512], F32), u=A.take([64, 512], F32),
                    w=A.take([64, 512], F32), wi=A.take([64, 512], I32), wf=A.take([64, 512], F32),
                    g1=A.take([64, 512], F32))

    def phase_kv():
        A = Alloc(kb, LIGHT_BASE)
        WKA = A.take([128, 8, 320], BF16)
        WKR = A.take([128, 8, 64], BF16)
        WKB = A.take([128, 2, 2048], BF16)
        WV = A.take([128, 2, 1024], BF16)
        GKV = A.take([128, 256], F32)
        AT, BT, _ = load_mod_vectors(A, 18432, gvec_fm(kv_g_in, 0), None, 0.0)
        R = rope_alloc(A)
        NS = 4
        xt = A.take([128, NS, D], F32)
        hT = A.take([128, 2, 8, 512], BF16)
        scs = make_scratch(A, NS, with_xo=False)
        cknN = [A.take([128, 256], BF16) for _ in range(NS)]
        ckvT = A.take([128, 2, 2, 512], BF16)
        vsb = A.take([128, NS, D], BF16)
        t1 = A.take([64, 512], F32); t2 = A.take([64, 512], F32)
        krs = A.take([64, 2, 512], BF16)
        ksb = A.take([128, 2, 512], BF16)
        for k in range(0, 8, 4):
            kb.dma("pool", WKA[:, k:k + 4, :],
                   mla_w_kv_a.cv(mla_w_kv_a.h[k * 128:(k + 4) * 128, :].rearrange("(a p) n -> p a n", p=128),
                                 k * 128 * 320, (k + 4) * 128 * 320))
        kb.dma("pool", WKB.all(), mla_w_kv_b.cv(mla_w_kv_b.h[:, :].rearrange("(a p) n -> p a n", p=128)))
        kb.dma("sp", GKV.all(), mla_g_kv.cv(mla_g_kv.h[0].partition_broadcast(128)))
        kb.S.op("act", lambda e: e.mul(out=WKR[:, :, 0:32].ap, in_=WKA[:, :, 288:320].ap, mul=-1.0),
                reads=[WKA[:, :, 288:320]], writes=[WKR[:, :, 0:32]])
        kb.cp("act", WKR[:, :, 32:64], WKA[:, :, 256:288])
        for c in range(2):
            srcv = WKB.cv(WKB.ap[:, c, :].rearrange("p (h e) -> p h e", h=8)[:, :, 128:256], c * 2048, (c + 1) * 2048)
            dstv = WV.cv(WV.ap[:, c, :].rearrange("p (h e) -> p h e", h=8), c * 1024, (c + 1) * 1024)
            kb.cp("dve", dstv, srcv)

        def tile_chain(t):
            bq, j = divmod(t, 4)
            hs = bq % 2
            jc = slice(j * 128, (j + 1) * 128)
            sc = scs[t % NS]; stat = sc["stat"]; junk = sc["junk"]; ckn = cknN[t % NS]
            yield from g_prenorm(kv_src, t, xt[:, t % NS, :], sc, t % 4, hT[:, hs, :, jc], AT, BT)
            ck_ps = kb.pbank(t % 4, [128, 256])
            for k in range(8):
                kb.mm(ck_ps.all(), hT[:, hs, k, jc], WKA[:, k, 0:256], k == 0, k == 7)
            yield
            kb.act(junk[:, 0:256], ck_ps.all(), AF.Square, accum=stat[:, 8:9])
            yield
            rstd(stat[:, 10:11], stat[:, 8:9], stat[:, 9:10], 256)
            yield
            kb.stt("dve", ckn.all(), ck_ps.all(), stat[:, 10:11], GKV.all(), ALU.mult, ALU.mult)
            yield
            pT2 = kb.pbank(t % 4, [128, 2, 128], BF16)
            for c in range(2):
                kb.tr(pT2[:, c, :], ckn[:, c * 128:(c + 1) * 128], identb.all())
            yield
            kb.cp("act", ckvT[:, hs, :, jc], pT2.all())
            yield
            for nh in range(2):
                v_ps = kb.pbank(t % 4, [128, 512])
                for c in range(2):
                    kb.mm(v_ps.all(), ckvT[:, hs, c, jc], WV[:, c, nh * 512:(nh + 1) * 512], c == 0, c == 1)
                yield
                kb.cp("act" if nh == 0 else "dve", vsb[:, t % NS, nh * 512:(nh + 1) * 512], v_ps.all())
                yield
            kb.dma("sp", v_d[t * 128:(t + 1) * 128, :], vsb[:, t % NS, :])
            yield

        def block_chain(bq):
            hs = bq % 2
            cols = (bq * 512, (bq + 1) * 512)
            kp_ps = kb.pbank(4, [64, 512]); kr_ps = kb.pbank(5, [64, 512])
            for k in range(8):
                kb.mm(kp_ps.all(), WKA[:, k, 256:320], hT[:, hs, k, :], k == 0, k == 7)
            for k in range(8):
                kb.mm(kr_ps.all(), WKR[:, k, :], hT[:, hs, k, :], k == 0, k == 7)
            yield
            kb.tt("dve", t1.all(), kp_ps.all(), R["cosT"][hs].all(), ALU.mult)
            kb.tt("dve", t2.all(), kr_ps.all(), R["sinT"][hs].all(), ALU.mult)
            yield
            kb.tt("pool", krs[:, hs, :], t1.all(), t2.all(), ALU.add)
            yield
            kb.dma("sp", krT_d[:, cols[0]:cols[1]], krs[:, hs, :])
            yield
            for h in range(8):
                kn_ps = kb.pbank((6, 7)[h % 2], [128, 512])
                for c in range(2):
                    kb.mm(kn_ps.all(), WKB[:, c, h * 256:h * 256 + 128], ckvT[:, hs, c, :], c == 0, c == 1)
                yield
                kb.cp("act" if h % 2 == 0 else "dve", ksb[:, h % 2, :], kn_ps.all())
                yield
                kb.dma("sp", kT_d[h, :, cols[0]:cols[1]], ksb[:, h % 2, :])
                yield

        tasks = []
        bidx = {}; ridx = {}
        for bq in range(SEQ // 512):
            base = len(tasks)
            dq = ([bidx[bq - 2]] if bq >= 2 else []) + ([ridx[bq - 1]] if bq >= 1 else [])
            ridx[bq] = len(tasks)
            tasks.append((lambda bq=bq: multi_chain([g_rope((bq * 512, (bq + 1) * 512), 1.0, R, bq % 2)] +
                                                    [tile_chain(bq * 4 + j) for j in range(4)]), dq))
            bidx[bq] = len(tasks)
            tasks.append((lambda bq=bq: block_chain(bq), [ridx[bq]] + ([bidx[bq - 1]] if bq >= 1 else [])))
        run_tasks(tasks, int(os.environ.get("LIGHT_W", "2")))

    def phase_mlaq():
        src = route["mla"][0]
        A = Alloc(kb, LIGHT_BASE)
        WDQ = A.take([128, 8, 384], BF16)
        WUQ = A.take([128, 3, 1536], BF16)
        WQR = A.take([128, 3, 8, 64], BF16)
        GQ = A.take([128, 384], F32)
        AT, BT, _ = load_mod_vectors(A, 9216 + 3072, gvec_fm(norm_g, (1, 1, 0)), None, 0.0)
        R = rope_alloc(A)
        NS = 4
        xt = A.take([128, NS, D], F32)
        hT = A.take([128, 2, 8, 512], BF16)
        scs = make_scratch(A, NS, with_xo=False)
        cqnN = [A.take([128, 384], BF16) for _ in range(NS)]
        cqT = A.take([128, 2, 3, 512], BF16)
        qsb = A.take([128, 2, 512], BF16)
        t1 = A.take([64, 512], F32); t2 = A.take([64, 512], F32)
        qrs = A.take([64, 2, 512], BF16)
        qs = 192.0 ** -0.5
        for k in range(0, 8, 4):
            kb.dma("pool", WDQ[:, k:k + 4, :],
                   mla_w_dq.cv(mla_w_dq.h[k * 128:(k + 4) * 128, :].rearrange("(a p) n -> p a n", p=128),
                               k * 128 * 384, (k + 4) * 128 * 384))
        kb.dma("pool", WUQ.all(), mla_w_uq.cv(mla_w_uq.h[:, :].rearrange("(a p) n -> p a n", p=128)))
        kb.dma("sp", GQ.all(), mla_g_q.cv(mla_g_q.h[0].partition_broadcast(128)))
        for c in range(3):
            w4 = WUQ.ap[:, c, :].rearrange("p (h e) -> p h e", h=8)
            src_x2 = WUQ.cv(w4[:, :, 160:192], c * 1536, (c + 1) * 1536)
            src_x1 = WUQ.cv(w4[:, :, 128:160], c * 1536, (c + 1) * 1536)
            kb.S.op("act", lambda e, c=c, src_x2=src_x2: e.mul(out=WQR[:, c, :, 0:32].ap, in_=src_x2.ap, mul=-1.0),
                    reads=[src_x2], writes=[WQR[:, c, :, 0:32]])
            kb.cp("act", WQR[:, c, :, 32:64], src_x1)

        def tile_chain(t):
            bq, j = divmod(t, 4)
            hs = bq % 2
            jc = slice(j * 128, (j + 1) * 128)
            sc = scs[t % NS]; stat = sc["stat"]; junk = sc["junk"]; cqn = cqnN[t % NS]
            yield from g_prenorm(src, t, xt[:, t % NS, :], sc, t % 4, hT[:, hs, :, jc], AT, BT)
            cq_ps = kb.pbank(t % 4, [128, 384])
            for k in range(8):
                kb.mm(cq_ps.all(), hT[:, hs, k, jc], WDQ[:, k, :], k == 0, k == 7)
            yield
            kb.act(junk[:, 0:384], cq_ps.all(), AF.Square, accum=stat[:, 8:9])
            yield
            rstd(stat[:, 10:11], stat[:, 8:9], stat[:, 9:10], 384)
            yield
            kb.stt("dve", cqn.all(), cq_ps.all(), stat[:, 10:11], GQ.all(), ALU.mult, ALU.mult)
            yield
            pT3 = kb.pbank(t % 4, [128, 3, 128], BF16)
            for c in range(3):
                kb.tr(pT3[:, c, :], cqn[:, c * 128:(c + 1) * 128], identb.all())
            yield
            kb.cp("act", cqT[:, hs, :, jc], pT3.all())
            yield

        def block_chain(bq):
            hs = bq % 2
            cols = (bq * 512, (bq + 1) * 512)
            for h in range(8):
                qn_ps = kb.pbank(4 + h % 2, [128, 512])
                qp_ps = kb.pbank(6, [64, 512])
                qr_ps = kb.pbank(7, [64, 512])
                for c in range(3):
                    kb.mm(qn_ps.all(), WUQ[:, c, h * 192:h * 192 + 128], cqT[:, hs, c, :], c == 0, c == 2)
                for c in range(3):
                    kb.mm(qp_ps.all(), WUQ[:, c, h * 192 + 128:h * 192 + 192], cqT[:, hs, c, :], c == 0, c == 2)
                for c in range(3):
                    kb.mm(qr_ps.all(), WQR[:, c, h, :], cqT[:, hs, c, :], c == 0, c == 2)
                yield
                kb.act(qsb[:, h % 2, :], qn_ps.all(), AF.Identity, scale=qs)
                kb.tt("dve", t1.all(), qp_ps.all(), R["cosT"][hs].all(), ALU.mult)
                kb.tt("dve", t2.all(), qr_ps.all(), R["sinT"][hs].all(), ALU.mult)
                yield
                kb.dma("sp", qT_d[h, 0:128, cols[0]:cols[1]], qsb[:, h % 2, :])
                kb.tt("pool", qrs[:, h % 2, :], t1.all(), t2.all(), ALU.add)
                yield
                kb.dma("sp", qT_d[h, 128:192, cols[0]:cols[1]], qrs[:, h % 2, :])
                yield

        tasks = []
        bidx = {}; ridx = {}
        for bq in range(SEQ // 512):
            base = len(tasks)
            dq = ([bidx[bq - 2]] if bq >= 2 else []) + ([ridx[bq - 1]] if bq >= 1 else [])
            ridx[bq] = len(tasks)
            tasks.append((lambda bq=bq: multi_chain([g_rope((bq * 512, (bq + 1) * 512), qs, R, bq % 2)] +
                                                    [tile_chain(bq * 4 + j) for j in range(4)]), dq))
            bidx[bq] = len(tasks)
            tasks.append((lambda bq=bq: block_chain(bq), [ridx[bq]] + ([bidx[bq - 1]] if bq >= 1 else [])))
        run_tasks(tasks, int(os.environ.get("LIGHT_W", "2")))

    def phase_attn():
        A = Alloc(kb, ATTN_BASE)
        KR = A.take([64, SEQ], BF16)
        KN = A.take([128, 2, SEQ], BF16)
        V = A.take([128, 2, NT, 128], BF16)
        QN = A.take([128, 2, 512], BF16)
        QR = A.take([64, 2, 512], BF16)
        NPT = 6
        PT = A.take([128, NPT, 512], BF16)
        RL = A.take([128, 2, 512], F32)
        OT = A.take([128, 2, 512], BF16)
        ACC = A.take([128, 2, 512], F32)
        kb.dma("sp", KR.all(), krT_d.all())
        SK = int(os.environ.get("ATTN_SKEW", "3"))
        blocks = []
        it = 0
        for h in range(8):
            for Q in range(SEQ // 512):
                nkt = 4 * Q + 4
                for kj in range(nkt):
                    blocks.append((h, Q, kj, nkt, it % 2))
                it += 1

        def s_block(i):
            h, Q, kj, nkt, par = blocks[i]
            hs = h % 2
            if kj == 0:
                cols = (Q * 512, (Q + 1) * 512)
                if Q == 0:
                    kb.dma("sp", KN[:, hs, :], kT_d[h, :, :])
                    for tq in range(0, NT, 8):
                        base = tq * 128 * D
                        kb.dma("sp", V[:, hs, tq:tq + 8, :],
                               v_d.cv(v_d.h[tq * 128:(tq + 8) * 128, h * 128:(h + 1) * 128].rearrange("(t p) v -> p t v", p=128),
                                      base, base + 8 * 128 * D))
                kb.dma("sp", QN[:, par, :], qT_d[h, 0:128, cols[0]:cols[1]])
                kb.dma("sp", QR[:, par, :], qT_d[h, 128:192, cols[0]:cols[1]])
                kb.memset("pool", ACC[:, par, :], 0.0)
            c0 = max(0, kj - 4 * Q) * 128
            ps = kb.pbank(i % 4, [128, 512])
            kb.mm(ps[:, c0:512], KN[:, hs, kj * 128:(kj + 1) * 128], QN[:, par, c0:512], True, False)
            if kj >= 4 * Q:
                kb.mm(ps[:, c0:c0 + 128], identb.all(), negb.all(), False, False)
            kb.mm(ps[:, c0:512], KR[:, kj * 128:(kj + 1) * 128], QR[:, par, c0:512], False, True)
            kb.act(PT[:, i % NPT, c0:512], ps[:, c0:512], AF.Exp)

        def pv_block(i):
            h, Q, kj, nkt, par = blocks[i]
            hs = h % 2
            c0 = max(0, kj - 4 * Q) * 128
            po = kb.pbank(4 + par, [128, 512])
            pl = kb.pbank(6 + par, [128, 512])
            kb.mm(po[:, c0:512], V[:, hs, kj, :], PT[:, i % NPT, c0:512], kj == 0, kj == nkt - 1)
            if kj % 2 == 0:
                kb.mm(pl[:, c0:512], oneb.all(), PT[:, i % NPT, c0:512], kj == 0, False)
            else:
                kb.tt("dve", ACC[:, par, c0:512], ACC[:, par, c0:512], PT[:, i % NPT, c0:512], ALU.add)
            if kj == nkt - 1:
                cols = (Q * 512, (Q + 1) * 512)
                kb.mm(pl.all(), onef.all(), ACC[:, par, :], False, True)
                kb.recip(RL[:, par, :], pl.all())
                kb.tt("dve", OT[:, par, :], po.all(), RL[:, par, :], ALU.mult)
                kb.dma("sp", oT_d[h, :, cols[0]:cols[1]], OT[:, par, :])

        nb = len(blocks)
        for i in range(min(SK, nb)):
            s_block(i)
        for i in range(nb):
            if i + SK < nb:
                s_block(i + SK)
            pv_block(i)

    def phase_mlao():
        src, dst = route["mla"]
        A = Alloc(kb, MLAO_BASE)
        WO = A.take([128, 8, D], BF16)
        _, _, G = load_mod_vectors(A, 9216 + 3072, None, gvec_rep(norm_g, (1, 1, 1)), 1.0)
        NS = int(os.environ.get("MLAO_NS", "4"))
        OTt = A.take([128, NS, 8, 128], BF16)
        xr = A.take([128, NS, D], F32)
        scs = make_scratch(A, NS)
        for k in range(0, 8, 2):
            kb.dma("pool", WO[:, k:k + 2, :],
                   mla_w_out.cv(mla_w_out.h[k * 128:(k + 2) * 128, :].rearrange("(a p) n -> p a n", p=128),
                                k * 128 * D, (k + 2) * 128 * D))

        def tile_chain(t):
            kb.dma("sp", OTt[:, t % NS, :, :],
                   oT_d.cv(oT_d.h[:, :, t * 128:(t + 1) * 128].rearrange("h v t -> v h t")))
            kb.dma("sp", xr[:, t % NS, :], src[t * 128:(t + 1) * 128, :])
            yield
            pys = [kb.pbank((2 * t) % 8, [128, 512]), kb.pbank((2 * t + 1) % 8, [128, 512])]
            for hf in range(2):
                for h in range(8):
                    kb.mm(pys[hf].all(), OTt[:, t % NS, h, :], WO[:, h, hf * 512:(hf + 1) * 512], h == 0, h == 7)
            yield
            yield from g_postnorm([p.all() for p in pys], xr[:, t % NS, :], scs[t % NS], G, dst, t)

        grp = int(os.environ.get("MLAO_G", "2"))
        run_tasks([(lambda t0=t0: multi_chain([tile_chain(t) for t in range(t0, min(NT, t0 + grp))]), []) for t0 in range(0, NT, grp)],
                  int(os.environ.get("MLAO_W", "2")))

    pre_ffn1a = "ffn1a" in plan and "kv" in plan and plan.index("kv") < plan.index("ffn1a")
    pre_ffn1b = "ffn1b" in plan and "mlaq" in plan
    pre_ffn0a = "ffn0a" in plan and "mods" in plan
    for p in plan:
        if p == "mods":
            if pre_ffn0a:
                ffn_load(0, 0)
            mods_setup(140032)
            for _ in range(12 if "gla" in plan else 40):
                mods_job()
        elif p == "ffn0a":
            phase_ffn("ffn0a", 0, 0, 0, 0, preloaded=pre_ffn0a)
        elif p == "ffn0b":
            phase_ffn("ffn0b", 0, 1, 6 * 1024, 2)
        elif p == "ffn1a":
            if pre_ffn1a:
                ffn_load(1, 0, lo=LIGHT_BASE - 4095)
            phase_ffn("ffn1a", 1, 0, 9216, 0, preloaded=pre_ffn1a)
        elif p == "ffn1b":
            phase_ffn("ffn1b", 1, 1, 9216 + 6 * 1024, 2, preloaded=pre_ffn1b)
        elif p == "gla":
            if os.environ.get("GLA_V1"):
                phase_gla()
            else:
                phase_gla2()
        elif p == "kv":
            if pre_ffn1a:
                ffn_load(1, 0, hi=LIGHT_BASE)
            phase_kv()
        elif p == "mlaq":
            if pre_ffn1b:
                ffn_load(1, 1, hi=LIGHT_BASE)
            phase_mlaq()
        elif p == "attn":
            if pre_ffn1b:
                ffn_load(1, 1, lo=LIGHT_BASE - 4095, hi=MLAO_BASE)
            phase_attn()
        elif p == "mlao":
            phase_mlao()
            if pre_ffn1b:
                ffn_load(1, 1, lo=MLAO_BASE - 4095)
        else:
            raise ValueError(p)
    while "mods" in plan and mods_state.get("jobs") and mods_state["next"] < len(mods_state["jobs"]):
        raise RuntimeError("mods jobs left unissued")
    if debug_out == "mods":
        pass
    S.finish()
    return kb


def host_inputs(inputs, b):
    f = lambda a: np.ascontiguousarray(np.asarray(a), dtype=np.float32)
    m = {
        "x": f(inputs["x"][b]),
        "c_fm": np.ascontiguousarray(f(inputs["c"][b]).reshape(8, 128).T),
        "pos": np.ascontiguousarray(np.asarray(inputs["positions"][b], dtype=np.int32).reshape(1, SEQ)),
        "consts": make_consts(),
        "cond_w": f(inputs["cond_w"]), "cond_b": f(inputs["cond_b"]), "norm_g": f(inputs["norm_g"]),
        "ffn_w_gu": f(inputs["ffn_w_gu"]), "ffn_w_down": f(inputs["ffn_w_down"]),
        "gla_w_in": f(inputs["gla_w_in"][0]), "gla_w_gate_up": f(inputs["gla_w_gate_up"][0]),
        "gla_b_gate": f(inputs["gla_b_gate"]).reshape(1, 512), "gla_g_out": f(inputs["gla_g_out"]).reshape(1, 256),
        "gla_w_out": f(inputs["gla_w_out"][0]),
        "kv_g_in": f(inputs["kv_g_in"]).reshape(1, D), "kv_cond_w": f(inputs["kv_cond_w"]),
        "kv_cond_b": f(inputs["kv_cond_b"]).reshape(1, 2 * D),
        "mla_w_kv_a": f(inputs["mla_w_kv_a"]), "mla_g_kv": f(inputs["mla_g_kv"]).reshape(1, 256),
        "mla_w_kv_b": f(inputs["mla_w_kv_b"]),
        "mla_w_dq": f(inputs["mla_w_dq"][0]), "mla_g_q": f(inputs["mla_g_q"]).reshape(1, 384),
        "mla_w_uq": f(inputs["mla_w_uq"][0]), "mla_w_out": f(inputs["mla_w_out"][0]),

    }
    return m


FULL_PLAN = ["mods", "ffn0a", "gla", "ffn0b", "kv", "ffn1a", "mlaq", "attn", "mlao", "ffn1b"]


def kernel(**inputs):
    kb = build(FULL_PLAN)
    with contextlib.ExitStack() as st:
        kb.S.emit(st)
        in_maps = [{k: v for k, v in host_inputs(inputs, b).items() if k in kb.din} for b in range(8)]
        res = run_bass_kernel_spmd(kb.nc, in_maps, core_ids=list(range(8)))
    return np.stack([np.asarray(r["y"], dtype=np.float32) for r in res.results], axis=0)
```
